# Optimizing a Trainium2 kernel written in Bass

```python
import math
import jax, jax.numpy as jnp
from jax import lax
import numpy as np

D_MODEL = 1024
BATCH = 8
SEQ = 4096
DEPTH = 2

HEAD_DIM = 64
ROPE_THETA = 10000.0
EPS = 1e-6
N_MIXERS = 2
DIL_GROUPS = ((128, 1), (512, 4), (2048, 16))
N_DIL_GROUPS = len(DIL_GROUPS)
DIL_HEADS = D_MODEL // HEAD_DIM
DIL_WIDTH = DIL_HEADS * HEAD_DIM
DIL_IN = 3 * N_DIL_GROUPS * DIL_WIDTH + DIL_WIDTH
DIFF_HEADS = D_MODEL // HEAD_DIM // 2
DIFF_QK_DIM = HEAD_DIM
DIFF_V_DIM = 2 * HEAD_DIM
DIFF_WIDTH = DIFF_HEADS * DIFF_V_DIM
DIFF_QK_WIDTH = 2 * DIFF_HEADS * DIFF_QK_DIM
DIFF_IN = 2 * DIFF_QK_WIDTH + 2 * DIFF_WIDTH
Q_BLOCK = 128
N_DIL_LAYERS = (DEPTH + 1) // 2
N_DIFF_LAYERS = DEPTH // 2

kernel_name = "hybrid_dilated_diff_attention_trunk"


def rmsnorm(x, g):
    x32 = x.astype(jnp.float32)
    y = x32 * lax.rsqrt(jnp.mean(x32 * x32, axis=-1, keepdims=True) + EPS)
    return y.astype(x.dtype) * g


def rope(x, pos):
    dh = x.shape[-1]
    freqs = ROPE_THETA ** (-jnp.arange(0, dh, 2, dtype=jnp.float32) / dh)
    ang = pos.astype(jnp.float32)[:, None] * freqs[None, :]
    cos = jnp.cos(ang)[None, :, None, :].astype(x.dtype)
    sin = jnp.sin(ang)[None, :, None, :].astype(x.dtype)
    x1, x2 = x[..., : dh // 2], x[..., dh // 2:]
    return jnp.concatenate([x1 * cos - x2 * sin, x2 * cos + x1 * sin], axis=-1)


def dilated_window_attention(q, k, v, window, dilation):
    B, S, H, Dh = q.shape
    w = window // dilation
    L = S // dilation
    nb = -(-L // w)
    Lp = nb * w

    def to_sub(t):
        t = t.reshape(B, L, dilation, H, Dh).transpose(0, 2, 1, 3, 4)
        t = jnp.pad(t, ((0, 0), (0, 0), (0, Lp - L), (0, 0), (0, 0)))
        return t.reshape(B, dilation, nb, w, H, Dh)

    qs, ks, vs = to_sub(q), to_sub(k), to_sub(v)

    def with_prev(t):
        prev = jnp.pad(t, ((0, 0), (0, 0), (1, 0), (0, 0), (0, 0), (0, 0)))[:, :, :-1]
        return jnp.concatenate([prev, t], axis=3)

    kb, vb = with_prev(ks), with_prev(vs)
    scores = jnp.einsum('brnqhd,brnkhd->brnhqk', qs, kb).astype(jnp.float32) * (Dh ** -0.5)
    qi = jnp.arange(w)[:, None]
    kj = jnp.arange(2 * w)[None, :]
    dist = w + qi - kj
    kpos = jnp.arange(nb)[:, None, None] * w + kj[None] - w
    mask = (dist >= 0)[None] & (dist <= w)[None] & (kpos >= 0)
    scores = jnp.where(mask[None, None, :, None], scores, -jnp.inf)
    m = jnp.max(scores, axis=-1, keepdims=True)
    p = jnp.exp(scores - m)
    den = jnp.sum(p, axis=-1)
    out = jnp.einsum('brnhqk,brnkhd->brnhqd', p.astype(v.dtype), vb).astype(jnp.float32)
    out = out / den[..., None]
    lse = m[..., 0] + jnp.log(den)
    out = out.transpose(0, 1, 2, 4, 3, 5).reshape(B, dilation, Lp, H, Dh)[:, :, :L]
    out = out.transpose(0, 2, 1, 3, 4).reshape(B, S, H, Dh)
    lse = lse.transpose(0, 1, 2, 4, 3).reshape(B, dilation, Lp, H)[:, :, :L]
    lse = lse.transpose(0, 2, 1, 3).reshape(B, S, H)
    return out, lse


def dilated_mixer(h, w_in, w_out):
    B, S, _ = h.shape
    pos = jnp.arange(S)
    proj = h @ w_in
    n_qkv = 3 * N_DIL_GROUPS * DIL_WIDTH
    qkv = proj[..., :n_qkv].reshape(B, S, 3, N_DIL_GROUPS * DIL_HEADS, HEAD_DIM)
    gate = proj[..., n_qkv:]
    q = rope(qkv[:, :, 0], pos)
    k = rope(qkv[:, :, 1], pos)
    v = qkv[:, :, 2]
    outs, lses = [], []
    for g, (window, dil) in enumerate(DIL_GROUPS):
        sl = slice(g * DIL_HEADS, (g + 1) * DIL_HEADS)
        o, lse = dilated_window_attention(q[:, :, sl], k[:, :, sl], v[:, :, sl], window, dil)
        outs.append(o)
        lses.append(lse)
    alpha = jax.nn.softmax(jnp.stack(lses, axis=0), axis=0)
    o = jnp.sum(alpha[..., None] * jnp.stack(outs, axis=0), axis=0)
    y = o.reshape(B, S, DIL_WIDTH).astype(h.dtype) * jax.nn.silu(gate)
    return y @ w_out


def diff_mixer(h, w_in, lq1, lk1, lq2, lk2, subln, w_out, lambda_init):
    B, S, _ = h.shape
    pos = jnp.arange(S)
    proj = h @ w_in
    q = proj[..., :DIFF_QK_WIDTH].reshape(B, S, 2 * DIFF_HEADS, DIFF_QK_DIM)
    k = proj[..., DIFF_QK_WIDTH:2 * DIFF_QK_WIDTH].reshape(B, S, 2 * DIFF_HEADS, DIFF_QK_DIM)
    v = proj[..., 2 * DIFF_QK_WIDTH:2 * DIFF_QK_WIDTH + DIFF_WIDTH].reshape(B, S, DIFF_HEADS, DIFF_V_DIM)
    gate = proj[..., 2 * DIFF_QK_WIDTH + DIFF_WIDTH:]
    q = rope(q, pos).reshape(B, S, DIFF_HEADS, 2, DIFF_QK_DIM)
    k = rope(k, pos).reshape(B, S, DIFF_HEADS, 2, DIFF_QK_DIM)
    lam = (jnp.exp(jnp.sum(lq1.astype(jnp.float32) * lk1.astype(jnp.float32)))
           - jnp.exp(jnp.sum(lq2.astype(jnp.float32) * lk2.astype(jnp.float32))) + lambda_init)
    n_qb = S // Q_BLOCK
    qb = q.reshape(B, n_qb, Q_BLOCK, DIFF_HEADS, 2, DIFF_QK_DIM).transpose(1, 0, 2, 3, 4, 5)
    scale = DIFF_QK_DIM ** -0.5
    kpos = jnp.arange(S)

    def block(args):
        qblk, bi = args
        s = jnp.einsum('bqhcd,bkhcd->bchqk', qblk, k).astype(jnp.float32) * scale
        qpos = bi * Q_BLOCK + jnp.arange(Q_BLOCK)
        s = jnp.where(kpos[None, :] <= qpos[:, None], s, -jnp.inf)
        p = jax.nn.softmax(s, axis=-1)
        a = p[:, 0] - lam * p[:, 1]
        return jnp.einsum('bhqk,bkhd->bqhd', a.astype(v.dtype), v)

    o = lax.map(block, (qb, jnp.arange(n_qb)))
    o = o.transpose(1, 0, 2, 3, 4).reshape(B, S, DIFF_HEADS, DIFF_V_DIM)
    o = rmsnorm(o, subln) * (1.0 - lambda_init)
    y = o.reshape(B, S, DIFF_WIDTH) * jax.nn.silu(gate)
    return y @ w_out


def setup_inputs(seed: int = 0) -> dict:
    key = jax.random.key(seed)
    ks = jax.random.split(key, 12)
    f32 = jnp.float32
    x = jax.random.normal(ks[0], (BATCH, SEQ, D_MODEL), f32)
    norm_pre = 1.0 + 0.05 * jax.random.normal(ks[1], (DEPTH, D_MODEL), f32)
    norm_post = 1.0 + 0.05 * jax.random.normal(ks[2], (DEPTH, D_MODEL), f32)
    dil_w_in = jax.random.normal(ks[3], (N_DIL_LAYERS, D_MODEL, DIL_IN), f32) * D_MODEL ** -0.5
    dil_w_out = jax.random.normal(ks[4], (N_DIL_LAYERS, DIL_WIDTH, D_MODEL), f32) * DIL_WIDTH ** -0.5
    diff_w_in = jax.random.normal(ks[5], (N_DIFF_LAYERS, D_MODEL, DIFF_IN), f32) * D_MODEL ** -0.5
    diff_w_out = jax.random.normal(ks[6], (N_DIFF_LAYERS, DIFF_WIDTH, D_MODEL), f32) * DIFF_WIDTH ** -0.5
    diff_lambda_q1 = 0.1 * jax.random.normal(ks[7], (N_DIFF_LAYERS, DIFF_QK_DIM), f32)
    diff_lambda_k1 = 0.1 * jax.random.normal(ks[8], (N_DIFF_LAYERS, DIFF_QK_DIM), f32)
    diff_lambda_q2 = 0.1 * jax.random.normal(ks[9], (N_DIFF_LAYERS, DIFF_QK_DIM), f32)
    diff_lambda_k2 = 0.1 * jax.random.normal(ks[10], (N_DIFF_LAYERS, DIFF_QK_DIM), f32)
    diff_subln = 1.0 + 0.05 * jax.random.normal(ks[11], (N_DIFF_LAYERS, DIFF_V_DIM), f32)
    return {"x": x, "norm_pre": norm_pre, "norm_post": norm_post,
            "dil_w_in": dil_w_in, "dil_w_out": dil_w_out,
            "diff_w_in": diff_w_in, "diff_w_out": diff_w_out,
            "diff_lambda_q1": diff_lambda_q1, "diff_lambda_k1": diff_lambda_k1,
            "diff_lambda_q2": diff_lambda_q2, "diff_lambda_k2": diff_lambda_k2,
            "diff_subln": diff_subln}


def reference(x, norm_pre, norm_post, dil_w_in, dil_w_out, diff_w_in, diff_w_out,
              diff_lambda_q1, diff_lambda_k1, diff_lambda_q2, diff_lambda_k2, diff_subln):
    h = x
    for i in range(DEPTH):
        u = rmsnorm(h, norm_pre[i])
        j = i // N_MIXERS
        if i % N_MIXERS == 0:
            y = dilated_mixer(u, dil_w_in[j], dil_w_out[j])
        else:
            lambda_init = 0.8 - 0.6 * math.exp(-0.3 * i)
            y = diff_mixer(u, diff_w_in[j], diff_lambda_q1[j], diff_lambda_k1[j],
                           diff_lambda_q2[j], diff_lambda_k2[j], diff_subln[j],
                           diff_w_out[j], lambda_init)
        h = h + rmsnorm(y, norm_post[i])
    return h
```

```python
import contextlib
import math
import numpy as np
import concourse.bass as bass
import concourse.mybir as mybir
from concourse.bass_utils import run_bass_kernel_spmd

F32 = mybir.dt.float32
BF16 = mybir.dt.bfloat16
AF = mybir.ActivationFunctionType
ALU = mybir.AluOpType
AX = mybir.AxisListType

SEM_ROT = 30000
N_DMA_SEMS = 24

S = 4096
D = 1024
NT = 32
NEG = -30000.0
NH0 = 16
DBGV = 0
ROPE_STEPS = 5
L0_STEPS = 10
EPS = 1e-6
LAMBDA_INIT = 0.8 - 0.6 * math.exp(-0.3 * 1)
DIL = ((128, 1), (512, 4), (2048, 16))


class _Rec:
    def __init__(self):
        self.call = None

    def __getattr__(self, name):
        def f(*a, **kw):
            self.call = (name, a, kw)
            return self
        return f


def _capture(fn):
    r = _Rec()
    fn(r)
    name, a, kw = r.call
    return lambda e: getattr(e, name)(*a, **kw)


class KB:
    ENGS = ("pe", "act", "dve", "pool", "sp")

    def __init__(self, nc):
        self.nc = nc
        self.stack = contextlib.ExitStack()
        self.prog = {e: [] for e in self.ENGS}
        self.sem = {}
        self.cnt = {}
        self.nsem = 0
        self.waited = {e: {} for e in self.ENGS}
        self.res = {}
        self.pending = {e: [] for e in self.ENGS}
        self.last_tok = {e: None for e in self.ENGS}
        for e in ("pe", "act", "dve", "pool"):
            self._new_eng_sem(e)
        self.dma_sems = []
        self.dma_pools = {"sp": [], "pool": []}
        for i in range(N_DMA_SEMS):
            h = self.stack.enter_context(nc.semaphore(f"dq{i}"))
            self.dma_sems.append([h, 0, f"dq{i}"])
            self.dma_pools["sp" if i < 16 else "pool"].append(self.dma_sems[-1])
        self.dma_rr = {"sp": 0, "pool": 0}
        self.n_instr = {e: 0 for e in self.ENGS}

    def sb(self, name, shape, dt):
        return self.stack.enter_context(self.nc.sbuf_tensor(name, list(shape), dt))

    def ps(self, name, shape, dt):
        return self.stack.enter_context(self.nc.psum_tensor(name, list(shape), dt))

    def _new_eng_sem(self, e):
        name = f"s_{e}_{self.nsem}"
        self.nsem += 1
        h = self.stack.enter_context(self.nc.semaphore(name))
        self.sem[e] = (h, name)
        self.cnt[e] = 0

    def _wait(self, eng, tok):
        if tok is None:
            return
        h, name, val, src = tok
        if src == eng and eng == "pe":
            return
        w = self.waited[eng]
        if w.get(name, 0) >= val:
            return
        w[name] = val
        self.prog[eng].append(lambda e, h=h, val=val: e.wait_ge(h, val))

    def _deps(self, reads, writes):
        deps = []
        for r in reads:
            st = self.res.get(r)
            if st and st[0] is not None:
                deps.append(st[0])
        for w in writes:
            st = self.res.get(w)
            if st:
                if st[0] is not None:
                    deps.append(st[0])
                deps.extend(st[1])
        return deps

    def _register(self, tok, reads, writes):
        for r in reads:
            st = self.res.setdefault(r, [None, []])
            st[1].append(tok)
            if len(st[1]) > 48:
                best = {}
                for t in st[1]:
                    if t[1] not in best or best[t[1]][2] < t[2]:
                        best[t[1]] = t
                st[1] = list(best.values())
        for w in writes:
            self.res[w] = [tok, []]

    def op(self, eng, fn, reads=(), writes=(), inc=True):
        fn = _capture(fn)
        writes = tuple(writes) + tuple(r for r in reads if r.startswith("ps"))
        reads = tuple(r for r in reads if not r.startswith("ps"))
        for tok in self._deps(reads, writes):
            self._wait(eng, tok)
        self.n_instr[eng] += 1
        if not inc:
            self.pending[eng].append((reads, writes))
            self.prog[eng].append(lambda e, fn=fn: fn(e))
            return None
        if self.cnt[eng] >= SEM_ROT:
            self._new_eng_sem(eng)
        h, name = self.sem[eng]
        self.cnt[eng] += 1
        tok = (h, name, self.cnt[eng], eng)
        self.prog[eng].append(lambda e, fn=fn, h=h: fn(e).then_inc(h, 1))
        for (r, w) in self.pending[eng]:
            self._register(tok, r, w)
        self.pending[eng] = []
        self._register(tok, reads, writes)
        self.last_tok[eng] = tok
        return tok

    def dma(self, out, in_, reads=(), writes=(), queue="sp", **kw):
        reads = tuple(reads)
        writes = tuple(writes)
        pool = self.dma_pools[queue]
        slot = pool[self.dma_rr[queue]]
        self.dma_rr[queue] = (self.dma_rr[queue] + 1) % len(pool)
        h, cur, name = slot
        if cur > 0:
            self._wait(queue, (h, name, cur, "dma"))
        for tok in self._deps(reads, writes):
            self._wait(queue, tok)
        slot[1] = cur + 16
        tok = (h, name, cur + 16, "dma")
        self.prog[queue].append(
            lambda e, out=out, in_=in_, h=h, kw=kw: e.dma_start(out=out, in_=in_, **kw).then_inc(h, 16))
        self._register(tok, reads, writes)
        self.n_instr[queue] += 1
        return tok

    def barrier(self):
        toks = [t for t in self.last_tok.values() if t is not None]
        dtoks = [(s[0], s[2], s[1], "dma") for s in self.dma_sems if s[1] > 0]
        for e in self.ENGS:
            for t in toks + dtoks:
                self._wait(e, t)

    def finish(self):
        toks = [t for t in self.last_tok.values() if t is not None]
        dtoks = [(s[0], s[2], s[1], "dma") for s in self.dma_sems if s[1] > 0]
        for t in toks + dtoks:
            self._wait("sp", t)
        with self.nc.Block() as block:
            @block.tensor
            def _(e):
                for f in self.prog["pe"]:
                    f(e)

            @block.scalar
            def _(e):
                for f in self.prog["act"]:
                    f(e)

            @block.vector
            def _(e):
                for f in self.prog["dve"]:
                    f(e)

            @block.gpsimd
            def _(e):
                for f in self.prog["pool"]:
                    f(e)

            @block.sync
            def _(e):
                for f in self.prog["sp"]:
                    f(e)
        self.stack.close()


def _prod(xs):
    p = 1
    for v in xs:
        p *= v
    return p


def build_program(stage):
    nc = bass.Bass("TRN2", target_bir_lowering=False)

    def din(name, shape):
        return nc.dram_tensor(name, list(shape), F32, kind="ExternalInput").ap()

    x_d = din("x", [S, D])
    w0_d = din("w0", [16, 1024, 640])
    w0o_d = din("w0o", [1024, 1024])
    w1_d = din("w1", [8, 1024, 512])
    w1o_d = din("w1o", [1024, 1024])
    gpre_d = din("gpre", [128, 16])
    gpost_d = din("gpost", [2, 1024])
    lamv_d = din("lamv", [4, 64])
    subln_d = din("subln", [1, 128])
    cs_d = din("cs", [128, S])
    sn_d = din("sn", [128, S])
    perm_d = din("perm", [128, 128])
    ident_d = din("ident", [128, 128])
    mask_d = din("mask0", [128, 256])
    out_d = nc.dram_tensor("out", [S, D], F32, kind="ExternalOutput").ap()
    y_scr = nc.dram_tensor("y_scr", [S, D], BF16, kind="Internal").ap()
    h1_scr = nc.dram_tensor("h1_scr", [S, D], F32, kind="Internal").ap()

    k = KB(nc)
    TOTAL = 206 * 1024
    M = k.sb("M", [128, TOTAL // 2], BF16)

    class Carve:
        def __init__(self, base):
            self.off = base

        def take(self, fs, dt):
            esz = 2 if dt == BF16 else 4
            nb = _prod(fs) * esz
            assert self.off % 4 == 0
            v = M[:, self.off // 2:(self.off + nb) // 2]
            if dt == F32:
                v = v.bitcast(F32)
            if len(fs) == 2:
                v = v.rearrange("p (a b) -> p a b", a=fs[0])
            elif len(fs) == 3:
                v = v.rearrange("p (a b c) -> p a b c", a=fs[0], b=fs[1])
            self.off += (nb + 63) // 64 * 64
            assert self.off <= TOTAL, (self.off, TOTAL)
            return v

    cm = Carve(0)
    uT = cm.take([8, S], BF16)
    ident16 = cm.take([128], BF16)
    perm16 = cm.take([128], BF16)
    mask16 = cm.take([256], BF16)
    identf = cm.take([128], F32)
    gpre = cm.take([16], F32)
    gpost = cm.take([2, 1024], F32)
    lamb = cm.take([4, 64], F32)
    lprod = cm.take([2, 64], F32)
    lsc = cm.take([8], F32)
    sublnc = cm.take([128], F32)
    stat = cm.take([16], F32)
    PH_BASE = cm.off

    ps = [k.ps(f"ps{i}", [128, 512], F32) for i in range(8)]

    def psr(i):
        return f"ps{i}"

    k.dma(ident16, ident_d, writes=["ident16"], queue="pool")
    k.dma(perm16, perm_d, writes=["perm16"], queue="pool")
    k.dma(mask16, mask_d, writes=["mask16"], queue="pool")
    k.dma(identf, ident_d, writes=["identf"])
    k.dma(gpre, gpre_d, writes=["gpre"])
    k.dma(gpost, gpost_d.unsqueeze(0).broadcast_to([128, 2, 1024]), writes=["gpost"])
    for i in range(8):
        k.op("dve", lambda e, i=i: e.memset(ps[i][:], 0.0), writes=[psr(i)])

    lam_needed = stage in ("full", "l1")
    if lam_needed:
        k.dma(lamb, lamv_d.unsqueeze(0).broadcast_to([128, 4, 64]), writes=["lamb"])
        k.dma(sublnc, subln_d.broadcast_to([128, 128]), writes=["sublnc_raw"])
        lv = lamb.rearrange("p (a b) c -> p a b c", a=2)
        k.op("dve", lambda e: e.tensor_tensor(out=lprod, in0=lv[:, :, 0, :], in1=lv[:, :, 1, :], op=ALU.mult),
             reads=["lamb"], writes=["lprod"])
        k.op("dve", lambda e: e.tensor_reduce(out=lsc[:, 0:2], in_=lprod, axis=AX.X, op=ALU.add),
             reads=["lprod"], writes=["lsc01"])
        k.op("act", lambda e: e.activation(lsc[:, 2:4], lsc[:, 0:2], AF.Exp), reads=["lsc01"], writes=["lsc23"])
        k.op("dve", lambda e: e.tensor_tensor(out=lsc[:, 4:5], in0=lsc[:, 3:4], in1=lsc[:, 2:3], op=ALU.subtract),
             reads=["lsc23"], writes=["lsc4"])
        k.op("dve", lambda e: e.tensor_scalar(lsc[:, 5:6], lsc[:, 4:5], -LAMBDA_INIT, None, ALU.add),
             reads=["lsc4"], writes=["neglam"])
        k.op("dve", lambda e: e.tensor_scalar(sublnc, sublnc, 0.5 * (1.0 - LAMBDA_INIT), None, ALU.mult),
             reads=["sublnc_raw"], writes=["sublnc"])
    neglam = lsc[:, 5:6]

    y_view = y_scr.rearrange("(t p) c -> p t c", p=128)

    def prenorm_a(src, src_res, tt, xn, xn_res, sqj):
        i = tt % 4
        k.op("act", lambda e: e.activation(sqj, src, AF.Square, accum_out=stat[:, i:i + 1]),
             reads=[src_res], writes=["sqj", f"ss{i}"])
        k.op("act", lambda e: e.activation(stat[:, 4 + i:5 + i], stat[:, i:i + 1], AF.Ln, scale=1.0 / D, bias=EPS),
             reads=[f"ss{i}"], writes=[f"lnv{i}"])
        k.op("act", lambda e: e.activation(stat[:, 8 + i:9 + i], stat[:, 4 + i:5 + i], AF.Exp, scale=-0.5),
             reads=[f"lnv{i}"], writes=[f"rstd{i}"])
        k.op("dve", lambda e: e.tensor_scalar(xn, src, stat[:, 8 + i:9 + i], None, ALU.mult),
             reads=[src_res, f"rstd{i}"], writes=[xn_res])

    def prenorm_b(tt, layer, xn, xn_res, bank=None):
        pb = 6 + (tt % 2) if bank is None else bank
        pT = ps[pb][:].bitcast(BF16)
        for kc in range(8):
            k.op("pe", lambda e, kc=kc: e.transpose(pT[:, kc * 128:(kc + 1) * 128], xn[:, kc * 128:(kc + 1) * 128],
                                                     ident16),
                 reads=[xn_res, "ident16"], writes=[psr(pb)], inc=(kc == 7))
        gb = gpre[:, layer * 8:(layer + 1) * 8].unsqueeze(2).broadcast_to([128, 8, 128])
        k.op("dve", lambda e: e.tensor_tensor(out=uT[:, :, tt * 128:(tt + 1) * 128],
                                              in0=pT.rearrange("p (a b) -> p a b", a=8), in1=gb, op=ALU.mult),
             reads=[psr(pb), "gpre"], writes=["uT"])

    def phase_prenorm_dram(src_d, layer):
        c = Carve(PH_BASE)
        xt = [c.take([1024], F32) for _ in range(4)]
        xn = [c.take([1024], BF16) for _ in range(3)]
        sqj = c.take([1024], BF16)
        for i in range(NT + 3):
            t = i
            if t < NT:
                k.dma(xt[t % 4], src_d[t * 128:(t + 1) * 128, :], writes=[f"xt{t % 4}"])
            t = i - 2
            if 0 <= t < NT:
                prenorm_a(xt[t % 4], f"xt{t % 4}", t, xn[t % 3], f"xn{t % 3}", sqj)
            t = i - 3
            if 0 <= t < NT:
                prenorm_b(t, layer, xn[t % 3], f"xn{t % 3}")

    rope_ctr = [0]

    def rope_a(bank, xs, eng="act"):
        s = rope_ctr[0] % 2
        rope_ctr[0] += 1
        if eng == "act":
            k.op("act", lambda e: e.activation(xs[s], ps[bank][:], AF.Copy), reads=[psr(bank)], writes=[f"xs{s}"])
        else:
            k.op("dve", lambda e: e.tensor_copy(xs[s], ps[bank][:]), reads=[psr(bank)], writes=[f"xs{s}"])
        return s

    def rope_b(s, bank, c, outs, xs, t1, t2, cs16, sn16):
        rb = 2 + s
        ch = slice(c * 512, (c + 1) * 512)
        k.op("pe", lambda e: e.matmul(ps[rb][:], perm16, xs[s], start=True, stop=True),
             reads=[f"xs{s}", "perm16"], writes=[psr(rb)])
        k.op("dve", lambda e: e.tensor_tensor(out=t1[s], in0=ps[bank][:], in1=cs16[:, ch], op=ALU.mult),
             reads=[psr(bank), "cs16"], writes=[f"t1{s}"])
        k.op("dve", lambda e: e.tensor_tensor(out=t2[s], in0=ps[rb][:], in1=sn16[:, ch], op=ALU.mult),
             reads=[psr(rb), "sn16"], writes=[f"t2{s}"])
        for oi, (dest, lo, hi, res, dd) in enumerate(outs):
            eng = "pool"
            if dd == 1:
                k.op(eng, lambda e, dest=dest, lo=lo, hi=hi: e.tensor_tensor(
                    out=dest[lo:hi, ch], in0=t1[s][lo:hi, :], in1=t2[s][lo:hi, :], op=ALU.add),
                    reads=[f"t1{s}", f"t2{s}"], writes=[f"{res}_{c}"])
            else:
                n = 512 // dd
                dv = dest[lo:hi, :].rearrange("p (r i) -> p r i", r=dd)[:, :, c * n:(c + 1) * n]
                a0 = t1[s][lo:hi, :].rearrange("p (i r) -> p r i", r=dd)
                a1 = t2[s][lo:hi, :].rearrange("p (i r) -> p r i", r=dd)
                k.op(eng, lambda e, dv=dv, a0=a0, a1=a1: e.tensor_tensor(out=dv, in0=a0, in1=a1, op=ALU.add),
                     reads=[f"t1{s}", f"t2{s}"], writes=[f"{res}_{cc}" for cc in range(8)])

    def run_pass(Wt, wres, col0, outs, bufs, extra=None, do_rope=True, copy_eng="act"):
        xs, t1, t2, cs16, sn16 = bufs
        slots = {}

        def stage2(c):
            if do_rope:
                rope_b(slots[c], c % 2, c, outs, xs, t1, t2, cs16, sn16)
            if extra is not None:
                extra(slots[c], c)

        for c in range(8):
            proj_fm(Wt, wres, col0, c % 2, c)
            slots[c] = rope_a(c % 2, xs, copy_eng)
            if c > 0:
                stage2(c - 1)
        stage2(7)

    def proj_fm(Wt, wres, col0, bank, c):
        for kc in range(8):
            k.op("pe", lambda e, kc=kc: e.matmul(ps[bank][:], Wt[:, kc, col0:col0 + 128],
                                                 uT[:, kc, c * 512:(c + 1) * 512], start=(kc == 0), stop=(kc == 7)),
                 reads=[wres, "uT"], writes=[psr(bank)], inc=(kc == 7))

    def phase_layer0():
        c = Carve(PH_BASE)
        QA = c.take([S], BF16)
        KA0 = c.take([S], BF16)
        KA1 = c.take([S], BF16)
        cs16 = c.take([S], BF16)
        sn16 = c.take([S], BF16)
        xs = [c.take([512], BF16) for _ in range(2)]
        t1 = [c.take([512], F32) for _ in range(2)]
        t2 = [c.take([512], F32) for _ in range(2)]
        PT = [c.take([256], BF16) for _ in range(4)]
        y4 = c.take([NT, 128], BF16)
        VT2 = c.take([2048], BF16)
        Vaug = c.take([3, NT, 65], BF16)
        gs = c.take([NT, 64], BF16)
        acc = c.take([S], F32)
        W0 = [c.take([8, 640], BF16) for _ in range(2)]
        thb = [c.take([4, 64], F32) for _ in range(2)]
        ob = [c.take([4, 64], F32) for _ in range(2)]
        rden = c.take([8], F32)
        m01 = c.take([256], BF16)
        k.op("dve", lambda e: e.tensor_scalar(m01, mask16, 0.0, None, ALU.is_equal), reads=["mask16"], writes=["m01"])

        k.dma(cs16, cs_d, writes=["cs16"], queue="pool", max_dma_last_dim=4096)
        k.dma(sn16, sn_d, writes=["sn16"], queue="pool", max_dma_last_dim=4096)
        k.op("pool", lambda e: e.memset(KA0[64:128, :], 0.0), writes=[f"KA0_{cc}" for cc in range(8)])
        k.op("pool", lambda e: e.memset(KA1[0:64, :], 0.0), writes=[f"KA1_{cc}" for cc in range(8)])
        k.op("pool", lambda e: e.memset(QA, 0.0), writes=[f"QA_{cc}" for cc in range(8)])
        k.op("pool", lambda e: e.memset(Vaug[:, :, :, 64:65], 1.0), writes=["Vaug0", "Vaug1", "Vaug2"])

        def load_w(h):
            k.dma(W0[h % 2], w0_d[h].rearrange("(kc p) c -> p kc c", p=128), writes=[f"W0_{h % 2}"], queue="pool")

        def tok(d, r, a, b):
            return slice(r + a * d, r + (b - 1) * d + 1, d)

        tr_ctr = [0]

        def v_from_xs(sx, g, cch):
            bank = 4 + tr_ctr[0] % 2
            tr_ctr[0] += 1
            pT = ps[bank][:].bitcast(BF16)
            for i in range(4):
                src = xs[sx][:, i * 128:(i + 1) * 128] if g == 0 else xs[sx][:, i:512:4]
                k.op("pe", lambda e, i=i, src=src: e.transpose(pT[:, i * 128:(i + 1) * 128], src, ident16),
                     reads=[f"xs{sx}", "ident16"], writes=[psr(bank)], inc=(i == 3))
            pv = pT[:, 0:512].rearrange("p (a b) -> p a b", a=4)
            if g == 0:
                dst = Vaug[:, 0, 4 * cch:4 * cch + 4, 0:64]
            else:
                dst = Vaug[:, 1, cch:NT:8, 0:64]
            k.op("act", lambda e: e.activation(dst, pv[:, :, 64:128], AF.Copy),
                 reads=[psr(bank)], writes=[f"Vaug{g}"])

        def gate_from_xs(sx, cch):
            bank = 4 + tr_ctr[0] % 2
            tr_ctr[0] += 1
            st_ = tr_ctr[0] % 2
            pT = ps[bank][:].bitcast(BF16)
            for i in range(4):
                k.op("pe", lambda e, i=i: e.transpose(pT[:, i * 128:(i + 1) * 128], xs[sx][:, i * 128:(i + 1) * 128], ident16),
                     reads=[f"xs{sx}", "ident16"], writes=[psr(bank)], inc=(i == 3))
            pv = pT[:, 0:512].rearrange("p (a b) -> p a b", a=4)
            k.op("act", lambda e: e.activation(thb[st_], pv[:, :, 64:128], AF.Tanh, scale=0.5),
                 reads=[psr(bank)], writes=[f"thb{st_}"])
            k.op("dve", lambda e: e.scalar_tensor_tensor(
                out=gs[:, 4 * cch:4 * cch + 4, :], in0=thb[st_], scalar=1.0, in1=pv[:, :, 64:128], op0=ALU.add, op1=ALU.mult),
                reads=[psr(bank), f"thb{st_}"], writes=["gs"])

        def v2_half(j):
            bank = 4 + tr_ctr[0] % 2
            tr_ctr[0] += 1
            pT = ps[bank][:].bitcast(BF16)
            for r in range(16):
                k.op("pe", lambda e, r=r: e.transpose(pT[:, r * 64:(r + 1) * 64], VT2[0:64, r:2048:16], ident16[0:64, 0:64]),
                     reads=["VT2", "ident16"], writes=[psr(bank)], inc=(r == 15))
            k.op("act", lambda e: e.activation(Vaug[:, 2, j:NT:2, 0:64], pT.rearrange("p (a b) -> p a b", a=16), AF.Copy),
                 reads=[psr(bank)], writes=["Vaug2"])

        def attn_group(g, Kt, kres, first):
            d = DIL[g][1]
            nb = S // d // 128
            steps = [(r, j) for r in range(d) for j in range(nb)]
            started = {}

            L_ = S // d

            def emit_S(i):
                r, j = steps[i]
                sl = i % 4
                bank = (0, 1, 4, 5)[i % 4]
                nq = 256 if j + 1 < nb else 128
                if d == 16:
                    k0 = r * L_ + j * 128
                    ksl, qsl = slice(k0, k0 + 128), slice(k0, k0 + nq)
                    rd = [f"{kres}_{cc}" for cc in range(8)] + [f"QA_{cc}" for cc in range(8)]
                else:
                    ksl, qsl = tok(d, r, j * 128, (j + 1) * 128), tok(d, r, j * 128, j * 128 + nq)
                    t_lo = j * 128 * d
                    t_hi = (j * 128 + nq) * d - 1
                    rd = [f"{kres}_{t_lo // 512}"] + [f"QA_{cc}" for cc in range(t_lo // 512, min(t_hi // 512, 7) + 1)]
                k.op("pe", lambda e: e.matmul(ps[bank][:, 0:nq], Kt[:, ksl], QA[:, qsl], start=True, stop=False),
                     reads=rd, writes=[psr(bank)], inc=False)
                k.op("pe", lambda e: e.matmul(ps[bank][:, 0:nq], ident16, mask16[:, 0:nq], start=False, stop=True),
                     reads=["ident16", "mask16"], writes=[psr(bank)])

            def emit_exp(i):
                r, j = steps[i]
                sl = i % 4
                bank = (0, 1, 4, 5)[i % 4]
                nq = 256 if j + 1 < nb else 128
                k.op("act", lambda e: e.activation(PT[sl][:, 0:nq], ps[bank][:, 0:nq], AF.Exp, scale=0.125),
                     reads=[psr(bank)], writes=[f"PT{sl}"])

            def pv_mm(b, V_b, pt_ap, sl, st, sp):
                fill = b // 4
                obank = 2 + fill % 2
                col = (b % 4) * 128
                k.op("pe", lambda e: e.matmul(ps[obank][0:65, col:col + 128], Vaug[:, g, V_b, :], pt_ap,
                                              start=st, stop=sp),
                     reads=[f"Vaug{g}", f"PT{sl}"], writes=[psr(obank)], inc=True)

            def emit_PV(i):
                r, j = steps[i]
                sl = i % 4
                b = r * nb + j
                has_next = j + 1 < nb
                pv_mm(b, b, PT[sl][:, 0:128], sl, st=(j == 0), sp=True)
                if has_next:
                    pv_mm(b + 1, b, PT[sl][:, 128:256], sl, st=True, sp=False)
                if b % 4 == 3:
                    fill = b // 4
                    obank = 2 + fill % 2
                    b0 = fill * 4
                    runs = []
                    bb = b0
                    while bb < b0 + 4:
                        rr, jj = divmod(bb, nb)
                        ln = min(4 - (bb - b0), nb - jj)
                        runs.append((rr, jj, ln, (bb - b0) * 128))
                        bb += ln
                    for (rr, jj, ln, col) in runs:
                        dst = acc[0:65, tok(d, rr, jj * 128, (jj + ln) * 128)]
                        src = ps[obank][0:65, col:col + ln * 128]
                        if first:
                            k.op("dve", lambda e, dst=dst, src=src: e.tensor_copy(dst, src),
                                 reads=[psr(obank)], writes=["acc"])
                        else:
                            k.op("dve", lambda e, dst=dst, src=src: e.tensor_tensor(out=dst, in0=src, in1=dst, op=ALU.add),
                                 reads=[psr(obank), "acc"], writes=["acc"])

            n = len(steps)
            for i0 in range(min(3, n)):
                emit_S(i0)
            for i in range(n):
                emit_exp(i)
                if i + 3 < n:
                    emit_S(i + 3)
                emit_PV(i)

        load_w(0)
        for h in range(NH0):
            Wt = W0[h % 2]
            wres = f"W0_{h % 2}"
            if h + 1 < NH0:
                load_w(h + 1)
            bufs = (xs, t1, t2, cs16, sn16)
            run_pass(Wt, wres, 256, [(QA, 0, 64, "QA", 16)], bufs, extra=lambda sx, cch: v_from_xs(sx, 0, cch))
            run_pass(Wt, wres, 384, [(KA0, 0, 64, "KA0", 16)], bufs, extra=lambda sx, cch: v_from_xs(sx, 1, cch))

            def extra_e(sx, cch):
                k.op("act", lambda e: e.activation(VT2[0:64, (cch % 4) * 512:(cch % 4 + 1) * 512], ps[cch % 2][0:64, :], AF.Copy),
                     reads=[psr(cch % 2)], writes=["VT2"])
                gate_from_xs(sx, cch)
                if cch % 4 == 3:
                    v2_half(cch // 4)
            run_pass(Wt, wres, 512, [], bufs, extra=extra_e, do_rope=False)
            attn_group(2, KA0, "KA0", first=True)
            run_pass(Wt, wres, 0, [(QA, 0, 128, "QA", 1)], bufs)
            run_pass(Wt, wres, 128, [(KA0, 0, 64, "KA0", 1), (KA1, 64, 128, "KA1", 1)], bufs)
            attn_group(0, KA0, "KA0", first=False)
            attn_group(1, KA1, "KA1", first=False)
            hq = h % 2
            if True:
              for t0 in range(0, NT, 4):
                fb = 6 + (t0 // 4) % 2
                s = (t0 // 4) % 2
                for i in range(4):
                    tt = t0 + i
                    k.op("pe", lambda e, i=i, tt=tt: e.transpose(ps[fb][:, i * 65:(i + 1) * 65],
                                                                 acc[0:65, tt * 128:(tt + 1) * 128], identf[0:65, 0:65]),
                         reads=["acc", "identf"], writes=[psr(fb)], inc=(i == 3))
                trv = ps[fb][:, 0:260].rearrange("p (a b) -> p a b", a=4)
                k.op("dve", lambda e, trv=trv, s=s: e.reciprocal(rden[:, s * 4:s * 4 + 4], trv[:, :, 64]),
                     reads=[psr(fb)], writes=[f"rden{s}"])
                k.op("dve", lambda e, trv=trv, s=s: e.tensor_tensor(
                    out=ob[s], in0=trv[:, :, 0:64], in1=rden[:, s * 4:s * 4 + 4].unsqueeze(2).broadcast_to([128, 4, 64]),
                    op=ALU.mult), reads=[psr(fb), f"rden{s}"], writes=[f"ob{s}"])
                k.op("dve", lambda e, s=s, t0=t0: e.scalar_tensor_tensor(
                    out=y4[:, t0:t0 + 4, hq * 64:(hq + 1) * 64], in0=ob[s], scalar=0.5, in1=gs[:, t0:t0 + 4, :],
                    op0=ALU.mult, op1=ALU.mult), reads=[f"ob{s}", "gs"], writes=["y4"])
            if hq == 1:
                q4 = h // 2
                for half in range(2):
                    k.dma(y_view[:, half * 16:(half + 1) * 16, q4 * 128:(q4 + 1) * 128],
                          y4[:, half * 16:(half + 1) * 16, :], reads=["y4"], writes=["y_scr"])

    def phase_tail(layer, wo_d, res_d, dst_d, next_prenorm):
        c = Carve(PH_BASE + 5 * 8192)
        Wo = c.take([8, 1024], BF16)
        yt = [c.take([1024], BF16) for _ in range(3)]
        yT = [c.take([8, 128], BF16) for _ in range(3)]
        xr = [c.take([1024], F32) for _ in range(6)]
        hn = [c.take([1024], F32) for _ in range(3)]
        xn = [c.take([1024], BF16) for _ in range(3)]
        sqj = c.take([1024], BF16)
        sqj2 = c.take([1024], BF16)
        st2 = c.take([16], F32)
        k.dma(Wo, wo_d.rearrange("(kc p) n -> p kc n", p=128), writes=["Wo"], queue="pool")

        def st_L(tt):
            rows = slice(tt * 128, (tt + 1) * 128)
            k.dma(yt[tt % 3], y_scr[rows, :], reads=["y_scr"], writes=[f"yt{tt % 3}"])
            k.dma(xr[tt % 6], res_d[rows, :], reads=["h1_scr"] if res_d is h1_scr else [], writes=[f"xr{tt % 6}"])

        def st_A(tt):
            s3 = tt % 3
            pb = 4
            pT = ps[pb][:].bitcast(BF16)
            for kc in range(8):
                k.op("pe", lambda e, kc=kc: e.transpose(pT[:, kc * 128:(kc + 1) * 128],
                                                         yt[s3][:, kc * 128:(kc + 1) * 128], ident16),
                     reads=[f"yt{s3}", "ident16"], writes=[psr(pb)], inc=(kc == 7))
            k.op("act", lambda e: e.activation(yT[s3], pT.rearrange("p (a b) -> p a b", a=8), AF.Copy),
                 reads=[psr(pb)], writes=[f"yT{s3}"])

        def st_B(tt):
            s3 = tt % 3
            i = tt % 4
            for half in range(2):
                ob_ = ((0, 1), (2, 3), (6, 7))[tt % 3][half]
                for kc in range(8):
                    k.op("pe", lambda e, kc=kc, half=half, ob_=ob_: e.matmul(
                        ps[ob_][:], yT[s3][:, kc, :], Wo[:, kc, half * 512:(half + 1) * 512],
                        start=(kc == 0), stop=(kc == 7)),
                        reads=[f"yT{s3}", "Wo"], writes=[psr(ob_)], inc=(kc == 7))
            for half in range(2):
                ob_ = ((0, 1), (2, 3), (6, 7))[tt % 3][half]
                k.op("act", lambda e, half=half, ob_=ob_: e.activation(
                    sqj2[:, half * 512:(half + 1) * 512], ps[ob_][:], AF.Square,
                    accum_out=st2[:, 2 * i + half:2 * i + half + 1]),
                    reads=[psr(ob_)], writes=[f"sqj2_{half}", f"p_ss{i}_{half}"])

        def st_C(tt):
            s3 = tt % 3
            i = tt % 4
            rows = slice(tt * 128, (tt + 1) * 128)
            k.op("dve", lambda e: e.tensor_tensor(out=st2[:, 8 + i:9 + i], in0=st2[:, 2 * i:2 * i + 1],
                                                  in1=st2[:, 2 * i + 1:2 * i + 2], op=ALU.add),
                 reads=[f"p_ss{i}_0", f"p_ss{i}_1"], writes=[f"p_sum{i}"])
            k.op("act", lambda e: e.activation(st2[:, 8 + i:9 + i], st2[:, 8 + i:9 + i], AF.Ln, scale=1.0 / D, bias=EPS),
                 reads=[f"p_sum{i}"], writes=[f"p_sum{i}"])
            k.op("act", lambda e: e.activation(st2[:, 12 + i:13 + i], st2[:, 8 + i:9 + i], AF.Exp, scale=-0.5),
                 reads=[f"p_sum{i}"], writes=[f"p_rstd{i}"])
            for half in range(2):
                ob_ = ((0, 1), (2, 3), (6, 7))[tt % 3][half]
                k.op("dve", lambda e, half=half, ob_=ob_: e.scalar_tensor_tensor(
                    out=hn[s3][:, half * 512:(half + 1) * 512], in0=ps[ob_][:], scalar=st2[:, 12 + i:13 + i],
                    in1=gpost[:, layer, half * 512:(half + 1) * 512], op0=ALU.mult, op1=ALU.mult),
                    reads=[psr(ob_), f"p_rstd{i}", "gpost"], writes=[f"hn{s3}"])
            k.op("pool", lambda e: e.tensor_tensor(out=hn[s3], in0=hn[s3], in1=xr[tt % 6], op=ALU.add),
                 reads=[f"hn{s3}", f"xr{tt % 6}"], writes=[f"hn{s3}"])
            k.dma(dst_d[rows, :], hn[s3], reads=[f"hn{s3}"], writes=["h1_scr"] if dst_d is h1_scr else [])

        for i in range(NT + 6):
            if i < NT:
                st_L(i)
            t = i - 2
            if 0 <= t < NT:
                st_A(t)
            t = i - 3
            if 0 <= t < NT:
                st_B(t)
            t = i - 4
            if 0 <= t < NT:
                st_C(t)
            if next_prenorm:
                t = i - 5
                if 0 <= t < NT:
                    prenorm_a(hn[t % 3], f"hn{t % 3}", t, xn[t % 3], f"xn{t % 3}", sqj)
                t = i - 6
                if 0 <= t < NT:
                    prenorm_b(t, layer + 1, xn[t % 3], f"xn{t % 3}", bank=5)

    def phase_layer1():
        c = Carve(PH_BASE)
        QA = c.take([S], BF16)
        KA0 = c.take([S], BF16)
        KA1 = c.take([S], BF16)
        cs16 = c.take([S], BF16)
        sn16 = c.take([S], BF16)
        xs = [c.take([512], BF16) for _ in range(2)]
        t1 = [c.take([512], F32) for _ in range(2)]
        t2 = [c.take([512], F32) for _ in range(2)]
        PT = [c.take([512], BF16) for _ in range(4)]
        y2 = c.take([NT, 256], BF16)
        Vaug = c.take([NT, 129], BF16)
        gsl = c.take([NT, 128], BF16)
        W1 = [c.take([8, 512], BF16) for _ in range(2)]
        thb = [c.take([2, 128], F32) for _ in range(2)]
        g1b = [c.take([2, 128], F32) for _ in range(2)]
        o0 = c.take([4, 128], F32)
        od = c.take([4, 128], F32)
        sq = c.take([4, 128], F32)
        rr = c.take([32], F32)

        if stage != "full":
            k.dma(cs16, cs_d, writes=["cs16"], queue="pool", max_dma_last_dim=4096)
            k.dma(sn16, sn_d, writes=["sn16"], queue="pool", max_dma_last_dim=4096)
            k.op("pool", lambda e: e.memset(KA0[64:128, :], 0.0), writes=[f"KA0_{cc}" for cc in range(8)])
            k.op("pool", lambda e: e.memset(KA1[0:64, :], 0.0), writes=[f"KA1_{cc}" for cc in range(8)])
        k.op("pool", lambda e: e.memset(Vaug[:, :, 128:129], 1.0), writes=["Vaug"])

        def load_w(h):
            k.dma(W1[h % 2], w1_d[h].rearrange("(kc p) c -> p kc c", p=128), writes=[f"W1_{h % 2}"], queue="pool")

        load_w(0)
        for h in range(8):
            Wt = W1[h % 2]
            wres = f"W1_{h % 2}"
            if h + 1 < 8:
                load_w(h + 1)
            bufs = (xs, t1, t2, cs16, sn16)
            run_pass(Wt, wres, 0, [(QA, 0, 128, "QA", 1)], bufs)
            run_pass(Wt, wres, 128, [(KA0, 0, 64, "KA0", 1), (KA1, 64, 128, "KA1", 1)], bufs)
            for t0 in range(0, NT, 2):
                bank = 4 + (t0 // 2) % 2
                s = (t0 // 2) % 2
                for bi in range(2):
                    tt = t0 + bi
                    for kc in range(8):
                        k.op("pe", lambda e, kc=kc, bi=bi, tt=tt: e.matmul(
                            ps[bank][:, bi * 256:(bi + 1) * 256], uT[:, kc, tt * 128:(tt + 1) * 128],
                            Wt[:, kc, 256:512], start=(kc == 0), stop=(kc == 7)),
                            reads=[wres, "uT"], writes=[psr(bank)], inc=(kc == 7 and bi == 1))
                pv = ps[bank][:].rearrange("p (a b) -> p a b", a=2)
                k.op("act", lambda e, pv=pv, t0=t0: e.activation(Vaug[:, t0:t0 + 2, 0:128], pv[:, :, 0:128], AF.Copy),
                     reads=[psr(bank)], writes=["Vaug"])
                k.op("act", lambda e, pv=pv, s=s: e.activation(thb[s], pv[:, :, 128:256], AF.Tanh, scale=0.5),
                     reads=[psr(bank)], writes=[f"thb{s}"])
                k.op("dve", lambda e, pv=pv, s=s: e.scalar_tensor_tensor(
                    out=g1b[s], in0=thb[s], scalar=1.0, in1=pv[:, :, 128:256], op0=ALU.add, op1=ALU.mult),
                    reads=[psr(bank), f"thb{s}"], writes=[f"g1b{s}"])
                k.op("pool", lambda e, s=s, t0=t0: e.tensor_tensor(
                    out=gsl[:, t0:t0 + 2, :], in0=g1b[s], in1=sublnc.unsqueeze(1).broadcast_to([128, 2, 128]),
                    op=ALU.mult), reads=[f"g1b{s}", "sublnc"], writes=["gsl"])

            steps = []
            for qc in range(8):
                for comp in range(2):
                    for kb in range(4 * qc + 4):
                        steps.append((qc, comp, kb))
            cc_ctr = [0]
            started = {}

            def geom(i):
                qc, comp, kb = steps[i]
                m = kb - 4 * qc
                diag = m >= 0
                q0 = kb * 128 if diag else qc * 512
                nq = (qc + 1) * 512 - q0
                return qc, comp, kb, diag, q0, nq

            def emit_S(i):
                qc, comp, kb, diag, q0, nq = geom(i)
                sl = i % 4
                bank = 4 + sl
                Kt, kres = (KA0, "KA0") if comp == 0 else (KA1, "KA1")
                k.op("pe", lambda e: e.matmul(ps[bank][:, 0:nq], Kt[:, kb * 128:(kb + 1) * 128], QA[:, q0:q0 + nq],
                                              start=True, stop=not diag),
                     reads=[f"{kres}_{kb // 4}", f"QA_{qc}"], writes=[psr(bank)], inc=not diag)
                if diag:
                    k.op("pe", lambda e: e.matmul(ps[bank][:, 0:128], ident16, mask16[:, 0:128], start=False, stop=True),
                         reads=["ident16", "mask16"], writes=[psr(bank)])

            def emit_exp(i):
                qc, comp, kb, diag, q0, nq = geom(i)
                sl = i % 4
                bank = 4 + sl
                k.op("act", lambda e: e.activation(PT[sl][:, 0:nq], ps[bank][:, 0:nq], AF.Exp, scale=0.125),
                     reads=[psr(bank)], writes=[f"PT{sl}"])

            def emit_PV(i):
                qc, comp, kb, diag, q0, nq = geom(i)
                sl = i % 4
                cc = qc * 2 + comp
                ob0 = (cc % 2) * 2
                nt = nq // 128
                for ti in range(nt):
                    qt = q0 // 128 + ti
                    lq = qt - qc * 4
                    obank = ob0 + lq // 2
                    col = (lq % 2) * 129
                    key = (cc, lq // 2)
                    st = key not in started
                    started[key] = True
                    last = (kb == 4 * qc + 3) and (ti == nt - 1)
                    k.op("pe", lambda e, ti=ti, obank=obank, col=col, st=st: e.matmul(
                        ps[obank][:, col:col + 129], PT[sl][:, ti * 128:(ti + 1) * 128], Vaug[:, kb, :],
                        start=st, stop=True, skip_group_check=True),
                        reads=["Vaug", f"PT{sl}"], writes=[psr(obank)], inc=(ti == nt - 1))
                if kb == 4 * qc + 3:
                    t0 = qc * 4
                    for hb in range(2):
                        obank = ob0 + hb
                        ov = ps[obank][:, 0:258].rearrange("p (a b) -> p a b", a=2)
                        ro = (cc % 4) * 4 + hb * 2
                        k.op("dve", lambda e, ov=ov, ro=ro: e.reciprocal(rr[:, ro:ro + 2], ov[:, :, 128]),
                             reads=[psr(obank)], writes=[f"rr{ro}"])
                        if comp == 0:
                            k.op("dve", lambda e, ov=ov, ro=ro, hb=hb: e.tensor_tensor(
                                out=o0[:, hb * 2:hb * 2 + 2, :], in0=ov[:, :, 0:128],
                                in1=rr[:, ro:ro + 2].unsqueeze(2).broadcast_to([128, 2, 128]), op=ALU.mult),
                                reads=[psr(obank), f"rr{ro}"], writes=[f"o0_{hb}"])
                        else:
                            k.op("dve", lambda e, ro=ro: e.tensor_scalar(rr[:, ro:ro + 2], rr[:, ro:ro + 2], neglam, None, ALU.mult),
                                 reads=[f"rr{ro}", "neglam"], writes=[f"rr{ro}"])
                            k.op("dve", lambda e, ov=ov, ro=ro, hb=hb: e.tensor_tensor(
                                out=od[:, hb * 2:hb * 2 + 2, :], in0=ov[:, :, 0:128],
                                in1=rr[:, ro:ro + 2].unsqueeze(2).broadcast_to([128, 2, 128]), op=ALU.mult),
                                reads=[psr(obank), f"rr{ro}"], writes=[f"od_{hb}"])
                    if comp == 1:
                        def f1():
                            k.op("pool", lambda e: e.tensor_tensor(out=od, in0=od, in1=o0, op=ALU.add),
                                 reads=["od_0", "od_1", "o0_0", "o0_1"], writes=["od_0", "od_1"])

                        def f2():
                            k.op("dve", lambda e: e.tensor_tensor(out=sq, in0=od, in1=od, op=ALU.mult),
                                 reads=["od_0", "od_1"], writes=["sq"])
                            k.op("dve", lambda e: e.tensor_reduce(out=rr[:, 16:20], in_=sq, axis=AX.X, op=ALU.add),
                                 reads=["sq"], writes=["rr16"])

                        def f3():
                            k.op("act", lambda e: e.activation(rr[:, 20:24], rr[:, 16:20], AF.Ln, scale=1.0 / 128, bias=EPS),
                                 reads=["rr16"], writes=["rr20"])
                            k.op("act", lambda e: e.activation(rr[:, 24:28], rr[:, 20:24], AF.Exp, scale=-0.5),
                                 reads=["rr20"], writes=["rr24"])

                        def f4():
                            k.op("dve", lambda e: e.tensor_tensor(
                                out=sq, in0=od, in1=rr[:, 24:28].unsqueeze(2).broadcast_to([128, 4, 128]), op=ALU.mult),
                                reads=["od_0", "od_1", "rr24", "sq"], writes=["sq"])

                        def f5(t0=t0):
                            hp = h % 2
                            k.op("pool", lambda e: e.tensor_tensor(
                                out=y2[:, t0:t0 + 4, hp * 128:(hp + 1) * 128], in0=sq, in1=gsl[:, t0:t0 + 4, :], op=ALU.mult),
                                reads=["sq", "gsl"], writes=["y2"])
                        for dly, fn_ in ((4, f1), (8, f2), (12, f3), (15, f4), (18, f5)):
                            deferred.append((i + dly, fn_))

            deferred = []
            n = len(steps)
            emit_S(0)
            emit_S(1)
            emit_S(2)
            for i in range(n):
                emit_exp(i)
                if i + 3 < n:
                    emit_S(i + 3)
                emit_PV(i)
                deferred.sort(key=lambda x: x[0])
                while deferred and deferred[0][0] <= i:
                    deferred.pop(0)[1]()
            while deferred:
                deferred.pop(0)[1]()
            if h % 2 == 1:
                q2 = h // 2
                for half in range(2):
                    k.dma(y_view[:, half * 16:(half + 1) * 16, q2 * 256:(q2 + 1) * 256],
                          y2[:, half * 16:(half + 1) * 16, :], reads=["y2"], writes=["y_scr"])

    if stage.startswith("d_"):
        if stage >= "d_1":
            phase_prenorm_dram(x_d, 0)
            k.barrier()
        if stage >= "d_2":
            phase_layer0()
            k.barrier()
        dbg = Carve(PH_BASE).take([1024], F32)
        k.dma(dbg, x_d[0:128, :], writes=["dbg"])
        k.dma(out_d[0:128, :], dbg, reads=["dbg"])
    elif stage == "l0":
        phase_prenorm_dram(x_d, 0)
        k.barrier()
        phase_layer0()
        k.barrier()
        phase_tail(0, w0o_d, x_d, out_d, next_prenorm=False)
    elif stage == "l1":
        phase_prenorm_dram(x_d, 1)
        k.barrier()
        phase_layer1()
        k.barrier()
        phase_tail(1, w1o_d, x_d, out_d, next_prenorm=False)
    else:
        phase_prenorm_dram(x_d, 0)
        k.barrier()
        phase_layer0()
        k.barrier()
        phase_tail(0, w0o_d, x_d, h1_scr, next_prenorm=True)
        k.barrier()
        phase_layer1()
        k.barrier()
        phase_tail(1, w1o_d, h1_scr, out_d, next_prenorm=False)
    k.finish()
    return nc, k


def _consts():
    f32 = np.float32
    freqs = (np.float32(10000.0) ** (-(np.arange(0, 64, 2, dtype=f32)) / np.float32(64))).astype(f32)
    pos = np.arange(S, dtype=f32)
    ang = (pos[:, None] * freqs[None, :]).astype(f32)
    cos = np.cos(ang).astype(f32).T
    sin = np.sin(ang).astype(f32).T
    cs = np.ascontiguousarray(np.tile(cos, (4, 1)))
    sn = np.ascontiguousarray(np.tile(sin, (4, 1)))
    perm = np.zeros((128, 128), f32)
    for m in range(128):
        if (m % 64) < 32:
            perm[m + 32, m] = -1.0
        else:
            perm[m - 32, m] = 1.0
    ident = np.eye(128, dtype=f32)
    kk = np.arange(128)[:, None]
    qq = np.arange(128)[None, :]
    tri_cur = np.where(kk <= qq, 0.0, NEG).astype(f32)
    tri_prev = np.where(kk >= qq, 0.0, NEG).astype(f32)
    mask0 = np.ascontiguousarray(np.concatenate([tri_cur, tri_prev], axis=1))
    return cs, sn, perm, ident, mask0


def _layout_weights(dil_w_in, dil_w_out, diff_w_in, diff_w_out):
    w = dil_w_in[0]
    w0 = np.empty((16, 1024, 640), np.float32)
    for h in range(16):
        def qc(g):
            return slice((g * 16 + h) * 64, (g * 16 + h) * 64 + 64)

        def kc_(g):
            return slice(3072 + (g * 16 + h) * 64, 3072 + (g * 16 + h) * 64 + 64)

        def vc(g):
            return slice(6144 + (g * 16 + h) * 64, 6144 + (g * 16 + h) * 64 + 64)
        gc = slice(9216 + h * 64, 9216 + h * 64 + 64)
        cols = [qc(0), qc(1), kc_(0), kc_(1), qc(2), vc(0), kc_(2), vc(1), vc(2), gc]
        for i, sl in enumerate(cols):
            w0[h, :, i * 64:(i + 1) * 64] = w[:, sl]
    w1i = diff_w_in[0]
    w1 = np.empty((8, 1024, 512), np.float32)
    for h in range(8):
        w1[h, :, 0:128] = w1i[:, (2 * h) * 64:(2 * h + 2) * 64]
        w1[h, :, 128:256] = w1i[:, 1024 + (2 * h) * 64:1024 + (2 * h + 2) * 64]
        w1[h, :, 256:384] = w1i[:, 2048 + h * 128:2048 + (h + 1) * 128]
        w1[h, :, 384:512] = w1i[:, 3072 + h * 128:3072 + (h + 1) * 128]
    return w0, np.ascontiguousarray(dil_w_out[0]), w1, np.ascontiguousarray(diff_w_out[0])


_CACHE = {}


def _get_program(stage):
    if stage not in _CACHE:
        _CACHE[stage] = build_program(stage)[0]
    return _CACHE[stage]


def _common_maps(norm_pre, norm_post, dil_w_in, dil_w_out, diff_w_in, diff_w_out,
                 diff_lambda_q1, diff_lambda_k1, diff_lambda_q2, diff_lambda_k2, diff_subln):
    cs, sn, perm, ident, mask0 = _consts()
    w0, w0o, w1, w1o = _layout_weights(np.asarray(dil_w_in, np.float32), np.asarray(dil_w_out, np.float32),
                                       np.asarray(diff_w_in, np.float32), np.asarray(diff_w_out, np.float32))
    npre = np.asarray(norm_pre, np.float32)
    gpre = np.ascontiguousarray(npre.reshape(2, 8, 128).transpose(2, 0, 1).reshape(128, 16))
    lamv = np.ascontiguousarray(np.concatenate([np.asarray(a, np.float32).reshape(1, 64) for a in
                                                (diff_lambda_q1, diff_lambda_k1, diff_lambda_q2, diff_lambda_k2)], 0))
    return {"w0": w0, "w0o": w0o, "w1": w1, "w1o": w1o, "gpre": gpre,
            "gpost": np.ascontiguousarray(np.asarray(norm_post, np.float32)),
            "lamv": lamv, "subln": np.ascontiguousarray(np.asarray(diff_subln, np.float32).reshape(1, 128)),
            "cs": cs, "sn": sn, "perm": perm, "ident": ident, "mask0": mask0}


def run_stage(stage, xs, common):
    nc = _get_program(stage)
    in_maps = [dict(common, x=np.ascontiguousarray(xs[b])) for b in range(len(xs))]
    res = run_bass_kernel_spmd(nc, in_maps, core_ids=list(range(len(xs))))
    return np.stack([r["out"] for r in res.results], 0)


def kernel(x, norm_pre, norm_post, dil_w_in, dil_w_out, diff_w_in, diff_w_out,
           diff_lambda_q1, diff_lambda_k1, diff_lambda_q2, diff_lambda_k2, diff_subln):
    x = np.asarray(x, np.float32)
    common = _common_maps(norm_pre, norm_post, dil_w_in, dil_w_out, diff_w_in, diff_w_out,
                          diff_lambda_q1, diff_lambda_k1, diff_lambda_q2, diff_lambda_k2, diff_subln)
    out = run_stage("full", [x[b] for b in range(8)], common)
    return out.astype(np.float32)
```

```python
import contextlib
import math
import numpy as np
import concourse.bass as bass
import concourse.mybir as mybir
from concourse.bass_utils import run_bass_kernel_spmd

F32 = mybir.dt.float32
BF16 = mybir.dt.bfloat16
AF = mybir.ActivationFunctionType
ALU = mybir.AluOpType
AX = mybir.AxisListType

SEM_ROT = 30000
N_DMA_SEMS = 24

S = 4096
D = 1024
NT = 32
NEG = -30000.0
NH0 = 16
DBGV = 0
ROPE_STEPS = 5
L0_STEPS = 10
EPS = 1e-6
LAMBDA_INIT = 0.8 - 0.6 * math.exp(-0.3 * 1)
DIL = ((128, 1), (512, 4), (2048, 16))


class _Rec:
    def __init__(self):
        self.call = None

    def __getattr__(self, name):
        def f(*a, **kw):
            self.call = (name, a, kw)
            return self
        return f


def _capture(fn):
    r = _Rec()
    fn(r)
    name, a, kw = r.call
    return lambda e: getattr(e, name)(*a, **kw)


class KB:
    ENGS = ("pe", "act", "dve", "pool", "sp")

    def __init__(self, nc):
        self.nc = nc
        self.stack = contextlib.ExitStack()
        self.prog = {e: [] for e in self.ENGS}
        self.sem = {}
        self.cnt = {}
        self.nsem = 0
        self.waited = {e: {} for e in self.ENGS}
        self.res = {}
        self.pending = {e: [] for e in self.ENGS}
        self.last_tok = {e: None for e in self.ENGS}
        for e in ("pe", "act", "dve", "pool"):
            self._new_eng_sem(e)
        self.dma_sems = []
        self.dma_pools = {"sp": [], "pool": []}
        for i in range(N_DMA_SEMS):
            h = self.stack.enter_context(nc.semaphore(f"dq{i}"))
            self.dma_sems.append([h, 0, f"dq{i}"])
            self.dma_pools["sp" if i < 16 else "pool"].append(self.dma_sems[-1])
        self.dma_rr = {"sp": 0, "pool": 0}
        self.n_instr = {e: 0 for e in self.ENGS}

    def sb(self, name, shape, dt):
        return self.stack.enter_context(self.nc.sbuf_tensor(name, list(shape), dt))

    def ps(self, name, shape, dt):
        return self.stack.enter_context(self.nc.psum_tensor(name, list(shape), dt))

    def _new_eng_sem(self, e):
        name = f"s_{e}_{self.nsem}"
        self.nsem += 1
        h = self.stack.enter_context(self.nc.semaphore(name))
        self.sem[e] = (h, name)
        self.cnt[e] = 0

    def _wait(self, eng, tok):
        if tok is None:
            return
        h, name, val, src = tok
        if src == eng and eng == "pe":
            return
        w = self.waited[eng]
        if w.get(name, 0) >= val:
            return
        w[name] = val
        self.prog[eng].append(lambda e, h=h, val=val: e.wait_ge(h, val))

    def _deps(self, reads, writes):
        deps = []
        for r in reads:
            st = self.res.get(r)
            if st and st[0] is not None:
                deps.append(st[0])
        for w in writes:
            st = self.res.get(w)
            if st:
                if st[0] is not None:
                    deps.append(st[0])
                deps.extend(st[1])
        return deps

    @staticmethod
    def _max_per_sem(deps):
        best = {}
        for t in deps:
            if t is None:
                continue
            if t[1] not in best or best[t[1]][2] < t[2]:
                best[t[1]] = t
        return list(best.values())

    def _register(self, tok, reads, writes):
        for r in reads:
            st = self.res.setdefault(r, [None, []])
            st[1].append(tok)
            if len(st[1]) > 48:
                best = {}
                for t in st[1]:
                    if t[1] not in best or best[t[1]][2] < t[2]:
                        best[t[1]] = t
                st[1] = list(best.values())
        for w in writes:
            self.res[w] = [tok, []]

    def op(self, eng, fn, reads=(), writes=(), inc=True):
        fn = _capture(fn)
        writes = tuple(writes) + tuple(r for r in reads if r.startswith("ps"))
        reads = tuple(r for r in reads if not r.startswith("ps"))
        for tok in self._max_per_sem(self._deps(reads, writes)):
            self._wait(eng, tok)
        self.n_instr[eng] += 1
        if not inc:
            self.pending[eng].append((reads, writes))
            self.prog[eng].append(lambda e, fn=fn: fn(e))
            return None
        if self.cnt[eng] >= SEM_ROT:
            self._new_eng_sem(eng)
        h, name = self.sem[eng]
        self.cnt[eng] += 1
        tok = (h, name, self.cnt[eng], eng)
        self.prog[eng].append(lambda e, fn=fn, h=h: fn(e).then_inc(h, 1))
        for (r, w) in self.pending[eng]:
            self._register(tok, r, w)
        self.pending[eng] = []
        self._register(tok, reads, writes)
        self.last_tok[eng] = tok
        return tok

    def dma(self, out, in_, reads=(), writes=(), queue="sp", **kw):
        reads = tuple(reads)
        writes = tuple(writes)
        pool = self.dma_pools[queue]
        slot = pool[self.dma_rr[queue]]
        self.dma_rr[queue] = (self.dma_rr[queue] + 1) % len(pool)
        h, cur, name = slot
        if cur > 0:
            self._wait(queue, (h, name, cur, "dma"))
        for tok in self._max_per_sem(self._deps(reads, writes)):
            self._wait(queue, tok)
        slot[1] = cur + 16
        tok = (h, name, cur + 16, "dma")
        self.prog[queue].append(
            lambda e, out=out, in_=in_, h=h, kw=kw: e.dma_start(out=out, in_=in_, **kw).then_inc(h, 16))
        self._register(tok, reads, writes)
        self.n_instr[queue] += 1
        return tok

    def barrier(self):
        toks = [t for t in self.last_tok.values() if t is not None]
        dtoks = [(s[0], s[2], s[1], "dma") for s in self.dma_sems if s[1] > 0]
        for e in self.ENGS:
            for t in toks + dtoks:
                self._wait(e, t)

    def finish(self):
        toks = [t for t in self.last_tok.values() if t is not None]
        dtoks = [(s[0], s[2], s[1], "dma") for s in self.dma_sems if s[1] > 0]
        for t in toks + dtoks:
            self._wait("sp", t)
        with self.nc.Block() as block:
            @block.tensor
            def _(e):
                for f in self.prog["pe"]:
                    f(e)

            @block.scalar
            def _(e):
                for f in self.prog["act"]:
                    f(e)

            @block.vector
            def _(e):
                for f in self.prog["dve"]:
                    f(e)

            @block.gpsimd
            def _(e):
                for f in self.prog["pool"]:
                    f(e)

            @block.sync
            def _(e):
                for f in self.prog["sp"]:
                    f(e)
        self.stack.close()


def _prod(xs):
    p = 1
    for v in xs:
        p *= v
    return p


def build_program(stage):
    nc = bass.Bass("TRN2", target_bir_lowering=False)

    def din(name, shape):
        return nc.dram_tensor(name, list(shape), F32, kind="ExternalInput").ap()

    x_d = din("x", [S, D])
    w0_d = din("w0", [16, 1024, 640])
    w0o_d = din("w0o", [1024, 1024])
    w1_d = din("w1", [8, 1024, 512])
    w1o_d = din("w1o", [1024, 1024])
    gpre_d = din("gpre", [128, 16])
    gpost_d = din("gpost", [2, 1024])
    lamv_d = din("lamv", [4, 64])
    subln_d = din("subln", [1, 128])
    cs_d = din("cs", [128, S])
    sn_d = din("sn", [128, S])
    perm_d = din("perm", [128, 128])
    ident_d = din("ident", [128, 128])
    mask_d = din("mask0", [128, 256])
    out_d = nc.dram_tensor("out", [S, D], F32, kind="ExternalOutput").ap()
    y_scr = nc.dram_tensor("y_scr", [S, D], BF16, kind="Internal").ap()
    h1_scr = nc.dram_tensor("h1_scr", [S, D], F32, kind="Internal").ap()

    k = KB(nc)
    TOTAL = 206 * 1024
    M = k.sb("M", [128, TOTAL // 2], BF16)

    class Carve:
        def __init__(self, base):
            self.off = base

        def take(self, fs, dt):
            esz = 2 if dt == BF16 else 4
            nb = _prod(fs) * esz
            assert self.off % 4 == 0
            v = M[:, self.off // 2:(self.off + nb) // 2]
            if dt == F32:
                v = v.bitcast(F32)
            if len(fs) == 2:
                v = v.rearrange("p (a b) -> p a b", a=fs[0])
            elif len(fs) == 3:
                v = v.rearrange("p (a b c) -> p a b c", a=fs[0], b=fs[1])
            self.off += (nb + 63) // 64 * 64
            assert self.off <= TOTAL, (self.off, TOTAL)
            return v

    cm = Carve(0)
    uT = cm.take([8, S], BF16)
    ident16 = cm.take([128], BF16)
    perm16 = cm.take([128], BF16)
    mask16 = cm.take([256], BF16)
    identf = cm.take([128], F32)
    gpre = cm.take([16], F32)
    gpost = cm.take([2, 1024], F32)
    lamb = cm.take([4, 64], F32)
    lprod = cm.take([2, 64], F32)
    lsc = cm.take([8], F32)
    sublnc = cm.take([128], F32)
    stat = cm.take([16], F32)
    PH_BASE = cm.off

    ps = [k.ps(f"ps{i}", [128, 512], F32) for i in range(8)]

    def psr(i):
        return f"ps{i}"

    k.dma(ident16, ident_d, writes=["ident16"], queue="pool")
    k.dma(perm16, perm_d, writes=["perm16"], queue="pool")
    k.dma(mask16, mask_d, writes=["mask16"], queue="pool")
    k.dma(identf, ident_d, writes=["identf"])
    k.dma(gpre, gpre_d, writes=["gpre"])
    k.dma(gpost, gpost_d.unsqueeze(0).broadcast_to([128, 2, 1024]), writes=["gpost"])
    for i in range(8):
        k.op("dve", lambda e, i=i: e.memset(ps[i][:], 0.0), writes=[psr(i)])

    lam_needed = stage in ("full", "l1")
    if lam_needed:
        k.dma(lamb, lamv_d.unsqueeze(0).broadcast_to([128, 4, 64]), writes=["lamb"])
        k.dma(sublnc, subln_d.broadcast_to([128, 128]), writes=["sublnc_raw"])
        lv = lamb.rearrange("p (a b) c -> p a b c", a=2)
        k.op("dve", lambda e: e.tensor_tensor(out=lprod, in0=lv[:, :, 0, :], in1=lv[:, :, 1, :], op=ALU.mult),
             reads=["lamb"], writes=["lprod"])
        k.op("dve", lambda e: e.tensor_reduce(out=lsc[:, 0:2], in_=lprod, axis=AX.X, op=ALU.add),
             reads=["lprod"], writes=["lsc01"])
        k.op("act", lambda e: e.activation(lsc[:, 2:4], lsc[:, 0:2], AF.Exp), reads=["lsc01"], writes=["lsc23"])
        k.op("dve", lambda e: e.tensor_tensor(out=lsc[:, 4:5], in0=lsc[:, 3:4], in1=lsc[:, 2:3], op=ALU.subtract),
             reads=["lsc23"], writes=["lsc4"])
        k.op("dve", lambda e: e.tensor_scalar(lsc[:, 5:6], lsc[:, 4:5], -LAMBDA_INIT, None, ALU.add),
             reads=["lsc4"], writes=["neglam"])
        k.op("dve", lambda e: e.tensor_scalar(sublnc, sublnc, 0.5 * (1.0 - LAMBDA_INIT), None, ALU.mult),
             reads=["sublnc_raw"], writes=["sublnc"])
    neglam = lsc[:, 5:6]

    y_view = y_scr.rearrange("(t p) c -> p t c", p=128)

    def prenorm_a(src, src_res, tt, xn, xn_res, sqj):
        i = tt % 4
        k.op("act", lambda e: e.activation(sqj, src, AF.Square, accum_out=stat[:, i:i + 1]),
             reads=[src_res], writes=["sqj", f"ss{i}"])
        k.op("act", lambda e: e.activation(stat[:, 4 + i:5 + i], stat[:, i:i + 1], AF.Ln, scale=1.0 / D, bias=EPS),
             reads=[f"ss{i}"], writes=[f"lnv{i}"])
        k.op("act", lambda e: e.activation(stat[:, 8 + i:9 + i], stat[:, 4 + i:5 + i], AF.Exp, scale=-0.5),
             reads=[f"lnv{i}"], writes=[f"rstd{i}"])
        k.op("dve", lambda e: e.tensor_scalar(xn, src, stat[:, 8 + i:9 + i], None, ALU.mult),
             reads=[src_res, f"rstd{i}"], writes=[xn_res])

    def prenorm_b(tt, layer, xn, xn_res, bank=None):
        pb = 6 + (tt % 2) if bank is None else bank
        pT = ps[pb][:].bitcast(BF16)
        for kc in range(8):
            k.op("pe", lambda e, kc=kc: e.transpose(pT[:, kc * 128:(kc + 1) * 128], xn[:, kc * 128:(kc + 1) * 128],
                                                     ident16),
                 reads=[xn_res, "ident16"], writes=[psr(pb)], inc=(kc == 7))
        gb = gpre[:, layer * 8:(layer + 1) * 8].unsqueeze(2).broadcast_to([128, 8, 128])
        k.op("dve", lambda e: e.tensor_tensor(out=uT[:, :, tt * 128:(tt + 1) * 128],
                                              in0=pT.rearrange("p (a b) -> p a b", a=8), in1=gb, op=ALU.mult),
             reads=[psr(pb), "gpre"], writes=["uT"])

    def phase_prenorm_dram(src_d, layer):
        c = Carve(PH_BASE)
        xt = [c.take([1024], F32) for _ in range(4)]
        xn = [c.take([1024], BF16) for _ in range(3)]
        sqj = c.take([1024], BF16)
        for i in range(NT + 3):
            t = i
            if t < NT:
                k.dma(xt[t % 4], src_d[t * 128:(t + 1) * 128, :], writes=[f"xt{t % 4}"])
            t = i - 2
            if 0 <= t < NT:
                prenorm_a(xt[t % 4], f"xt{t % 4}", t, xn[t % 3], f"xn{t % 3}", sqj)
            t = i - 3
            if 0 <= t < NT:
                prenorm_b(t, layer, xn[t % 3], f"xn{t % 3}")

    rope_ctr = [0]

    def rope_a(bank, xs, eng="act"):
        s = rope_ctr[0] % 2
        rope_ctr[0] += 1
        if eng == "act":
            k.op("act", lambda e: e.activation(xs[s], ps[bank][:], AF.Copy), reads=[psr(bank)], writes=[f"xs{s}"])
        else:
            k.op("dve", lambda e: e.tensor_copy(xs[s], ps[bank][:]), reads=[psr(bank)], writes=[f"xs{s}"])
        return s

    def rope_b(s, bank, c, outs, xs, t1, t2, cs16, sn16):
        rb = 2 + s
        ch = slice(c * 512, (c + 1) * 512)
        k.op("pe", lambda e: e.matmul(ps[rb][:], perm16, xs[s], start=True, stop=True),
             reads=[f"xs{s}", "perm16"], writes=[psr(rb)])
        k.op("dve", lambda e: e.tensor_tensor(out=t1[s], in0=ps[bank][:], in1=cs16[:, ch], op=ALU.mult),
             reads=[psr(bank), "cs16"], writes=[f"t1{s}"])
        k.op("dve", lambda e: e.tensor_tensor(out=t2[s], in0=ps[rb][:], in1=sn16[:, ch], op=ALU.mult),
             reads=[psr(rb), "sn16"], writes=[f"t2{s}"])
        for oi, (dest, lo, hi, res, dd) in enumerate(outs):
            eng = "pool"
            if dd == 1:
                k.op(eng, lambda e, dest=dest, lo=lo, hi=hi: e.tensor_tensor(
                    out=dest[lo:hi, ch], in0=t1[s][lo:hi, :], in1=t2[s][lo:hi, :], op=ALU.add),
                    reads=[f"t1{s}", f"t2{s}"], writes=[f"{res}_{c}"])
            else:
                n = 512 // dd
                dv = dest[lo:hi, :].rearrange("p (r i) -> p r i", r=dd)[:, :, c * n:(c + 1) * n]
                a0 = t1[s][lo:hi, :].rearrange("p (i r) -> p r i", r=dd)
                a1 = t2[s][lo:hi, :].rearrange("p (i r) -> p r i", r=dd)
                k.op(eng, lambda e, dv=dv, a0=a0, a1=a1: e.tensor_tensor(out=dv, in0=a0, in1=a1, op=ALU.add),
                     reads=[f"t1{s}", f"t2{s}"], writes=[f"{res}_{cc}" for cc in range(8)])

    def run_pass(Wt, wres, col0, outs, bufs, extra=None, do_rope=True, copy_eng="act"):
        xs, t1, t2, cs16, sn16 = bufs
        slots = {}

        def stage2(c):
            if do_rope:
                rope_b(slots[c], c % 2, c, outs, xs, t1, t2, cs16, sn16)
            if extra is not None:
                extra(slots[c], c)

        for c in range(8):
            proj_fm(Wt, wres, col0, c % 2, c)
            slots[c] = rope_a(c % 2, xs, copy_eng)
            if c > 0:
                stage2(c - 1)
        stage2(7)

    def proj_fm(Wt, wres, col0, bank, c):
        for kc in range(8):
            k.op("pe", lambda e, kc=kc: e.matmul(ps[bank][:], Wt[:, kc, col0:col0 + 128],
                                                 uT[:, kc, c * 512:(c + 1) * 512], start=(kc == 0), stop=(kc == 7)),
                 reads=[wres, "uT"], writes=[psr(bank)], inc=(kc == 7))

    def phase_layer0():
        c = Carve(PH_BASE)
        QA = c.take([S], BF16)
        KA0 = c.take([S], BF16)
        KA1 = c.take([S], BF16)
        cs16 = c.take([S], BF16)
        sn16 = c.take([S], BF16)
        xs = [c.take([512], BF16) for _ in range(2)]
        t1 = [c.take([512], F32) for _ in range(2)]
        t2 = [c.take([512], F32) for _ in range(2)]
        PT = [c.take([256], BF16) for _ in range(4)]
        y4 = c.take([NT, 128], BF16)
        VT2 = c.take([2048], BF16)
        Vaug = c.take([3, NT, 65], BF16)
        gs = c.take([NT, 64], BF16)
        acc = c.take([S], F32)
        W0 = [c.take([8, 640], BF16) for _ in range(2)]
        thb = [c.take([4, 64], F32) for _ in range(2)]
        ob = [c.take([4, 64], F32) for _ in range(2)]
        rden = c.take([8], F32)
        m01 = c.take([256], BF16)
        k.op("dve", lambda e: e.tensor_scalar(m01, mask16, 0.0, None, ALU.is_equal), reads=["mask16"], writes=["m01"])

        k.dma(cs16, cs_d, writes=["cs16"], queue="pool", max_dma_last_dim=4096)
        k.dma(sn16, sn_d, writes=["sn16"], queue="pool", max_dma_last_dim=4096)
        k.op("pool", lambda e: e.memset(KA0[64:128, :], 0.0), writes=[f"KA0_{cc}" for cc in range(8)])
        k.op("pool", lambda e: e.memset(KA1[0:64, :], 0.0), writes=[f"KA1_{cc}" for cc in range(8)])
        k.op("pool", lambda e: e.memset(QA, 0.0), writes=[f"QA_{cc}" for cc in range(8)])
        k.op("pool", lambda e: e.memset(Vaug[:, :, :, 64:65], 1.0), writes=["Vaug0", "Vaug1", "Vaug2"])

        def load_w(h):
            k.dma(W0[h % 2], w0_d[h].rearrange("(kc p) c -> p kc c", p=128), writes=[f"W0_{h % 2}"], queue="pool")

        def tok(d, r, a, b):
            return slice(r + a * d, r + (b - 1) * d + 1, d)

        tr_ctr = [0]

        def v_from_xs(sx, g, cch):
            bank = 4 + tr_ctr[0] % 2
            tr_ctr[0] += 1
            pT = ps[bank][:].bitcast(BF16)
            for i in range(4):
                src = xs[sx][:, i * 128:(i + 1) * 128] if g == 0 else xs[sx][:, i:512:4]
                k.op("pe", lambda e, i=i, src=src: e.transpose(pT[:, i * 128:(i + 1) * 128], src, ident16),
                     reads=[f"xs{sx}", "ident16"], writes=[psr(bank)], inc=(i == 3))
            pv = pT[:, 0:512].rearrange("p (a b) -> p a b", a=4)
            if g == 0:
                dst = Vaug[:, 0, 4 * cch:4 * cch + 4, 0:64]
            else:
                dst = Vaug[:, 1, cch:NT:8, 0:64]
            k.op("act", lambda e: e.activation(dst, pv[:, :, 64:128], AF.Copy),
                 reads=[psr(bank)], writes=[f"Vaug{g}"])

        def gate_from_xs(sx, cch):
            bank = 4 + tr_ctr[0] % 2
            tr_ctr[0] += 1
            st_ = tr_ctr[0] % 2
            pT = ps[bank][:].bitcast(BF16)
            for i in range(4):
                k.op("pe", lambda e, i=i: e.transpose(pT[:, i * 128:(i + 1) * 128], xs[sx][:, i * 128:(i + 1) * 128], ident16),
                     reads=[f"xs{sx}", "ident16"], writes=[psr(bank)], inc=(i == 3))
            pv = pT[:, 0:512].rearrange("p (a b) -> p a b", a=4)
            k.op("act", lambda e: e.activation(thb[st_], pv[:, :, 64:128], AF.Tanh, scale=0.5),
                 reads=[psr(bank)], writes=[f"thb{st_}"])
            k.op("dve", lambda e: e.scalar_tensor_tensor(
                out=gs[:, 4 * cch:4 * cch + 4, :], in0=thb[st_], scalar=1.0, in1=pv[:, :, 64:128], op0=ALU.add, op1=ALU.mult),
                reads=[psr(bank), f"thb{st_}"], writes=["gs"])

        def v2_half(j):
            bank = 4 + tr_ctr[0] % 2
            tr_ctr[0] += 1
            pT = ps[bank][:].bitcast(BF16)
            for r in range(16):
                k.op("pe", lambda e, r=r: e.transpose(pT[:, r * 64:(r + 1) * 64], VT2[0:64, r:2048:16], ident16[0:64, 0:64]),
                     reads=["VT2", "ident16"], writes=[psr(bank)], inc=(r == 15))
            k.op("act", lambda e: e.activation(Vaug[:, 2, j:NT:2, 0:64], pT.rearrange("p (a b) -> p a b", a=16), AF.Copy),
                 reads=[psr(bank)], writes=["Vaug2"])

        def attn_group(g, Kt, kres, first):
            d = DIL[g][1]
            nb = S // d // 128
            steps = [(r, j) for r in range(d) for j in range(nb)]
            started = {}

            L_ = S // d

            def emit_S(i):
                r, j = steps[i]
                sl = i % 4
                bank = (0, 1, 4, 5)[i % 4]
                nq = 256 if j + 1 < nb else 128
                if d == 16:
                    k0 = r * L_ + j * 128
                    ksl, qsl = slice(k0, k0 + 128), slice(k0, k0 + nq)
                    rd = [f"{kres}_{cc}" for cc in range(8)] + [f"QA_{cc}" for cc in range(8)]
                else:
                    ksl, qsl = tok(d, r, j * 128, (j + 1) * 128), tok(d, r, j * 128, j * 128 + nq)
                    t_lo = j * 128 * d
                    t_hi = (j * 128 + nq) * d - 1
                    rd = [f"{kres}_{t_lo // 512}"] + [f"QA_{cc}" for cc in range(t_lo // 512, min(t_hi // 512, 7) + 1)]
                k.op("pe", lambda e: e.matmul(ps[bank][:, 0:nq], Kt[:, ksl], QA[:, qsl], start=True, stop=False),
                     reads=rd, writes=[psr(bank)], inc=False)
                k.op("pe", lambda e: e.matmul(ps[bank][:, 0:nq], ident16, mask16[:, 0:nq], start=False, stop=True),
                     reads=["ident16", "mask16"], writes=[psr(bank)])

            def emit_exp(i):
                r, j = steps[i]
                sl = i % 4
                bank = (0, 1, 4, 5)[i % 4]
                nq = 256 if j + 1 < nb else 128
                k.op("act", lambda e: e.activation(PT[sl][:, 0:nq], ps[bank][:, 0:nq], AF.Exp, scale=0.125),
                     reads=[psr(bank)], writes=[f"PT{sl}"])

            def pv_mm(b, V_b, pt_ap, sl, st, sp):
                fill = b // 4
                obank = 2 + fill % 2
                col = (b % 4) * 128
                k.op("pe", lambda e: e.matmul(ps[obank][0:65, col:col + 128], Vaug[:, g, V_b, :], pt_ap,
                                              start=st, stop=sp),
                     reads=[f"Vaug{g}", f"PT{sl}"], writes=[psr(obank)], inc=True)

            def emit_PV(i):
                r, j = steps[i]
                sl = i % 4
                b = r * nb + j
                has_next = j + 1 < nb
                pv_mm(b, b, PT[sl][:, 0:128], sl, st=(j == 0), sp=True)
                if has_next:
                    pv_mm(b + 1, b, PT[sl][:, 128:256], sl, st=True, sp=False)
                if b % 4 == 3:
                    fill = b // 4
                    obank = 2 + fill % 2
                    b0 = fill * 4
                    runs = []
                    bb = b0
                    while bb < b0 + 4:
                        rr, jj = divmod(bb, nb)
                        ln = min(4 - (bb - b0), nb - jj)
                        runs.append((rr, jj, ln, (bb - b0) * 128))
                        bb += ln
                    for (rr, jj, ln, col) in runs:
                        dst = acc[0:65, tok(d, rr, jj * 128, (jj + ln) * 128)]
                        src = ps[obank][0:65, col:col + ln * 128]
                        if first:
                            k.op("dve", lambda e, dst=dst, src=src: e.tensor_copy(dst, src),
                                 reads=[psr(obank)], writes=["acc"])
                        else:
                            k.op("dve", lambda e, dst=dst, src=src: e.tensor_tensor(out=dst, in0=src, in1=dst, op=ALU.add),
                                 reads=[psr(obank), "acc"], writes=["acc"])

            n = len(steps)
            for i0 in range(min(3, n)):
                emit_S(i0)
            for i in range(n):
                emit_exp(i)
                if i + 3 < n:
                    emit_S(i + 3)
                emit_PV(i)

        load_w(0)
        for h in range(NH0):
            Wt = W0[h % 2]
            wres = f"W0_{h % 2}"
            if h + 1 < NH0:
                load_w(h + 1)
            bufs = (xs, t1, t2, cs16, sn16)
            run_pass(Wt, wres, 256, [(QA, 0, 64, "QA", 16)], bufs, extra=lambda sx, cch: v_from_xs(sx, 0, cch))
            run_pass(Wt, wres, 384, [(KA0, 0, 64, "KA0", 16)], bufs, extra=lambda sx, cch: v_from_xs(sx, 1, cch))

            def extra_e(sx, cch):
                k.op("act", lambda e: e.activation(VT2[0:64, (cch % 4) * 512:(cch % 4 + 1) * 512], ps[cch % 2][0:64, :], AF.Copy),
                     reads=[psr(cch % 2)], writes=["VT2"])
                gate_from_xs(sx, cch)
                if cch % 4 == 3:
                    v2_half(cch // 4)
            run_pass(Wt, wres, 512, [], bufs, extra=extra_e, do_rope=False)
            attn_group(2, KA0, "KA0", first=True)
            run_pass(Wt, wres, 0, [(QA, 0, 128, "QA", 1)], bufs)
            run_pass(Wt, wres, 128, [(KA0, 0, 64, "KA0", 1), (KA1, 64, 128, "KA1", 1)], bufs)
            attn_group(0, KA0, "KA0", first=False)
            attn_group(1, KA1, "KA1", first=False)
            hq = h % 2
            if True:
              for t0 in range(0, NT, 4):
                fb = 6 + (t0 // 4) % 2
                s = (t0 // 4) % 2
                for i in range(4):
                    tt = t0 + i
                    k.op("pe", lambda e, i=i, tt=tt: e.transpose(ps[fb][:, i * 65:(i + 1) * 65],
                                                                 acc[0:65, tt * 128:(tt + 1) * 128], identf[0:65, 0:65]),
                         reads=["acc", "identf"], writes=[psr(fb)], inc=(i == 3))
                trv = ps[fb][:, 0:260].rearrange("p (a b) -> p a b", a=4)
                k.op("dve", lambda e, trv=trv, s=s: e.reciprocal(rden[:, s * 4:s * 4 + 4], trv[:, :, 64]),
                     reads=[psr(fb)], writes=[f"rden{s}"])
                k.op("dve", lambda e, trv=trv, s=s: e.tensor_tensor(
                    out=ob[s], in0=trv[:, :, 0:64], in1=rden[:, s * 4:s * 4 + 4].unsqueeze(2).broadcast_to([128, 4, 64]),
                    op=ALU.mult), reads=[psr(fb), f"rden{s}"], writes=[f"ob{s}"])
                k.op("dve", lambda e, s=s, t0=t0: e.scalar_tensor_tensor(
                    out=y4[:, t0:t0 + 4, hq * 64:(hq + 1) * 64], in0=ob[s], scalar=0.5, in1=gs[:, t0:t0 + 4, :],
                    op0=ALU.mult, op1=ALU.mult), reads=[f"ob{s}", "gs"], writes=["y4"])
            if hq == 1:
                q4 = h // 2
                for half in range(2):
                    k.dma(y_view[:, half * 16:(half + 1) * 16, q4 * 128:(q4 + 1) * 128],
                          y4[:, half * 16:(half + 1) * 16, :], reads=["y4"], writes=["y_scr"])

    def phase_tail(layer, wo_d, res_d, dst_d, next_prenorm):
        c = Carve(PH_BASE + 5 * 8192)
        Wo = c.take([8, 1024], BF16)
        yt = [c.take([1024], BF16) for _ in range(3)]
        yT = [c.take([8, 128], BF16) for _ in range(3)]
        xr = [c.take([1024], F32) for _ in range(6)]
        hn = [c.take([1024], F32) for _ in range(3)]
        xn = [c.take([1024], BF16) for _ in range(3)]
        sqj = c.take([1024], BF16)
        sqj2 = c.take([1024], BF16)
        st2 = c.take([16], F32)
        k.dma(Wo, wo_d.rearrange("(kc p) n -> p kc n", p=128), writes=["Wo"], queue="pool")

        def st_L(tt):
            rows = slice(tt * 128, (tt + 1) * 128)
            k.dma(yt[tt % 3], y_scr[rows, :], reads=["y_scr"], writes=[f"yt{tt % 3}"])
            k.dma(xr[tt % 6], res_d[rows, :], reads=["h1_scr"] if res_d is h1_scr else [], writes=[f"xr{tt % 6}"])

        def st_A(tt):
            s3 = tt % 3
            pb = 4
            pT = ps[pb][:].bitcast(BF16)
            for kc in range(8):
                k.op("pe", lambda e, kc=kc: e.transpose(pT[:, kc * 128:(kc + 1) * 128],
                                                         yt[s3][:, kc * 128:(kc + 1) * 128], ident16),
                     reads=[f"yt{s3}", "ident16"], writes=[psr(pb)], inc=(kc == 7))
            k.op("act", lambda e: e.activation(yT[s3], pT.rearrange("p (a b) -> p a b", a=8), AF.Copy),
                 reads=[psr(pb)], writes=[f"yT{s3}"])

        def st_B(tt):
            s3 = tt % 3
            i = tt % 4
            for half in range(2):
                ob_ = ((0, 1), (2, 3), (6, 7))[tt % 3][half]
                for kc in range(8):
                    k.op("pe", lambda e, kc=kc, half=half, ob_=ob_: e.matmul(
                        ps[ob_][:], yT[s3][:, kc, :], Wo[:, kc, half * 512:(half + 1) * 512],
                        start=(kc == 0), stop=(kc == 7)),
                        reads=[f"yT{s3}", "Wo"], writes=[psr(ob_)], inc=(kc == 7))
            for half in range(2):
                ob_ = ((0, 1), (2, 3), (6, 7))[tt % 3][half]
                k.op("act", lambda e, half=half, ob_=ob_: e.activation(
                    sqj2[:, half * 512:(half + 1) * 512], ps[ob_][:], AF.Square,
                    accum_out=st2[:, 2 * i + half:2 * i + half + 1]),
                    reads=[psr(ob_)], writes=[f"sqj2_{half}", f"p_ss{i}_{half}"])

        def st_C(tt):
            s3 = tt % 3
            i = tt % 4
            rows = slice(tt * 128, (tt + 1) * 128)
            k.op("dve", lambda e: e.tensor_tensor(out=st2[:, 8 + i:9 + i], in0=st2[:, 2 * i:2 * i + 1],
                                                  in1=st2[:, 2 * i + 1:2 * i + 2], op=ALU.add),
                 reads=[f"p_ss{i}_0", f"p_ss{i}_1"], writes=[f"p_sum{i}"])
            k.op("act", lambda e: e.activation(st2[:, 8 + i:9 + i], st2[:, 8 + i:9 + i], AF.Ln, scale=1.0 / D, bias=EPS),
                 reads=[f"p_sum{i}"], writes=[f"p_sum{i}"])
            k.op("act", lambda e: e.activation(st2[:, 12 + i:13 + i], st2[:, 8 + i:9 + i], AF.Exp, scale=-0.5),
                 reads=[f"p_sum{i}"], writes=[f"p_rstd{i}"])
            for half in range(2):
                ob_ = ((0, 1), (2, 3), (6, 7))[tt % 3][half]
                k.op("dve", lambda e, half=half, ob_=ob_: e.scalar_tensor_tensor(
                    out=hn[s3][:, half * 512:(half + 1) * 512], in0=ps[ob_][:], scalar=st2[:, 12 + i:13 + i],
                    in1=gpost[:, layer, half * 512:(half + 1) * 512], op0=ALU.mult, op1=ALU.mult),
                    reads=[psr(ob_), f"p_rstd{i}", "gpost"], writes=[f"hn{s3}"])
            k.op("pool", lambda e: e.tensor_tensor(out=hn[s3], in0=hn[s3], in1=xr[tt % 6], op=ALU.add),
                 reads=[f"hn{s3}", f"xr{tt % 6}"], writes=[f"hn{s3}"])
            k.dma(dst_d[rows, :], hn[s3], reads=[f"hn{s3}"], writes=["h1_scr"] if dst_d is h1_scr else [])

        for i in range(NT + 6):
            if i < NT:
                st_L(i)
            t = i - 2
            if 0 <= t < NT:
                st_A(t)
            t = i - 3
            if 0 <= t < NT:
                st_B(t)
            t = i - 4
            if 0 <= t < NT:
                st_C(t)
            if next_prenorm:
                t = i - 5
                if 0 <= t < NT:
                    prenorm_a(hn[t % 3], f"hn{t % 3}", t, xn[t % 3], f"xn{t % 3}", sqj)
                t = i - 6
                if 0 <= t < NT:
                    prenorm_b(t, layer + 1, xn[t % 3], f"xn{t % 3}", bank=5)

    def phase_layer1():
        c = Carve(PH_BASE)
        QA = c.take([S], BF16)
        KA0 = c.take([S], BF16)
        KA1 = c.take([S], BF16)
        cs16 = c.take([S], BF16)
        sn16 = c.take([S], BF16)
        xs = [c.take([512], BF16) for _ in range(2)]
        t1 = [c.take([512], F32) for _ in range(2)]
        t2 = [c.take([512], F32) for _ in range(2)]
        PT = [c.take([512], BF16) for _ in range(4)]
        y2 = c.take([NT, 256], BF16)
        Vaug = c.take([NT, 129], BF16)
        gsl = c.take([NT, 128], BF16)
        W1 = [c.take([8, 512], BF16) for _ in range(2)]
        thb = [c.take([2, 128], F32) for _ in range(2)]
        g1b = [c.take([2, 128], F32) for _ in range(2)]
        o0 = c.take([4, 128], F32)
        od = c.take([4, 128], F32)
        sq = c.take([4, 128], F32)
        rr = c.take([32], F32)

        if stage != "full":
            k.dma(cs16, cs_d, writes=["cs16"], queue="pool", max_dma_last_dim=4096)
            k.dma(sn16, sn_d, writes=["sn16"], queue="pool", max_dma_last_dim=4096)
            k.op("pool", lambda e: e.memset(KA0[64:128, :], 0.0), writes=[f"KA0_{cc}" for cc in range(8)])
            k.op("pool", lambda e: e.memset(KA1[0:64, :], 0.0), writes=[f"KA1_{cc}" for cc in range(8)])
        k.op("pool", lambda e: e.memset(Vaug[:, :, 128:129], 1.0), writes=["Vaug"])

        def load_w(h):
            k.dma(W1[h % 2], w1_d[h].rearrange("(kc p) c -> p kc c", p=128), writes=[f"W1_{h % 2}"], queue="pool")

        load_w(0)
        for h in range(8):
            Wt = W1[h % 2]
            wres = f"W1_{h % 2}"
            if h + 1 < 8:
                load_w(h + 1)
            bufs = (xs, t1, t2, cs16, sn16)
            run_pass(Wt, wres, 0, [(QA, 0, 128, "QA", 1)], bufs)
            run_pass(Wt, wres, 128, [(KA0, 0, 64, "KA0", 1), (KA1, 64, 128, "KA1", 1)], bufs)
            for t0 in range(0, NT, 2):
                bank = 4 + (t0 // 2) % 2
                s = (t0 // 2) % 2
                for bi in range(2):
                    tt = t0 + bi
                    for kc in range(8):
                        k.op("pe", lambda e, kc=kc, bi=bi, tt=tt: e.matmul(
                            ps[bank][:, bi * 256:(bi + 1) * 256], uT[:, kc, tt * 128:(tt + 1) * 128],
                            Wt[:, kc, 256:512], start=(kc == 0), stop=(kc == 7)),
                            reads=[wres, "uT"], writes=[psr(bank)], inc=(kc == 7 and bi == 1))
                pv = ps[bank][:].rearrange("p (a b) -> p a b", a=2)
                k.op("act", lambda e, pv=pv, t0=t0: e.activation(Vaug[:, t0:t0 + 2, 0:128], pv[:, :, 0:128], AF.Copy),
                     reads=[psr(bank)], writes=["Vaug"])
                k.op("act", lambda e, pv=pv, s=s: e.activation(thb[s], pv[:, :, 128:256], AF.Tanh, scale=0.5),
                     reads=[psr(bank)], writes=[f"thb{s}"])
                k.op("dve", lambda e, pv=pv, s=s: e.scalar_tensor_tensor(
                    out=g1b[s], in0=thb[s], scalar=1.0, in1=pv[:, :, 128:256], op0=ALU.add, op1=ALU.mult),
                    reads=[psr(bank), f"thb{s}"], writes=[f"g1b{s}"])
                k.op("pool", lambda e, s=s, t0=t0: e.tensor_tensor(
                    out=gsl[:, t0:t0 + 2, :], in0=g1b[s], in1=sublnc.unsqueeze(1).broadcast_to([128, 2, 128]),
                    op=ALU.mult), reads=[f"g1b{s}", "sublnc"], writes=["gsl"])

            steps = []
            for qc in range(8):
                for comp in range(2):
                    for kb in range(4 * qc + 4):
                        steps.append((qc, comp, kb))
            cc_ctr = [0]
            started = {}

            def geom(i):
                qc, comp, kb = steps[i]
                m = kb - 4 * qc
                diag = m >= 0
                q0 = kb * 128 if diag else qc * 512
                nq = (qc + 1) * 512 - q0
                return qc, comp, kb, diag, q0, nq

            def emit_S(i):
                qc, comp, kb, diag, q0, nq = geom(i)
                sl = i % 4
                bank = 4 + sl
                Kt, kres = (KA0, "KA0") if comp == 0 else (KA1, "KA1")
                k.op("pe", lambda e: e.matmul(ps[bank][:, 0:nq], Kt[:, kb * 128:(kb + 1) * 128], QA[:, q0:q0 + nq],
                                              start=True, stop=not diag),
                     reads=[f"{kres}_{kb // 4}", f"QA_{qc}"], writes=[psr(bank)], inc=not diag)
                if diag:
                    k.op("pe", lambda e: e.matmul(ps[bank][:, 0:128], ident16, mask16[:, 0:128], start=False, stop=True),
                         reads=["ident16", "mask16"], writes=[psr(bank)])

            def emit_exp(i):
                qc, comp, kb, diag, q0, nq = geom(i)
                sl = i % 4
                bank = 4 + sl
                k.op("act", lambda e: e.activation(PT[sl][:, 0:nq], ps[bank][:, 0:nq], AF.Exp, scale=0.125),
                     reads=[psr(bank)], writes=[f"PT{sl}"])

            def emit_PV(i):
                qc, comp, kb, diag, q0, nq = geom(i)
                sl = i % 4
                cc = qc * 2 + comp
                ob0 = (cc % 2) * 2
                nt = nq // 128
                for ti in range(nt):
                    qt = q0 // 128 + ti
                    lq = qt - qc * 4
                    obank = ob0 + lq // 2
                    col = (lq % 2) * 129
                    key = (cc, lq // 2)
                    st = key not in started
                    started[key] = True
                    last = (kb == 4 * qc + 3) and (ti == nt - 1)
                    k.op("pe", lambda e, ti=ti, obank=obank, col=col, st=st: e.matmul(
                        ps[obank][:, col:col + 129], PT[sl][:, ti * 128:(ti + 1) * 128], Vaug[:, kb, :],
                        start=st, stop=True, skip_group_check=True),
                        reads=["Vaug", f"PT{sl}"], writes=[psr(obank)], inc=(ti == nt - 1))
                if kb == 4 * qc + 3:
                    t0 = qc * 4
                    for hb in range(2):
                        obank = ob0 + hb
                        ov = ps[obank][:, 0:258].rearrange("p (a b) -> p a b", a=2)
                        ro = (cc % 4) * 4 + hb * 2
                        k.op("dve", lambda e, ov=ov, ro=ro: e.reciprocal(rr[:, ro:ro + 2], ov[:, :, 128]),
                             reads=[psr(obank)], writes=[f"rr{ro}"])
                        if comp == 0:
                            k.op("dve", lambda e, ov=ov, ro=ro, hb=hb: e.tensor_tensor(
                                out=o0[:, hb * 2:hb * 2 + 2, :], in0=ov[:, :, 0:128],
                                in1=rr[:, ro:ro + 2].unsqueeze(2).broadcast_to([128, 2, 128]), op=ALU.mult),
                                reads=[psr(obank), f"rr{ro}"], writes=[f"o0_{hb}"])
                        else:
                            k.op("dve", lambda e, ro=ro: e.tensor_scalar(rr[:, ro:ro + 2], rr[:, ro:ro + 2], neglam, None, ALU.mult),
                                 reads=[f"rr{ro}", "neglam"], writes=[f"rr{ro}"])
                            k.op("dve", lambda e, ov=ov, ro=ro, hb=hb: e.tensor_tensor(
                                out=od[:, hb * 2:hb * 2 + 2, :], in0=ov[:, :, 0:128],
                                in1=rr[:, ro:ro + 2].unsqueeze(2).broadcast_to([128, 2, 128]), op=ALU.mult),
                                reads=[psr(obank), f"rr{ro}"], writes=[f"od_{hb}"])
                    if comp == 1:
                        def f1():
                            k.op("pool", lambda e: e.tensor_tensor(out=od, in0=od, in1=o0, op=ALU.add),
                                 reads=["od_0", "od_1", "o0_0", "o0_1"], writes=["od_0", "od_1"])

                        def f2():
                            k.op("dve", lambda e: e.tensor_tensor(out=sq, in0=od, in1=od, op=ALU.mult),
                                 reads=["od_0", "od_1"], writes=["sq"])
                            k.op("dve", lambda e: e.tensor_reduce(out=rr[:, 16:20], in_=sq, axis=AX.X, op=ALU.add),
                                 reads=["sq"], writes=["rr16"])

                        def f3():
                            k.op("act", lambda e: e.activation(rr[:, 20:24], rr[:, 16:20], AF.Ln, scale=1.0 / 128, bias=EPS),
                                 reads=["rr16"], writes=["rr20"])
                            k.op("act", lambda e: e.activation(rr[:, 24:28], rr[:, 20:24], AF.Exp, scale=-0.5),
                                 reads=["rr20"], writes=["rr24"])

                        def f4():
                            k.op("dve", lambda e: e.tensor_tensor(
                                out=sq, in0=od, in1=rr[:, 24:28].unsqueeze(2).broadcast_to([128, 4, 128]), op=ALU.mult),
                                reads=["od_0", "od_1", "rr24", "sq"], writes=["sq"])

                        def f5(t0=t0):
                            hp = h % 2
                            k.op("pool", lambda e: e.tensor_tensor(
                                out=y2[:, t0:t0 + 4, hp * 128:(hp + 1) * 128], in0=sq, in1=gsl[:, t0:t0 + 4, :], op=ALU.mult),
                                reads=["sq", "gsl"], writes=["y2"])
                        for dly, fn_ in ((4, f1), (8, f2), (12, f3), (15, f4), (18, f5)):
                            deferred.append((i + dly, fn_))

            deferred = []
            n = len(steps)
            emit_S(0)
            emit_S(1)
            emit_S(2)
            for i in range(n):
                emit_exp(i)
                if i + 3 < n:
                    emit_S(i + 3)
                emit_PV(i)
                deferred.sort(key=lambda x: x[0])
                while deferred and deferred[0][0] <= i:
                    deferred.pop(0)[1]()
            while deferred:
                deferred.pop(0)[1]()
            if h % 2 == 1:
                q2 = h // 2
                for half in range(2):
                    k.dma(y_view[:, half * 16:(half + 1) * 16, q2 * 256:(q2 + 1) * 256],
                          y2[:, half * 16:(half + 1) * 16, :], reads=["y2"], writes=["y_scr"])

    if stage.startswith("d_"):
        if stage >= "d_1":
            phase_prenorm_dram(x_d, 0)
            k.barrier()
        if stage >= "d_2":
            phase_layer0()
            k.barrier()
        dbg = Carve(PH_BASE).take([1024], F32)
        k.dma(dbg, x_d[0:128, :], writes=["dbg"])
        k.dma(out_d[0:128, :], dbg, reads=["dbg"])
    elif stage == "l0":
        phase_prenorm_dram(x_d, 0)
        k.barrier()
        phase_layer0()
        k.barrier()
        phase_tail(0, w0o_d, x_d, out_d, next_prenorm=False)
    elif stage == "l1":
        phase_prenorm_dram(x_d, 1)
        k.barrier()
        phase_layer1()
        k.barrier()
        phase_tail(1, w1o_d, x_d, out_d, next_prenorm=False)
    else:
        phase_prenorm_dram(x_d, 0)
        k.barrier()
        phase_layer0()
        k.barrier()
        phase_tail(0, w0o_d, x_d, h1_scr, next_prenorm=True)
        k.barrier()
        phase_layer1()
        k.barrier()
        phase_tail(1, w1o_d, h1_scr, out_d, next_prenorm=False)
    k.finish()
    return nc, k


def _consts():
    f32 = np.float32
    freqs = (np.float32(10000.0) ** (-(np.arange(0, 64, 2, dtype=f32)) / np.float32(64))).astype(f32)
    pos = np.arange(S, dtype=f32)
    ang = (pos[:, None] * freqs[None, :]).astype(f32)
    cos = np.cos(ang).astype(f32).T
    sin = np.sin(ang).astype(f32).T
    cs = np.ascontiguousarray(np.tile(cos, (4, 1)))
    sn = np.ascontiguousarray(np.tile(sin, (4, 1)))
    perm = np.zeros((128, 128), f32)
    for m in range(128):
        if (m % 64) < 32:
            perm[m + 32, m] = -1.0
        else:
            perm[m - 32, m] = 1.0
    ident = np.eye(128, dtype=f32)
    kk = np.arange(128)[:, None]
    qq = np.arange(128)[None, :]
    tri_cur = np.where(kk <= qq, 0.0, NEG).astype(f32)
    tri_prev = np.where(kk >= qq, 0.0, NEG).astype(f32)
    mask0 = np.ascontiguousarray(np.concatenate([tri_cur, tri_prev], axis=1))
    return cs, sn, perm, ident, mask0


def _layout_weights(dil_w_in, dil_w_out, diff_w_in, diff_w_out):
    w = dil_w_in[0]
    w0 = np.empty((16, 1024, 640), np.float32)
    for h in range(16):
        def qc(g):
            return slice((g * 16 + h) * 64, (g * 16 + h) * 64 + 64)

        def kc_(g):
            return slice(3072 + (g * 16 + h) * 64, 3072 + (g * 16 + h) * 64 + 64)

        def vc(g):
            return slice(6144 + (g * 16 + h) * 64, 6144 + (g * 16 + h) * 64 + 64)
        gc = slice(9216 + h * 64, 9216 + h * 64 + 64)
        cols = [qc(0), qc(1), kc_(0), kc_(1), qc(2), vc(0), kc_(2), vc(1), vc(2), gc]
        for i, sl in enumerate(cols):
            w0[h, :, i * 64:(i + 1) * 64] = w[:, sl]
    w1i = diff_w_in[0]
    w1 = np.empty((8, 1024, 512), np.float32)
    for h in range(8):
        w1[h, :, 0:128] = w1i[:, (2 * h) * 64:(2 * h + 2) * 64]
        w1[h, :, 128:256] = w1i[:, 1024 + (2 * h) * 64:1024 + (2 * h + 2) * 64]
        w1[h, :, 256:384] = w1i[:, 2048 + h * 128:2048 + (h + 1) * 128]
        w1[h, :, 384:512] = w1i[:, 3072 + h * 128:3072 + (h + 1) * 128]
    return w0, np.ascontiguousarray(dil_w_out[0]), w1, np.ascontiguousarray(diff_w_out[0])


_CACHE = {}


def _get_program(stage):
    if stage not in _CACHE:
        _CACHE[stage] = build_program(stage)[0]
    return _CACHE[stage]


def _common_maps(norm_pre, norm_post, dil_w_in, dil_w_out, diff_w_in, diff_w_out,
                 diff_lambda_q1, diff_lambda_k1, diff_lambda_q2, diff_lambda_k2, diff_subln):
    cs, sn, perm, ident, mask0 = _consts()
    w0, w0o, w1, w1o = _layout_weights(np.asarray(dil_w_in, np.float32), np.asarray(dil_w_out, np.float32),
                                       np.asarray(diff_w_in, np.float32), np.asarray(diff_w_out, np.float32))
    npre = np.asarray(norm_pre, np.float32)
    gpre = np.ascontiguousarray(npre.reshape(2, 8, 128).transpose(2, 0, 1).reshape(128, 16))
    lamv = np.ascontiguousarray(np.concatenate([np.asarray(a, np.float32).reshape(1, 64) for a in
                                                (diff_lambda_q1, diff_lambda_k1, diff_lambda_q2, diff_lambda_k2)], 0))
    return {"w0": w0, "w0o": w0o, "w1": w1, "w1o": w1o, "gpre": gpre,
            "gpost": np.ascontiguousarray(np.asarray(norm_post, np.float32)),
            "lamv": lamv, "subln": np.ascontiguousarray(np.asarray(diff_subln, np.float32).reshape(1, 128)),
            "cs": cs, "sn": sn, "perm": perm, "ident": ident, "mask0": mask0}


def run_stage(stage, xs, common):
    nc = _get_program(stage)
    in_maps = [dict(common, x=np.ascontiguousarray(xs[b])) for b in range(len(xs))]
    res = run_bass_kernel_spmd(nc, in_maps, core_ids=list(range(len(xs))))
    return np.stack([r["out"] for r in res.results], 0)


def kernel(x, norm_pre, norm_post, dil_w_in, dil_w_out, diff_w_in, diff_w_out,
           diff_lambda_q1, diff_lambda_k1, diff_lambda_q2, diff_lambda_k2, diff_subln):
    x = np.asarray(x, np.float32)
    common = _common_maps(norm_pre, norm_post, dil_w_in, dil_w_out, diff_w_in, diff_w_out,
                          diff_lambda_q1, diff_lambda_k1, diff_lambda_q2, diff_lambda_k2, diff_subln)
    out = run_stage("full", [x[b] for b in range(8)], common)
    return out.astype(np.float32)
```

```python
import contextlib
import math
import numpy as np
import concourse.bass as bass
import concourse.mybir as mybir
from concourse.bass_utils import run_bass_kernel_spmd

F32 = mybir.dt.float32
BF16 = mybir.dt.bfloat16
AF = mybir.ActivationFunctionType
ALU = mybir.AluOpType
AX = mybir.AxisListType

SEM_ROT = 30000
N_DMA_SEMS = 24

S = 4096
D = 1024
NT = 32
NEG = -30000.0
NH0 = 16
DBGV = 0
ROPE_STEPS = 5
L0_STEPS = 10
EPS = 1e-6
LAMBDA_INIT = 0.8 - 0.6 * math.exp(-0.3 * 1)
DIL = ((128, 1), (512, 4), (2048, 16))


class _Rec:
    def __init__(self):
        self.call = None

    def __getattr__(self, name):
        def f(*a, **kw):
            self.call = (name, a, kw)
            return self
        return f


def _capture(fn):
    r = _Rec()
    fn(r)
    name, a, kw = r.call
    return lambda e: getattr(e, name)(*a, **kw)


class KB:
    ENGS = ("pe", "act", "dve", "pool", "sp")

    def __init__(self, nc):
        self.nc = nc
        self.stack = contextlib.ExitStack()
        self.prog = {e: [] for e in self.ENGS}
        self.sem = {}
        self.cnt = {}
        self.nsem = 0
        self.waited = {e: {} for e in self.ENGS}
        self.res = {}
        self.pending = {e: [] for e in self.ENGS}
        self.last_tok = {e: None for e in self.ENGS}
        for e in ("pe", "act", "dve", "pool"):
            self._new_eng_sem(e)
        self.dma_sems = []
        self.dma_pools = {"sp": [], "pool": []}
        for i in range(N_DMA_SEMS):
            h = self.stack.enter_context(nc.semaphore(f"dq{i}"))
            self.dma_sems.append([h, 0, f"dq{i}"])
            self.dma_pools["sp" if i < 16 else "pool"].append(self.dma_sems[-1])
        self.dma_rr = {"sp": 0, "pool": 0}
        self.n_instr = {e: 0 for e in self.ENGS}

    def sb(self, name, shape, dt):
        return self.stack.enter_context(self.nc.sbuf_tensor(name, list(shape), dt))

    def ps(self, name, shape, dt):
        return self.stack.enter_context(self.nc.psum_tensor(name, list(shape), dt))

    def _new_eng_sem(self, e):
        name = f"s_{e}_{self.nsem}"
        self.nsem += 1
        h = self.stack.enter_context(self.nc.semaphore(name))
        self.sem[e] = (h, name)
        self.cnt[e] = 0

    def _wait(self, eng, tok):
        if tok is None:
            return
        h, name, val, src = tok
        if src == eng and eng == "pe":
            return
        w = self.waited[eng]
        if w.get(name, 0) >= val:
            return
        w[name] = val
        self.prog[eng].append(lambda e, h=h, val=val: e.wait_ge(h, val))

    def _deps(self, reads, writes):
        deps = []
        for r in reads:
            st = self.res.get(r)
            if st and st[0] is not None:
                deps.append(st[0])
        for w in writes:
            st = self.res.get(w)
            if st:
                if st[0] is not None:
                    deps.append(st[0])
                deps.extend(st[1])
        return deps

    @staticmethod
    def _max_per_sem(deps):
        best = {}
        for t in deps:
            if t is None:
                continue
            if t[1] not in best or best[t[1]][2] < t[2]:
                best[t[1]] = t
        return list(best.values())

    def _register(self, tok, reads, writes):
        for r in reads:
            st = self.res.setdefault(r, [None, []])
            st[1].append(tok)
            if len(st[1]) > 48:
                best = {}
                for t in st[1]:
                    if t[1] not in best or best[t[1]][2] < t[2]:
                        best[t[1]] = t
                st[1] = list(best.values())
        for w in writes:
            self.res[w] = [tok, []]

    def op(self, eng, fn, reads=(), writes=(), inc=True):
        fn = _capture(fn)
        writes = tuple(writes) + tuple(r for r in reads if r.startswith("ps"))
        reads = tuple(r for r in reads if not r.startswith("ps"))
        for tok in self._max_per_sem(self._deps(reads, writes)):
            self._wait(eng, tok)
        self.n_instr[eng] += 1
        if not inc:
            self.pending[eng].append((reads, writes))
            self.prog[eng].append(lambda e, fn=fn: fn(e))
            return None
        if self.cnt[eng] >= SEM_ROT:
            self._new_eng_sem(eng)
        h, name = self.sem[eng]
        self.cnt[eng] += 1
        tok = (h, name, self.cnt[eng], eng)
        self.prog[eng].append(lambda e, fn=fn, h=h: fn(e).then_inc(h, 1))
        for (r, w) in self.pending[eng]:
            self._register(tok, r, w)
        self.pending[eng] = []
        self._register(tok, reads, writes)
        self.last_tok[eng] = tok
        return tok

    def dma(self, out, in_, reads=(), writes=(), queue="sp", **kw):
        reads = tuple(reads)
        writes = tuple(writes)
        pool = self.dma_pools[queue]
        slot = pool[self.dma_rr[queue]]
        self.dma_rr[queue] = (self.dma_rr[queue] + 1) % len(pool)
        h, cur, name = slot
        if cur > 0:
            self._wait(queue, (h, name, cur, "dma"))
        for tok in self._max_per_sem(self._deps(reads, writes)):
            self._wait(queue, tok)
        slot[1] = cur + 16
        tok = (h, name, cur + 16, "dma")
        self.prog[queue].append(
            lambda e, out=out, in_=in_, h=h, kw=kw: e.dma_start(out=out, in_=in_, **kw).then_inc(h, 16))
        self._register(tok, reads, writes)
        self.n_instr[queue] += 1
        return tok

    def barrier(self):
        toks = [t for t in self.last_tok.values() if t is not None]
        dtoks = [(s[0], s[2], s[1], "dma") for s in self.dma_sems if s[1] > 0]
        for e in self.ENGS:
            for t in toks + dtoks:
                self._wait(e, t)

    def finish(self):
        toks = [t for t in self.last_tok.values() if t is not None]
        dtoks = [(s[0], s[2], s[1], "dma") for s in self.dma_sems if s[1] > 0]
        for t in toks + dtoks:
            self._wait("sp", t)
        with self.nc.Block() as block:
            @block.tensor
            def _(e):
                for f in self.prog["pe"]:
                    f(e)

            @block.scalar
            def _(e):
                for f in self.prog["act"]:
                    f(e)

            @block.vector
            def _(e):
                for f in self.prog["dve"]:
                    f(e)

            @block.gpsimd
            def _(e):
                for f in self.prog["pool"]:
                    f(e)

            @block.sync
            def _(e):
                for f in self.prog["sp"]:
                    f(e)
        self.stack.close()


def _prod(xs):
    p = 1
    for v in xs:
        p *= v
    return p


def build_program(stage):
    nc = bass.Bass("TRN2", target_bir_lowering=False)

    def din(name, shape):
        return nc.dram_tensor(name, list(shape), F32, kind="ExternalInput").ap()

    x_d = din("x", [S, D])
    w0_d = din("w0", [16, 1024, 640])
    w0o_d = din("w0o", [1024, 1024])
    w1_d = din("w1", [8, 1024, 512])
    w1o_d = din("w1o", [1024, 1024])
    gpre_d = din("gpre", [128, 16])
    gpost_d = din("gpost", [2, 1024])
    lamv_d = din("lamv", [4, 64])
    subln_d = din("subln", [1, 128])
    cs_d = din("cs", [128, S])
    sn_d = din("sn", [128, S])
    perm_d = din("perm", [128, 128])
    ident_d = din("ident", [128, 128])
    mask_d = din("mask0", [128, 256])
    out_d = nc.dram_tensor("out", [S, D], F32, kind="ExternalOutput").ap()
    y_scr = nc.dram_tensor("y_scr", [S, D], BF16, kind="Internal").ap()
    h1_scr = nc.dram_tensor("h1_scr", [S, D], F32, kind="Internal").ap()

    k = KB(nc)
    TOTAL = 206 * 1024
    M = k.sb("M", [128, TOTAL // 2], BF16)

    class Carve:
        def __init__(self, base):
            self.off = base

        def take(self, fs, dt):
            esz = 2 if dt == BF16 else 4
            nb = _prod(fs) * esz
            assert self.off % 4 == 0
            v = M[:, self.off // 2:(self.off + nb) // 2]
            if dt == F32:
                v = v.bitcast(F32)
            if len(fs) == 2:
                v = v.rearrange("p (a b) -> p a b", a=fs[0])
            elif len(fs) == 3:
                v = v.rearrange("p (a b c) -> p a b c", a=fs[0], b=fs[1])
            self.off += (nb + 63) // 64 * 64
            assert self.off <= TOTAL, (self.off, TOTAL)
            return v

    cm = Carve(0)
    uT = cm.take([8, S], BF16)
    ident16 = cm.take([128], BF16)
    perm16 = cm.take([128], BF16)
    mask16 = cm.take([256], BF16)
    identf = cm.take([128], F32)
    gpre = cm.take([16], F32)
    gpost = cm.take([2, 1024], F32)
    lamb = cm.take([4, 64], F32)
    lprod = cm.take([2, 64], F32)
    lsc = cm.take([8], F32)
    sublnc = cm.take([128], F32)
    stat = cm.take([16], F32)
    PH_BASE = cm.off

    ps = [k.ps(f"ps{i}", [128, 512], F32) for i in range(8)]

    def psr(i):
        return f"ps{i}"

    k.dma(ident16, ident_d, writes=["ident16"], queue="pool")
    k.dma(perm16, perm_d, writes=["perm16"], queue="pool")
    k.dma(mask16, mask_d, writes=["mask16"], queue="pool")
    k.dma(identf, ident_d, writes=["identf"])
    k.dma(gpre, gpre_d, writes=["gpre"])
    k.dma(gpost, gpost_d.unsqueeze(0).broadcast_to([128, 2, 1024]), writes=["gpost"])
    for i in range(8):
        k.op("dve", lambda e, i=i: e.memset(ps[i][:], 0.0), writes=[psr(i)])

    lam_needed = stage in ("full", "l1")
    if lam_needed:
        k.dma(lamb, lamv_d.unsqueeze(0).broadcast_to([128, 4, 64]), writes=["lamb"])
        k.dma(sublnc, subln_d.broadcast_to([128, 128]), writes=["sublnc_raw"])
        lv = lamb.rearrange("p (a b) c -> p a b c", a=2)
        k.op("dve", lambda e: e.tensor_tensor(out=lprod, in0=lv[:, :, 0, :], in1=lv[:, :, 1, :], op=ALU.mult),
             reads=["lamb"], writes=["lprod"])
        k.op("dve", lambda e: e.tensor_reduce(out=lsc[:, 0:2], in_=lprod, axis=AX.X, op=ALU.add),
             reads=["lprod"], writes=["lsc01"])
        k.op("act", lambda e: e.activation(lsc[:, 2:4], lsc[:, 0:2], AF.Exp), reads=["lsc01"], writes=["lsc23"])
        k.op("dve", lambda e: e.tensor_tensor(out=lsc[:, 4:5], in0=lsc[:, 3:4], in1=lsc[:, 2:3], op=ALU.subtract),
             reads=["lsc23"], writes=["lsc4"])
        k.op("dve", lambda e: e.tensor_scalar(lsc[:, 5:6], lsc[:, 4:5], -LAMBDA_INIT, None, ALU.add),
             reads=["lsc4"], writes=["neglam"])
        k.op("dve", lambda e: e.tensor_scalar(sublnc, sublnc, 0.5 * (1.0 - LAMBDA_INIT), None, ALU.mult),
             reads=["sublnc_raw"], writes=["sublnc"])
    neglam = lsc[:, 5:6]

    y_view = y_scr.rearrange("(t p) c -> p t c", p=128)

    def prenorm_a(src, src_res, tt, xn, xn_res, sqj):
        i = tt % 4
        k.op("act", lambda e: e.activation(sqj, src, AF.Square, accum_out=stat[:, i:i + 1]),
             reads=[src_res], writes=["sqj", f"ss{i}"])
        k.op("act", lambda e: e.activation(stat[:, 4 + i:5 + i], stat[:, i:i + 1], AF.Ln, scale=1.0 / D, bias=EPS),
             reads=[f"ss{i}"], writes=[f"lnv{i}"])
        k.op("act", lambda e: e.activation(stat[:, 8 + i:9 + i], stat[:, 4 + i:5 + i], AF.Exp, scale=-0.5),
             reads=[f"lnv{i}"], writes=[f"rstd{i}"])
        k.op("dve", lambda e: e.tensor_scalar(xn, src, stat[:, 8 + i:9 + i], None, ALU.mult),
             reads=[src_res, f"rstd{i}"], writes=[xn_res])

    def prenorm_b(tt, layer, xn, xn_res, bank=None):
        pb = 6 + (tt % 2) if bank is None else bank
        pT = ps[pb][:].bitcast(BF16)
        for kc in range(8):
            k.op("pe", lambda e, kc=kc: e.transpose(pT[:, kc * 128:(kc + 1) * 128], xn[:, kc * 128:(kc + 1) * 128],
                                                     ident16),
                 reads=[xn_res, "ident16"], writes=[psr(pb)], inc=(kc == 7))
        gb = gpre[:, layer * 8:(layer + 1) * 8].unsqueeze(2).broadcast_to([128, 8, 128])
        k.op("dve", lambda e: e.tensor_tensor(out=uT[:, :, tt * 128:(tt + 1) * 128],
                                              in0=pT.rearrange("p (a b) -> p a b", a=8), in1=gb, op=ALU.mult),
             reads=[psr(pb), "gpre"], writes=["uT"])

    def phase_prenorm_dram(src_d, layer):
        c = Carve(PH_BASE)
        xt = [c.take([1024], F32) for _ in range(4)]
        xn = [c.take([1024], BF16) for _ in range(3)]
        sqj = c.take([1024], BF16)
        for i in range(NT + 3):
            t = i
            if t < NT:
                k.dma(xt[t % 4], src_d[t * 128:(t + 1) * 128, :], writes=[f"xt{t % 4}"])
            t = i - 2
            if 0 <= t < NT:
                prenorm_a(xt[t % 4], f"xt{t % 4}", t, xn[t % 3], f"xn{t % 3}", sqj)
            t = i - 3
            if 0 <= t < NT:
                prenorm_b(t, layer, xn[t % 3], f"xn{t % 3}")

    rope_ctr = [0]

    def rope_a(bank, xs, eng="act"):
        s = rope_ctr[0] % 3
        rope_ctr[0] += 1
        if eng == "act":
            k.op("act", lambda e: e.activation(xs[s], ps[bank][:], AF.Copy), reads=[psr(bank)], writes=[f"xs{s}"])
        else:
            k.op("dve", lambda e: e.tensor_copy(xs[s], ps[bank][:]), reads=[psr(bank)], writes=[f"xs{s}"])
        return s

    def rope_b(s, bank, c, outs, xs, t1, t2, cs16, sn16):
        rb = 2
        ch = slice(c * 512, (c + 1) * 512)
        xs_s = s
        s = c % 2
        k.op("pe", lambda e: e.matmul(ps[rb][:], perm16, xs[xs_s], start=True, stop=True),
             reads=[f"xs{xs_s}", "perm16"], writes=[psr(rb)])
        k.op("dve", lambda e: e.tensor_tensor(out=t1[s], in0=ps[bank][:], in1=cs16[:, ch], op=ALU.mult),
             reads=[psr(bank), "cs16"], writes=[f"t1{s}"])
        k.op("dve", lambda e: e.tensor_tensor(out=t2[s], in0=ps[rb][:], in1=sn16[:, ch], op=ALU.mult),
             reads=[psr(rb), "sn16"], writes=[f"t2{s}"])
        for oi, (dest, lo, hi, res, dd) in enumerate(outs):
            eng = "pool"
            if dd == 1:
                k.op(eng, lambda e, dest=dest, lo=lo, hi=hi: e.tensor_tensor(
                    out=dest[lo:hi, ch], in0=t1[s][lo:hi, :], in1=t2[s][lo:hi, :], op=ALU.add),
                    reads=[f"t1{s}", f"t2{s}"], writes=[f"{res}_{c}"])
            else:
                n = 512 // dd
                dv = dest[lo:hi, :].rearrange("p (r i) -> p r i", r=dd)[:, :, c * n:(c + 1) * n]
                a0 = t1[s][lo:hi, :].rearrange("p (i r) -> p r i", r=dd)
                a1 = t2[s][lo:hi, :].rearrange("p (i r) -> p r i", r=dd)
                k.op(eng, lambda e, dv=dv, a0=a0, a1=a1: e.tensor_tensor(out=dv, in0=a0, in1=a1, op=ALU.add),
                     reads=[f"t1{s}", f"t2{s}"], writes=[f"{res}_{cc}" for cc in range(8)])

    def run_pass(Wt, wres, col0, outs, bufs, extra=None, do_rope=True, copy_eng="act"):
        xs, t1, t2, cs16, sn16 = bufs
        slots = {}

        XB = (0, 1, 3)

        def stage2(c):
            if do_rope:
                rope_b(slots[c], XB[c % 3], c, outs, xs, t1, t2, cs16, sn16)
            if extra is not None:
                extra(slots[c], c)

        for c in range(8):
            proj_fm(Wt, wres, col0, XB[c % 3], c)
            slots[c] = rope_a(XB[c % 3], xs, copy_eng)
            if c > 0:
                stage2(c - 1)
        stage2(7)

    def proj_fm(Wt, wres, col0, bank, c):
        for kc in range(8):
            k.op("pe", lambda e, kc=kc: e.matmul(ps[bank][:], Wt[:, kc, col0:col0 + 128],
                                                 uT[:, kc, c * 512:(c + 1) * 512], start=(kc == 0), stop=(kc == 7)),
                 reads=[wres, "uT"], writes=[psr(bank)], inc=(kc == 7))

    def phase_layer0():
        c = Carve(PH_BASE)
        QA = c.take([S], BF16)
        KA0 = c.take([S], BF16)
        KA1 = c.take([S], BF16)
        cs16 = c.take([S], BF16)
        sn16 = c.take([S], BF16)
        xs = [c.take([512], BF16) for _ in range(3)]
        t1 = [c.take([512], F32) for _ in range(2)]
        t2 = [c.take([512], F32) for _ in range(2)]
        PT = [c.take([256], BF16) for _ in range(4)]
        y4 = c.take([NT, 128], BF16)
        VT2 = c.take([2048], BF16)
        Vaug = c.take([3, NT, 65], BF16)
        gs = c.take([NT, 64], BF16)
        acc = c.take([S], F32)
        W0 = [c.take([8, 640], BF16) for _ in range(2)]
        thb = [c.take([4, 64], F32) for _ in range(2)]
        ob = [c.take([4, 64], F32) for _ in range(2)]
        rden = c.take([8], F32)
        m01 = c.take([256], BF16)
        k.op("dve", lambda e: e.tensor_scalar(m01, mask16, 0.0, None, ALU.is_equal), reads=["mask16"], writes=["m01"])

        k.dma(cs16, cs_d, writes=["cs16"], queue="pool", max_dma_last_dim=4096)
        k.dma(sn16, sn_d, writes=["sn16"], queue="pool", max_dma_last_dim=4096)
        k.op("pool", lambda e: e.memset(KA0[64:128, :], 0.0), writes=[f"KA0_{cc}" for cc in range(8)])
        k.op("pool", lambda e: e.memset(KA1[0:64, :], 0.0), writes=[f"KA1_{cc}" for cc in range(8)])
        k.op("pool", lambda e: e.memset(QA, 0.0), writes=[f"QA_{cc}" for cc in range(8)])
        k.op("pool", lambda e: e.memset(Vaug[:, :, :, 64:65], 1.0), writes=["Vaug0", "Vaug1", "Vaug2"])

        def load_w(h):
            k.dma(W0[h % 2], w0_d[h].rearrange("(kc p) c -> p kc c", p=128), writes=[f"W0_{h % 2}"], queue="pool")

        def tok(d, r, a, b):
            return slice(r + a * d, r + (b - 1) * d + 1, d)

        tr_ctr = [0]

        def v_from_xs(sx, g, cch):
            bank = 4 + tr_ctr[0] % 2
            tr_ctr[0] += 1
            pT = ps[bank][:].bitcast(BF16)
            for i in range(4):
                src = xs[sx][:, i * 128:(i + 1) * 128] if g == 0 else xs[sx][:, i:512:4]
                k.op("pe", lambda e, i=i, src=src: e.transpose(pT[:, i * 128:(i + 1) * 128], src, ident16),
                     reads=[f"xs{sx}", "ident16"], writes=[psr(bank)], inc=(i == 3))
            pv = pT[:, 0:512].rearrange("p (a b) -> p a b", a=4)
            if g == 0:
                dst = Vaug[:, 0, 4 * cch:4 * cch + 4, 0:64]
            else:
                dst = Vaug[:, 1, cch:NT:8, 0:64]
            k.op("act", lambda e: e.activation(dst, pv[:, :, 64:128], AF.Copy),
                 reads=[psr(bank)], writes=[f"Vaug{g}"])

        def gate_from_xs(sx, cch):
            bank = 4 + tr_ctr[0] % 2
            tr_ctr[0] += 1
            st_ = tr_ctr[0] % 2
            pT = ps[bank][:].bitcast(BF16)
            for i in range(4):
                k.op("pe", lambda e, i=i: e.transpose(pT[:, i * 128:(i + 1) * 128], xs[sx][:, i * 128:(i + 1) * 128], ident16),
                     reads=[f"xs{sx}", "ident16"], writes=[psr(bank)], inc=(i == 3))
            pv = pT[:, 0:512].rearrange("p (a b) -> p a b", a=4)
            k.op("act", lambda e: e.activation(thb[st_], pv[:, :, 64:128], AF.Tanh, scale=0.5),
                 reads=[psr(bank)], writes=[f"thb{st_}"])
            k.op("dve", lambda e: e.scalar_tensor_tensor(
                out=gs[:, 4 * cch:4 * cch + 4, :], in0=thb[st_], scalar=1.0, in1=pv[:, :, 64:128], op0=ALU.add, op1=ALU.mult),
                reads=[psr(bank), f"thb{st_}"], writes=["gs"])

        def v2_half(j):
            bank = 4 + tr_ctr[0] % 2
            tr_ctr[0] += 1
            pT = ps[bank][:].bitcast(BF16)
            for r in range(16):
                k.op("pe", lambda e, r=r: e.transpose(pT[:, r * 64:(r + 1) * 64], VT2[0:64, r:2048:16], ident16[0:64, 0:64]),
                     reads=["VT2", "ident16"], writes=[psr(bank)], inc=(r == 15))
            k.op("act", lambda e: e.activation(Vaug[:, 2, j:NT:2, 0:64], pT.rearrange("p (a b) -> p a b", a=16), AF.Copy),
                 reads=[psr(bank)], writes=["Vaug2"])

        def attn_group(g, Kt, kres, first):
            d = DIL[g][1]
            nb = S // d // 128
            steps = [(r, j) for r in range(d) for j in range(nb)]
            started = {}

            L_ = S // d

            def emit_S(i):
                r, j = steps[i]
                sl = i % 4
                bank = (0, 1, 4, 5)[i % 4]
                nq = 256 if j + 1 < nb else 128
                if d == 16:
                    k0 = r * L_ + j * 128
                    ksl, qsl = slice(k0, k0 + 128), slice(k0, k0 + nq)
                    rd = [f"{kres}_{cc}" for cc in range(8)] + [f"QA_{cc}" for cc in range(8)]
                else:
                    ksl, qsl = tok(d, r, j * 128, (j + 1) * 128), tok(d, r, j * 128, j * 128 + nq)
                    t_lo = j * 128 * d
                    t_hi = (j * 128 + nq) * d - 1
                    rd = [f"{kres}_{t_lo // 512}"] + [f"QA_{cc}" for cc in range(t_lo // 512, min(t_hi // 512, 7) + 1)]
                k.op("pe", lambda e: e.matmul(ps[bank][:, 0:nq], Kt[:, ksl], QA[:, qsl], start=True, stop=False),
                     reads=rd, writes=[psr(bank)], inc=False)
                k.op("pe", lambda e: e.matmul(ps[bank][:, 0:nq], ident16, mask16[:, 0:nq], start=False, stop=True),
                     reads=["ident16", "mask16"], writes=[psr(bank)])

            def emit_exp(i):
                r, j = steps[i]
                sl = i % 4
                bank = (0, 1, 4, 5)[i % 4]
                nq = 256 if j + 1 < nb else 128
                k.op("act", lambda e: e.activation(PT[sl][:, 0:nq], ps[bank][:, 0:nq], AF.Exp, scale=0.125),
                     reads=[psr(bank)], writes=[f"PT{sl}"])

            def pv_mm(b, V_b, pt_ap, sl, st, sp):
                fill = b // 4
                obank = 2 + fill % 2
                col = (b % 4) * 128
                k.op("pe", lambda e: e.matmul(ps[obank][0:65, col:col + 128], Vaug[:, g, V_b, :], pt_ap,
                                              start=st, stop=sp),
                     reads=[f"Vaug{g}", f"PT{sl}"], writes=[psr(obank)], inc=True)

            def emit_PV(i):
                r, j = steps[i]
                sl = i % 4
                b = r * nb + j
                has_next = j + 1 < nb
                pv_mm(b, b, PT[sl][:, 0:128], sl, st=(j == 0), sp=True)
                if has_next:
                    pv_mm(b + 1, b, PT[sl][:, 128:256], sl, st=True, sp=False)
                if b % 4 == 3:
                    fill = b // 4
                    obank = 2 + fill % 2
                    b0 = fill * 4
                    runs = []
                    bb = b0
                    while bb < b0 + 4:
                        rr, jj = divmod(bb, nb)
                        ln = min(4 - (bb - b0), nb - jj)
                        runs.append((rr, jj, ln, (bb - b0) * 128))
                        bb += ln
                    for (rr, jj, ln, col) in runs:
                        dst = acc[0:65, tok(d, rr, jj * 128, (jj + ln) * 128)]
                        src = ps[obank][0:65, col:col + ln * 128]
                        if first:
                            k.op("dve", lambda e, dst=dst, src=src: e.tensor_copy(dst, src),
                                 reads=[psr(obank)], writes=["acc"])
                        else:
                            k.op("dve", lambda e, dst=dst, src=src: e.tensor_tensor(out=dst, in0=src, in1=dst, op=ALU.add),
                                 reads=[psr(obank), "acc"], writes=["acc"])

            n = len(steps)
            for i0 in range(min(3, n)):
                emit_S(i0)
            for i in range(n):
                emit_exp(i)
                if i + 3 < n:
                    emit_S(i + 3)
                emit_PV(i)

        load_w(0)
        for h in range(NH0):
            Wt = W0[h % 2]
            wres = f"W0_{h % 2}"
            if h + 1 < NH0:
                load_w(h + 1)
            bufs = (xs, t1, t2, cs16, sn16)
            run_pass(Wt, wres, 256, [(QA, 0, 64, "QA", 16)], bufs, extra=lambda sx, cch: v_from_xs(sx, 0, cch))
            run_pass(Wt, wres, 384, [(KA0, 0, 64, "KA0", 16)], bufs, extra=lambda sx, cch: v_from_xs(sx, 1, cch))

            def extra_e(sx, cch):
                xb = (0, 1, 3)[cch % 3]
                k.op("act", lambda e: e.activation(VT2[0:64, (cch % 4) * 512:(cch % 4 + 1) * 512], ps[xb][0:64, :], AF.Copy),
                     reads=[psr(xb)], writes=["VT2"])
                gate_from_xs(sx, cch)
                if cch % 4 == 3:
                    v2_half(cch // 4)
            run_pass(Wt, wres, 512, [], bufs, extra=extra_e, do_rope=False)
            attn_group(2, KA0, "KA0", first=True)
            run_pass(Wt, wres, 0, [(QA, 0, 128, "QA", 1)], bufs)
            run_pass(Wt, wres, 128, [(KA0, 0, 64, "KA0", 1), (KA1, 64, 128, "KA1", 1)], bufs)
            attn_group(0, KA0, "KA0", first=False)
            attn_group(1, KA1, "KA1", first=False)
            hq = h % 2
            if True:
              for t0 in range(0, NT, 4):
                fb = (6, 7, 4, 5)[(t0 // 4) % 4]
                s = (t0 // 4) % 2
                for i in range(4):
                    tt = t0 + i
                    k.op("pe", lambda e, i=i, tt=tt: e.transpose(ps[fb][:, i * 65:(i + 1) * 65],
                                                                 acc[0:65, tt * 128:(tt + 1) * 128], identf[0:65, 0:65]),
                         reads=["acc", "identf"], writes=[psr(fb)], inc=(i == 3))
                trv = ps[fb][:, 0:260].rearrange("p (a b) -> p a b", a=4)
                k.op("dve", lambda e, trv=trv, s=s: e.reciprocal(rden[:, s * 4:s * 4 + 4], trv[:, :, 64]),
                     reads=[psr(fb)], writes=[f"rden{s}"])
                k.op("dve", lambda e, trv=trv, s=s: e.tensor_tensor(
                    out=ob[s], in0=trv[:, :, 0:64], in1=rden[:, s * 4:s * 4 + 4].unsqueeze(2).broadcast_to([128, 4, 64]),
                    op=ALU.mult), reads=[psr(fb), f"rden{s}"], writes=[f"ob{s}"])
                k.op("dve", lambda e, s=s, t0=t0: e.scalar_tensor_tensor(
                    out=y4[:, t0:t0 + 4, hq * 64:(hq + 1) * 64], in0=ob[s], scalar=0.5, in1=gs[:, t0:t0 + 4, :],
                    op0=ALU.mult, op1=ALU.mult), reads=[f"ob{s}", "gs"], writes=["y4"])
            if hq == 1:
                q4 = h // 2
                for half in range(2):
                    k.dma(y_view[:, half * 16:(half + 1) * 16, q4 * 128:(q4 + 1) * 128],
                          y4[:, half * 16:(half + 1) * 16, :], reads=["y4"], writes=["y_scr"])

    def phase_tail(layer, wo_d, res_d, dst_d, next_prenorm):
        c = Carve(PH_BASE + 5 * 8192)
        Wo = c.take([8, 1024], BF16)
        yt = [c.take([1024], BF16) for _ in range(3)]
        yT = [c.take([8, 128], BF16) for _ in range(3)]
        xr = [c.take([1024], F32) for _ in range(6)]
        hn = [c.take([1024], F32) for _ in range(3)]
        xn = [c.take([1024], BF16) for _ in range(3)]
        sqj = c.take([1024], BF16)
        sqj2 = c.take([1024], BF16)
        st2 = c.take([16], F32)
        k.dma(Wo, wo_d.rearrange("(kc p) n -> p kc n", p=128), writes=["Wo"], queue="pool")

        def st_L(tt):
            rows = slice(tt * 128, (tt + 1) * 128)
            k.dma(yt[tt % 3], y_scr[rows, :], reads=["y_scr"], writes=[f"yt{tt % 3}"])
            k.dma(xr[tt % 6], res_d[rows, :], reads=["h1_scr"] if res_d is h1_scr else [], writes=[f"xr{tt % 6}"])

        def st_A(tt):
            s3 = tt % 3
            pb = 4
            pT = ps[pb][:].bitcast(BF16)
            for kc in range(8):
                k.op("pe", lambda e, kc=kc: e.transpose(pT[:, kc * 128:(kc + 1) * 128],
                                                         yt[s3][:, kc * 128:(kc + 1) * 128], ident16),
                     reads=[f"yt{s3}", "ident16"], writes=[psr(pb)], inc=(kc == 7))
            k.op("act", lambda e: e.activation(yT[s3], pT.rearrange("p (a b) -> p a b", a=8), AF.Copy),
                 reads=[psr(pb)], writes=[f"yT{s3}"])

        def st_B(tt):
            s3 = tt % 3
            i = tt % 4
            for half in range(2):
                ob_ = ((0, 1), (2, 3), (6, 7))[tt % 3][half]
                for kc in range(8):
                    k.op("pe", lambda e, kc=kc, half=half, ob_=ob_: e.matmul(
                        ps[ob_][:], yT[s3][:, kc, :], Wo[:, kc, half * 512:(half + 1) * 512],
                        start=(kc == 0), stop=(kc == 7)),
                        reads=[f"yT{s3}", "Wo"], writes=[psr(ob_)], inc=(kc == 7))
            for half in range(2):
                ob_ = ((0, 1), (2, 3), (6, 7))[tt % 3][half]
                k.op("act", lambda e, half=half, ob_=ob_: e.activation(
                    sqj2[:, half * 512:(half + 1) * 512], ps[ob_][:], AF.Square,
                    accum_out=st2[:, 2 * i + half:2 * i + half + 1]),
                    reads=[psr(ob_)], writes=[f"sqj2_{half}", f"p_ss{i}_{half}"])

        def st_C(tt):
            s3 = tt % 3
            i = tt % 4
            rows = slice(tt * 128, (tt + 1) * 128)
            k.op("dve", lambda e: e.tensor_tensor(out=st2[:, 8 + i:9 + i], in0=st2[:, 2 * i:2 * i + 1],
                                                  in1=st2[:, 2 * i + 1:2 * i + 2], op=ALU.add),
                 reads=[f"p_ss{i}_0", f"p_ss{i}_1"], writes=[f"p_sum{i}"])
            k.op("act", lambda e: e.activation(st2[:, 8 + i:9 + i], st2[:, 8 + i:9 + i], AF.Ln, scale=1.0 / D, bias=EPS),
                 reads=[f"p_sum{i}"], writes=[f"p_sum{i}"])
            k.op("act", lambda e: e.activation(st2[:, 12 + i:13 + i], st2[:, 8 + i:9 + i], AF.Exp, scale=-0.5),
                 reads=[f"p_sum{i}"], writes=[f"p_rstd{i}"])
            for half in range(2):
                ob_ = ((0, 1), (2, 3), (6, 7))[tt % 3][half]
                k.op("dve", lambda e, half=half, ob_=ob_: e.scalar_tensor_tensor(
                    out=hn[s3][:, half * 512:(half + 1) * 512], in0=ps[ob_][:], scalar=st2[:, 12 + i:13 + i],
                    in1=gpost[:, layer, half * 512:(half + 1) * 512], op0=ALU.mult, op1=ALU.mult),
                    reads=[psr(ob_), f"p_rstd{i}", "gpost"], writes=[f"hn{s3}"])
            k.op("pool", lambda e: e.tensor_tensor(out=hn[s3], in0=hn[s3], in1=xr[tt % 6], op=ALU.add),
                 reads=[f"hn{s3}", f"xr{tt % 6}"], writes=[f"hn{s3}"])
            k.dma(dst_d[rows, :], hn[s3], reads=[f"hn{s3}"], writes=["h1_scr"] if dst_d is h1_scr else [])

        for i in range(NT + 6):
            if i < NT:
                st_L(i)
            t = i - 2
            if 0 <= t < NT:
                st_A(t)
            t = i - 3
            if 0 <= t < NT:
                st_B(t)
            t = i - 4
            if 0 <= t < NT:
                st_C(t)
            if next_prenorm:
                t = i - 5
                if 0 <= t < NT:
                    prenorm_a(hn[t % 3], f"hn{t % 3}", t, xn[t % 3], f"xn{t % 3}", sqj)
                t = i - 6
                if 0 <= t < NT:
                    prenorm_b(t, layer + 1, xn[t % 3], f"xn{t % 3}", bank=5)

    def phase_layer1():
        c = Carve(PH_BASE)
        QA = c.take([S], BF16)
        KA0 = c.take([S], BF16)
        KA1 = c.take([S], BF16)
        cs16 = c.take([S], BF16)
        sn16 = c.take([S], BF16)
        xs = [c.take([512], BF16) for _ in range(3)]
        t1 = [c.take([512], F32) for _ in range(2)]
        t2 = [c.take([512], F32) for _ in range(2)]
        PT = [c.take([512], BF16) for _ in range(4)]
        y2 = c.take([NT, 256], BF16)
        Vaug = c.take([NT, 129], BF16)
        gsl = c.take([NT, 128], BF16)
        W1 = [c.take([8, 512], BF16) for _ in range(2)]
        thb = [c.take([2, 128], F32) for _ in range(2)]
        g1b = [c.take([2, 128], F32) for _ in range(2)]
        o0 = c.take([4, 128], F32)
        od = c.take([4, 128], F32)
        sq = c.take([4, 128], F32)
        rr = c.take([32], F32)

        if stage != "full":
            k.dma(cs16, cs_d, writes=["cs16"], queue="pool", max_dma_last_dim=4096)
            k.dma(sn16, sn_d, writes=["sn16"], queue="pool", max_dma_last_dim=4096)
            k.op("pool", lambda e: e.memset(KA0[64:128, :], 0.0), writes=[f"KA0_{cc}" for cc in range(8)])
            k.op("pool", lambda e: e.memset(KA1[0:64, :], 0.0), writes=[f"KA1_{cc}" for cc in range(8)])
        k.op("pool", lambda e: e.memset(Vaug[:, :, 128:129], 1.0), writes=["Vaug"])

        def load_w(h):
            k.dma(W1[h % 2], w1_d[h].rearrange("(kc p) c -> p kc c", p=128), writes=[f"W1_{h % 2}"], queue="pool")

        load_w(0)
        for h in range(8):
            Wt = W1[h % 2]
            wres = f"W1_{h % 2}"
            if h + 1 < 8:
                load_w(h + 1)
            bufs = (xs, t1, t2, cs16, sn16)
            run_pass(Wt, wres, 0, [(QA, 0, 128, "QA", 1)], bufs)
            run_pass(Wt, wres, 128, [(KA0, 0, 64, "KA0", 1), (KA1, 64, 128, "KA1", 1)], bufs)
            for t0 in range(0, NT, 2):
                bank = 4 + (t0 // 2) % 2
                s = (t0 // 2) % 2
                for bi in range(2):
                    tt = t0 + bi
                    for kc in range(8):
                        k.op("pe", lambda e, kc=kc, bi=bi, tt=tt: e.matmul(
                            ps[bank][:, bi * 256:(bi + 1) * 256], uT[:, kc, tt * 128:(tt + 1) * 128],
                            Wt[:, kc, 256:512], start=(kc == 0), stop=(kc == 7)),
                            reads=[wres, "uT"], writes=[psr(bank)], inc=(kc == 7 and bi == 1))
                pv = ps[bank][:].rearrange("p (a b) -> p a b", a=2)
                k.op("act", lambda e, pv=pv, t0=t0: e.activation(Vaug[:, t0:t0 + 2, 0:128], pv[:, :, 0:128], AF.Copy),
                     reads=[psr(bank)], writes=["Vaug"])
                k.op("act", lambda e, pv=pv, s=s: e.activation(thb[s], pv[:, :, 128:256], AF.Tanh, scale=0.5),
                     reads=[psr(bank)], writes=[f"thb{s}"])
                k.op("dve", lambda e, pv=pv, s=s: e.scalar_tensor_tensor(
                    out=g1b[s], in0=thb[s], scalar=1.0, in1=pv[:, :, 128:256], op0=ALU.add, op1=ALU.mult),
                    reads=[psr(bank), f"thb{s}"], writes=[f"g1b{s}"])
                k.op("pool", lambda e, s=s, t0=t0: e.tensor_tensor(
                    out=gsl[:, t0:t0 + 2, :], in0=g1b[s], in1=sublnc.unsqueeze(1).broadcast_to([128, 2, 128]),
                    op=ALU.mult), reads=[f"g1b{s}", "sublnc"], writes=["gsl"])

            steps = []
            for qc in range(8):
                for comp in range(2):
                    for kb in range(4 * qc + 4):
                        steps.append((qc, comp, kb))
            cc_ctr = [0]
            started = {}

            def geom(i):
                qc, comp, kb = steps[i]
                m = kb - 4 * qc
                diag = m >= 0
                q0 = kb * 128 if diag else qc * 512
                nq = (qc + 1) * 512 - q0
                return qc, comp, kb, diag, q0, nq

            def emit_S(i):
                qc, comp, kb, diag, q0, nq = geom(i)
                sl = i % 4
                bank = 4 + sl
                Kt, kres = (KA0, "KA0") if comp == 0 else (KA1, "KA1")
                k.op("pe", lambda e: e.matmul(ps[bank][:, 0:nq], Kt[:, kb * 128:(kb + 1) * 128], QA[:, q0:q0 + nq],
                                              start=True, stop=not diag),
                     reads=[f"{kres}_{kb // 4}", f"QA_{qc}"], writes=[psr(bank)], inc=not diag)
                if diag:
                    k.op("pe", lambda e: e.matmul(ps[bank][:, 0:128], ident16, mask16[:, 0:128], start=False, stop=True),
                         reads=["ident16", "mask16"], writes=[psr(bank)])

            def emit_exp(i):
                qc, comp, kb, diag, q0, nq = geom(i)
                sl = i % 4
                bank = 4 + sl
                k.op("act", lambda e: e.activation(PT[sl][:, 0:nq], ps[bank][:, 0:nq], AF.Exp, scale=0.125),
                     reads=[psr(bank)], writes=[f"PT{sl}"])

            def emit_PV(i):
                qc, comp, kb, diag, q0, nq = geom(i)
                sl = i % 4
                cc = qc * 2 + comp
                ob0 = (cc % 2) * 2
                nt = nq // 128
                for ti in range(nt):
                    qt = q0 // 128 + ti
                    lq = qt - qc * 4
                    obank = ob0 + lq // 2
                    col = (lq % 2) * 129
                    key = (cc, lq // 2)
                    st = key not in started
                    started[key] = True
                    last = (kb == 4 * qc + 3) and (ti == nt - 1)
                    k.op("pe", lambda e, ti=ti, obank=obank, col=col, st=st: e.matmul(
                        ps[obank][:, col:col + 129], PT[sl][:, ti * 128:(ti + 1) * 128], Vaug[:, kb, :],
                        start=st, stop=True, skip_group_check=True),
                        reads=["Vaug", f"PT{sl}"], writes=[psr(obank)], inc=(ti == nt - 1))
                if kb == 4 * qc + 3:
                    t0 = qc * 4
                    for hb in range(2):
                        obank = ob0 + hb
                        ov = ps[obank][:, 0:258].rearrange("p (a b) -> p a b", a=2)
                        ro = (cc % 4) * 4 + hb * 2
                        k.op("dve", lambda e, ov=ov, ro=ro: e.reciprocal(rr[:, ro:ro + 2], ov[:, :, 128]),
                             reads=[psr(obank)], writes=[f"rr{ro}"])
                        if comp == 0:
                            k.op("dve", lambda e, ov=ov, ro=ro, hb=hb: e.tensor_tensor(
                                out=o0[:, hb * 2:hb * 2 + 2, :], in0=ov[:, :, 0:128],
                                in1=rr[:, ro:ro + 2].unsqueeze(2).broadcast_to([128, 2, 128]), op=ALU.mult),
                                reads=[psr(obank), f"rr{ro}"], writes=[f"o0_{hb}"])
                        else:
                            k.op("dve", lambda e, ro=ro: e.tensor_scalar(rr[:, ro:ro + 2], rr[:, ro:ro + 2], neglam, None, ALU.mult),
                                 reads=[f"rr{ro}", "neglam"], writes=[f"rr{ro}"])
                            k.op("dve", lambda e, ov=ov, ro=ro, hb=hb: e.tensor_tensor(
                                out=od[:, hb * 2:hb * 2 + 2, :], in0=ov[:, :, 0:128],
                                in1=rr[:, ro:ro + 2].unsqueeze(2).broadcast_to([128, 2, 128]), op=ALU.mult),
                                reads=[psr(obank), f"rr{ro}"], writes=[f"od_{hb}"])
                    if comp == 1:
                        def f1():
                            k.op("pool", lambda e: e.tensor_tensor(out=od, in0=od, in1=o0, op=ALU.add),
                                 reads=["od_0", "od_1", "o0_0", "o0_1"], writes=["od_0", "od_1"])

                        def f2():
                            k.op("dve", lambda e: e.tensor_tensor(out=sq, in0=od, in1=od, op=ALU.mult),
                                 reads=["od_0", "od_1"], writes=["sq"])
                            k.op("dve", lambda e: e.tensor_reduce(out=rr[:, 16:20], in_=sq, axis=AX.X, op=ALU.add),
                                 reads=["sq"], writes=["rr16"])

                        def f3():
                            k.op("act", lambda e: e.activation(rr[:, 20:24], rr[:, 16:20], AF.Ln, scale=1.0 / 128, bias=EPS),
                                 reads=["rr16"], writes=["rr20"])
                            k.op("act", lambda e: e.activation(rr[:, 24:28], rr[:, 20:24], AF.Exp, scale=-0.5),
                                 reads=["rr20"], writes=["rr24"])

                        def f4():
                            k.op("dve", lambda e: e.tensor_tensor(
                                out=sq, in0=od, in1=rr[:, 24:28].unsqueeze(2).broadcast_to([128, 4, 128]), op=ALU.mult),
                                reads=["od_0", "od_1", "rr24", "sq"], writes=["sq"])

                        def f5(t0=t0):
                            hp = h % 2
                            k.op("pool", lambda e: e.tensor_tensor(
                                out=y2[:, t0:t0 + 4, hp * 128:(hp + 1) * 128], in0=sq, in1=gsl[:, t0:t0 + 4, :], op=ALU.mult),
                                reads=["sq", "gsl"], writes=["y2"])
                        for dly, fn_ in ((4, f1), (8, f2), (12, f3), (15, f4), (18, f5)):
                            deferred.append((i + dly, fn_))

            deferred = []
            n = len(steps)
            emit_S(0)
            emit_S(1)
            emit_S(2)
            for i in range(n):
                emit_exp(i)
                if i + 3 < n:
                    emit_S(i + 3)
                emit_PV(i)
                deferred.sort(key=lambda x: x[0])
                while deferred and deferred[0][0] <= i:
                    deferred.pop(0)[1]()
            while deferred:
                deferred.pop(0)[1]()
            if h % 2 == 1:
                q2 = h // 2
                for half in range(2):
                    k.dma(y_view[:, half * 16:(half + 1) * 16, q2 * 256:(q2 + 1) * 256],
                          y2[:, half * 16:(half + 1) * 16, :], reads=["y2"], writes=["y_scr"])

    if stage.startswith("d_"):
        if stage >= "d_1":
            phase_prenorm_dram(x_d, 0)
            k.barrier()
        if stage >= "d_2":
            phase_layer0()
            k.barrier()
        dbg = Carve(PH_BASE).take([1024], F32)
        k.dma(dbg, x_d[0:128, :], writes=["dbg"])
        k.dma(out_d[0:128, :], dbg, reads=["dbg"])
    elif stage == "l0":
        phase_prenorm_dram(x_d, 0)
        k.barrier()
        phase_layer0()
        k.barrier()
        phase_tail(0, w0o_d, x_d, out_d, next_prenorm=False)
    elif stage == "l1":
        phase_prenorm_dram(x_d, 1)
        k.barrier()
        phase_layer1()
        k.barrier()
        phase_tail(1, w1o_d, x_d, out_d, next_prenorm=False)
    else:
        phase_prenorm_dram(x_d, 0)
        k.barrier()
        phase_layer0()
        k.barrier()
        phase_tail(0, w0o_d, x_d, h1_scr, next_prenorm=True)
        k.barrier()
        phase_layer1()
        k.barrier()
        phase_tail(1, w1o_d, h1_scr, out_d, next_prenorm=False)
    k.finish()
    return nc, k


def _consts():
    f32 = np.float32
    freqs = (np.float32(10000.0) ** (-(np.arange(0, 64, 2, dtype=f32)) / np.float32(64))).astype(f32)
    pos = np.arange(S, dtype=f32)
    ang = (pos[:, None] * freqs[None, :]).astype(f32)
    cos = np.cos(ang).astype(f32).T
    sin = np.sin(ang).astype(f32).T
    cs = np.ascontiguousarray(np.tile(cos, (4, 1)))
    sn = np.ascontiguousarray(np.tile(sin, (4, 1)))
    perm = np.zeros((128, 128), f32)
    for m in range(128):
        if (m % 64) < 32:
            perm[m + 32, m] = -1.0
        else:
            perm[m - 32, m] = 1.0
    ident = np.eye(128, dtype=f32)
    kk = np.arange(128)[:, None]
    qq = np.arange(128)[None, :]
    tri_cur = np.where(kk <= qq, 0.0, NEG).astype(f32)
    tri_prev = np.where(kk >= qq, 0.0, NEG).astype(f32)
    mask0 = np.ascontiguousarray(np.concatenate([tri_cur, tri_prev], axis=1))
    return cs, sn, perm, ident, mask0


def _layout_weights(dil_w_in, dil_w_out, diff_w_in, diff_w_out):
    w = dil_w_in[0]
    w0 = np.empty((16, 1024, 640), np.float32)
    for h in range(16):
        def qc(g):
            return slice((g * 16 + h) * 64, (g * 16 + h) * 64 + 64)

        def kc_(g):
            return slice(3072 + (g * 16 + h) * 64, 3072 + (g * 16 + h) * 64 + 64)

        def vc(g):
            return slice(6144 + (g * 16 + h) * 64, 6144 + (g * 16 + h) * 64 + 64)
        gc = slice(9216 + h * 64, 9216 + h * 64 + 64)
        cols = [qc(0), qc(1), kc_(0), kc_(1), qc(2), vc(0), kc_(2), vc(1), vc(2), gc]
        for i, sl in enumerate(cols):
            w0[h, :, i * 64:(i + 1) * 64] = w[:, sl]
    w1i = diff_w_in[0]
    w1 = np.empty((8, 1024, 512), np.float32)
    for h in range(8):
        w1[h, :, 0:128] = w1i[:, (2 * h) * 64:(2 * h + 2) * 64]
        w1[h, :, 128:256] = w1i[:, 1024 + (2 * h) * 64:1024 + (2 * h + 2) * 64]
        w1[h, :, 256:384] = w1i[:, 2048 + h * 128:2048 + (h + 1) * 128]
        w1[h, :, 384:512] = w1i[:, 3072 + h * 128:3072 + (h + 1) * 128]
    return w0, np.ascontiguousarray(dil_w_out[0]), w1, np.ascontiguousarray(diff_w_out[0])


_CACHE = {}


def _get_program(stage):
    if stage not in _CACHE:
        _CACHE[stage] = build_program(stage)[0]
    return _CACHE[stage]


def _common_maps(norm_pre, norm_post, dil_w_in, dil_w_out, diff_w_in, diff_w_out,
                 diff_lambda_q1, diff_lambda_k1, diff_lambda_q2, diff_lambda_k2, diff_subln):
    cs, sn, perm, ident, mask0 = _consts()
    w0, w0o, w1, w1o = _layout_weights(np.asarray(dil_w_in, np.float32), np.asarray(dil_w_out, np.float32),
                                       np.asarray(diff_w_in, np.float32), np.asarray(diff_w_out, np.float32))
    npre = np.asarray(norm_pre, np.float32)
    gpre = np.ascontiguousarray(npre.reshape(2, 8, 128).transpose(2, 0, 1).reshape(128, 16))
    lamv = np.ascontiguousarray(np.concatenate([np.asarray(a, np.float32).reshape(1, 64) for a in
                                                (diff_lambda_q1, diff_lambda_k1, diff_lambda_q2, diff_lambda_k2)], 0))
    return {"w0": w0, "w0o": w0o, "w1": w1, "w1o": w1o, "gpre": gpre,
            "gpost": np.ascontiguousarray(np.asarray(norm_post, np.float32)),
            "lamv": lamv, "subln": np.ascontiguousarray(np.asarray(diff_subln, np.float32).reshape(1, 128)),
            "cs": cs, "sn": sn, "perm": perm, "ident": ident, "mask0": mask0}


def run_stage(stage, xs, common):
    nc = _get_program(stage)
    in_maps = [dict(common, x=np.ascontiguousarray(xs[b])) for b in range(len(xs))]
    res = run_bass_kernel_spmd(nc, in_maps, core_ids=list(range(len(xs))))
    return np.stack([r["out"] for r in res.results], 0)


def kernel(x, norm_pre, norm_post, dil_w_in, dil_w_out, diff_w_in, diff_w_out,
           diff_lambda_q1, diff_lambda_k1, diff_lambda_q2, diff_lambda_k2, diff_subln):
    x = np.asarray(x, np.float32)
    common = _common_maps(norm_pre, norm_post, dil_w_in, dil_w_out, diff_w_in, diff_w_out,
                          diff_lambda_q1, diff_lambda_k1, diff_lambda_q2, diff_lambda_k2, diff_subln)
    out = run_stage("full", [x[b] for b in range(8)], common)
    return out.astype(np.float32)
```

```python
import contextlib
import math
import numpy as np
import concourse.bass as bass
import concourse.mybir as mybir
from concourse.bass_utils import run_bass_kernel_spmd

F32 = mybir.dt.float32
BF16 = mybir.dt.bfloat16
AF = mybir.ActivationFunctionType
ALU = mybir.AluOpType
AX = mybir.AxisListType

SEM_ROT = 30000
N_DMA_SEMS = 24

S = 4096
D = 1024
NT = 32
NEG = -30000.0
NH0 = 16
DBGV = 0
ROPE_STEPS = 5
L0_STEPS = 10
EPS = 1e-6
LAMBDA_INIT = 0.8 - 0.6 * math.exp(-0.3 * 1)
DIL = ((128, 1), (512, 4), (2048, 16))


class _Rec:
    def __init__(self):
        self.call = None

    def __getattr__(self, name):
        def f(*a, **kw):
            self.call = (name, a, kw)
            return self
        return f


def _capture(fn):
    r = _Rec()
    fn(r)
    name, a, kw = r.call
    return lambda e: getattr(e, name)(*a, **kw)


class KB:
    ENGS = ("pe", "act", "dve", "pool", "sp")

    def __init__(self, nc):
        self.nc = nc
        self.stack = contextlib.ExitStack()
        self.prog = {e: [] for e in self.ENGS}
        self.sem = {}
        self.cnt = {}
        self.nsem = 0
        self.waited = {e: {} for e in self.ENGS}
        self.res = {}
        self.pending = {e: [] for e in self.ENGS}
        self.last_tok = {e: None for e in self.ENGS}
        for e in ("pe", "act", "dve", "pool"):
            self._new_eng_sem(e)
        self.dma_sems = []
        self.dma_pools = {"sp": [], "pool": []}
        for i in range(N_DMA_SEMS):
            h = self.stack.enter_context(nc.semaphore(f"dq{i}"))
            self.dma_sems.append([h, 0, f"dq{i}"])
            self.dma_pools["sp" if i < 16 else "pool"].append(self.dma_sems[-1])
        self.dma_rr = {"sp": 0, "pool": 0}
        self.n_instr = {e: 0 for e in self.ENGS}

    def sb(self, name, shape, dt):
        return self.stack.enter_context(self.nc.sbuf_tensor(name, list(shape), dt))

    def ps(self, name, shape, dt):
        return self.stack.enter_context(self.nc.psum_tensor(name, list(shape), dt))

    def _new_eng_sem(self, e):
        name = f"s_{e}_{self.nsem}"
        self.nsem += 1
        h = self.stack.enter_context(self.nc.semaphore(name))
        self.sem[e] = (h, name)
        self.cnt[e] = 0

    def _wait(self, eng, tok):
        if tok is None:
            return
        h, name, val, src = tok
        if src == eng and eng == "pe":
            return
        w = self.waited[eng]
        if w.get(name, 0) >= val:
            return
        w[name] = val
        self.prog[eng].append(lambda e, h=h, val=val: e.wait_ge(h, val))

    def _deps(self, reads, writes):
        deps = []
        for r in reads:
            st = self.res.get(r)
            if st and st[0] is not None:
                deps.append(st[0])
        for w in writes:
            st = self.res.get(w)
            if st:
                if st[0] is not None:
                    deps.append(st[0])
                deps.extend(st[1])
        return deps

    @staticmethod
    def _max_per_sem(deps):
        best = {}
        for t in deps:
            if t is None:
                continue
            if t[1] not in best or best[t[1]][2] < t[2]:
                best[t[1]] = t
        return list(best.values())

    def _register(self, tok, reads, writes):
        for r in reads:
            st = self.res.setdefault(r, [None, []])
            st[1].append(tok)
            if len(st[1]) > 48:
                best = {}
                for t in st[1]:
                    if t[1] not in best or best[t[1]][2] < t[2]:
                        best[t[1]] = t
                st[1] = list(best.values())
        for w in writes:
            self.res[w] = [tok, []]

    def op(self, eng, fn, reads=(), writes=(), inc=True):
        fn = _capture(fn)
        writes = tuple(writes) + tuple(r for r in reads if r.startswith("ps"))
        reads = tuple(r for r in reads if not r.startswith("ps"))
        for tok in self._max_per_sem(self._deps(reads, writes)):
            self._wait(eng, tok)
        self.n_instr[eng] += 1
        if not inc:
            self.pending[eng].append((reads, writes))
            self.prog[eng].append(lambda e, fn=fn: fn(e))
            return None
        if self.cnt[eng] >= SEM_ROT:
            self._new_eng_sem(eng)
        h, name = self.sem[eng]
        self.cnt[eng] += 1
        tok = (h, name, self.cnt[eng], eng)
        self.prog[eng].append(lambda e, fn=fn, h=h: fn(e).then_inc(h, 1))
        for (r, w) in self.pending[eng]:
            self._register(tok, r, w)
        self.pending[eng] = []
        self._register(tok, reads, writes)
        self.last_tok[eng] = tok
        return tok

    def dma(self, out, in_, reads=(), writes=(), queue="sp", **kw):
        reads = tuple(reads)
        writes = tuple(writes)
        pool = self.dma_pools[queue]
        slot = pool[self.dma_rr[queue]]
        self.dma_rr[queue] = (self.dma_rr[queue] + 1) % len(pool)
        h, cur, name = slot
        if cur > 0:
            self._wait(queue, (h, name, cur, "dma"))
        for tok in self._max_per_sem(self._deps(reads, writes)):
            self._wait(queue, tok)
        slot[1] = cur + 16
        tok = (h, name, cur + 16, "dma")
        self.prog[queue].append(
            lambda e, out=out, in_=in_, h=h, kw=kw: e.dma_start(out=out, in_=in_, **kw).then_inc(h, 16))
        self._register(tok, reads, writes)
        self.n_instr[queue] += 1
        return tok

    def barrier(self):
        toks = [t for t in self.last_tok.values() if t is not None]
        dtoks = [(s[0], s[2], s[1], "dma") for s in self.dma_sems if s[1] > 0]
        for e in self.ENGS:
            for t in toks + dtoks:
                self._wait(e, t)

    def finish(self):
        toks = [t for t in self.last_tok.values() if t is not None]
        dtoks = [(s[0], s[2], s[1], "dma") for s in self.dma_sems if s[1] > 0]
        for t in toks + dtoks:
            self._wait("sp", t)
        with self.nc.Block() as block:
            @block.tensor
            def _(e):
                for f in self.prog["pe"]:
                    f(e)

            @block.scalar
            def _(e):
                for f in self.prog["act"]:
                    f(e)

            @block.vector
            def _(e):
                for f in self.prog["dve"]:
                    f(e)

            @block.gpsimd
            def _(e):
                for f in self.prog["pool"]:
                    f(e)

            @block.sync
            def _(e):
                for f in self.prog["sp"]:
                    f(e)
        self.stack.close()


def _prod(xs):
    p = 1
    for v in xs:
        p *= v
    return p


def build_program(stage):
    nc = bass.Bass("TRN2", target_bir_lowering=False)

    def din(name, shape):
        return nc.dram_tensor(name, list(shape), F32, kind="ExternalInput").ap()

    x_d = din("x", [S, D])
    w0_d = din("w0", [16, 1024, 640])
    w0o_d = din("w0o", [1024, 1024])
    w1_d = din("w1", [8, 1024, 512])
    w1o_d = din("w1o", [1024, 1024])
    gpre_d = din("gpre", [128, 16])
    gpost_d = din("gpost", [2, 1024])
    lamv_d = din("lamv", [4, 64])
    subln_d = din("subln", [1, 128])
    cs_d = din("cs", [128, S])
    sn_d = din("sn", [128, S])
    perm_d = din("perm", [128, 128])
    ident_d = din("ident", [128, 128])
    mask_d = din("mask0", [128, 256])
    out_d = nc.dram_tensor("out", [S, D], F32, kind="ExternalOutput").ap()
    y_scr = nc.dram_tensor("y_scr", [S, D], BF16, kind="Internal").ap()
    h1_scr = nc.dram_tensor("h1_scr", [S, D], F32, kind="Internal").ap()

    k = KB(nc)
    TOTAL = 206 * 1024
    M = k.sb("M", [128, TOTAL // 2], BF16)

    class Carve:
        def __init__(self, base):
            self.off = base

        def take(self, fs, dt):
            esz = 2 if dt == BF16 else 4
            nb = _prod(fs) * esz
            assert self.off % 4 == 0
            v = M[:, self.off // 2:(self.off + nb) // 2]
            if dt == F32:
                v = v.bitcast(F32)
            if len(fs) == 2:
                v = v.rearrange("p (a b) -> p a b", a=fs[0])
            elif len(fs) == 3:
                v = v.rearrange("p (a b c) -> p a b c", a=fs[0], b=fs[1])
            self.off += (nb + 63) // 64 * 64
            assert self.off <= TOTAL, (self.off, TOTAL)
            return v

    cm = Carve(0)
    uT = cm.take([8, S], BF16)
    ident16 = cm.take([128], BF16)
    perm16 = cm.take([128], BF16)
    mask16 = cm.take([256], BF16)
    identf = cm.take([128], F32)
    gpre = cm.take([16], F32)
    gpost = cm.take([2, 1024], F32)
    lamb = cm.take([4, 64], F32)
    lprod = cm.take([2, 64], F32)
    lsc = cm.take([8], F32)
    sublnc = cm.take([128], F32)
    stat = cm.take([16], F32)
    PH_BASE = cm.off

    ps = [k.ps(f"ps{i}", [128, 512], F32) for i in range(8)]

    def psr(i):
        return f"ps{i}"

    k.dma(ident16, ident_d, writes=["ident16"], queue="pool")
    k.dma(perm16, perm_d, writes=["perm16"], queue="pool")
    k.dma(mask16, mask_d, writes=["mask16"], queue="pool")
    k.dma(identf, ident_d, writes=["identf"])
    k.dma(gpre, gpre_d, writes=["gpre"])
    k.dma(gpost, gpost_d.unsqueeze(0).broadcast_to([128, 2, 1024]), writes=["gpost"])
    for i in range(8):
        k.op("dve", lambda e, i=i: e.memset(ps[i][:], 0.0), writes=[psr(i)])

    lam_needed = stage in ("full", "l1")
    if lam_needed:
        k.dma(lamb, lamv_d.unsqueeze(0).broadcast_to([128, 4, 64]), writes=["lamb"])
        k.dma(sublnc, subln_d.broadcast_to([128, 128]), writes=["sublnc_raw"])
        lv = lamb.rearrange("p (a b) c -> p a b c", a=2)
        k.op("dve", lambda e: e.tensor_tensor(out=lprod, in0=lv[:, :, 0, :], in1=lv[:, :, 1, :], op=ALU.mult),
             reads=["lamb"], writes=["lprod"])
        k.op("dve", lambda e: e.tensor_reduce(out=lsc[:, 0:2], in_=lprod, axis=AX.X, op=ALU.add),
             reads=["lprod"], writes=["lsc01"])
        k.op("act", lambda e: e.activation(lsc[:, 2:4], lsc[:, 0:2], AF.Exp), reads=["lsc01"], writes=["lsc23"])
        k.op("dve", lambda e: e.tensor_tensor(out=lsc[:, 4:5], in0=lsc[:, 3:4], in1=lsc[:, 2:3], op=ALU.subtract),
             reads=["lsc23"], writes=["lsc4"])
        k.op("dve", lambda e: e.tensor_scalar(lsc[:, 5:6], lsc[:, 4:5], -LAMBDA_INIT, None, ALU.add),
             reads=["lsc4"], writes=["neglam"])
        k.op("dve", lambda e: e.tensor_scalar(sublnc, sublnc, 0.5 * (1.0 - LAMBDA_INIT), None, ALU.mult),
             reads=["sublnc_raw"], writes=["sublnc"])
    neglam = lsc[:, 5:6]

    y_view = y_scr.rearrange("(t p) c -> p t c", p=128)

    def prenorm_a(src, src_res, tt, xn, xn_res, sqj):
        i = tt % 4
        k.op("act", lambda e: e.activation(sqj, src, AF.Square, accum_out=stat[:, i:i + 1]),
             reads=[src_res], writes=["sqj", f"ss{i}"])
        k.op("act", lambda e: e.activation(stat[:, 4 + i:5 + i], stat[:, i:i + 1], AF.Ln, scale=1.0 / D, bias=EPS),
             reads=[f"ss{i}"], writes=[f"lnv{i}"])
        k.op("act", lambda e: e.activation(stat[:, 8 + i:9 + i], stat[:, 4 + i:5 + i], AF.Exp, scale=-0.5),
             reads=[f"lnv{i}"], writes=[f"rstd{i}"])
        k.op("dve", lambda e: e.tensor_scalar(xn, src, stat[:, 8 + i:9 + i], None, ALU.mult),
             reads=[src_res, f"rstd{i}"], writes=[xn_res])

    def prenorm_b(tt, layer, xn, xn_res, bank=None):
        pb = 6 + (tt % 2) if bank is None else bank
        pT = ps[pb][:].bitcast(BF16)
        for kc in range(8):
            k.op("pe", lambda e, kc=kc: e.transpose(pT[:, kc * 128:(kc + 1) * 128], xn[:, kc * 128:(kc + 1) * 128],
                                                     ident16),
                 reads=[xn_res, "ident16"], writes=[psr(pb)], inc=(kc == 7))
        gb = gpre[:, layer * 8:(layer + 1) * 8].unsqueeze(2).broadcast_to([128, 8, 128])
        k.op("dve", lambda e: e.tensor_tensor(out=uT[:, :, tt * 128:(tt + 1) * 128],
                                              in0=pT.rearrange("p (a b) -> p a b", a=8), in1=gb, op=ALU.mult),
             reads=[psr(pb), "gpre"], writes=["uT"])

    def phase_prenorm_dram(src_d, layer):
        c = Carve(PH_BASE)
        xt = [c.take([1024], F32) for _ in range(4)]
        xn = [c.take([1024], BF16) for _ in range(3)]
        sqj = c.take([1024], BF16)
        for i in range(NT + 3):
            t = i
            if t < NT:
                k.dma(xt[t % 4], src_d[t * 128:(t + 1) * 128, :], writes=[f"xt{t % 4}"])
            t = i - 2
            if 0 <= t < NT:
                prenorm_a(xt[t % 4], f"xt{t % 4}", t, xn[t % 3], f"xn{t % 3}", sqj)
            t = i - 3
            if 0 <= t < NT:
                prenorm_b(t, layer, xn[t % 3], f"xn{t % 3}")

    rope_ctr = [0]

    def rope_a(bank, xs, eng="act"):
        s = rope_ctr[0] % 3
        rope_ctr[0] += 1
        if eng == "act":
            k.op("act", lambda e: e.activation(xs[s], ps[bank][:], AF.Copy), reads=[psr(bank)], writes=[f"xs{s}"])
        else:
            k.op("dve", lambda e: e.tensor_copy(xs[s], ps[bank][:]), reads=[psr(bank)], writes=[f"xs{s}"])
        return s

    def rope_b(s, bank, c, outs, xs, t1, t2, cs16, sn16):
        rb = 2
        ch = slice(c * 512, (c + 1) * 512)
        xs_s = s
        s = c % 2
        k.op("pe", lambda e: e.matmul(ps[rb][:], perm16, xs[xs_s], start=True, stop=True),
             reads=[f"xs{xs_s}", "perm16"], writes=[psr(rb)])
        k.op("dve", lambda e: e.tensor_tensor(out=t1[s], in0=ps[bank][:], in1=cs16[:, ch], op=ALU.mult),
             reads=[psr(bank), "cs16"], writes=[f"t1{s}"])
        k.op("dve", lambda e: e.tensor_tensor(out=t2[s], in0=ps[rb][:], in1=sn16[:, ch], op=ALU.mult),
             reads=[psr(rb), "sn16"], writes=[f"t2{s}"])
        for oi, (dest, lo, hi, res, dd) in enumerate(outs):
            eng = "pool"
            if dd == 1:
                k.op(eng, lambda e, dest=dest, lo=lo, hi=hi: e.tensor_tensor(
                    out=dest[lo:hi, ch], in0=t1[s][lo:hi, :], in1=t2[s][lo:hi, :], op=ALU.add),
                    reads=[f"t1{s}", f"t2{s}"], writes=[f"{res}_{c}"])
            else:
                n = 512 // dd
                dv = dest[lo:hi, :].rearrange("p (r i) -> p r i", r=dd)[:, :, c * n:(c + 1) * n]
                a0 = t1[s][lo:hi, :].rearrange("p (i r) -> p r i", r=dd)
                a1 = t2[s][lo:hi, :].rearrange("p (i r) -> p r i", r=dd)
                k.op(eng, lambda e, dv=dv, a0=a0, a1=a1: e.tensor_tensor(out=dv, in0=a0, in1=a1, op=ALU.add),
                     reads=[f"t1{s}", f"t2{s}"], writes=[f"{res}_{cc}" for cc in range(8)])

    def run_pass(Wt, wres, col0, outs, bufs, extra=None, do_rope=True, copy_eng="act"):
        xs, t1, t2, cs16, sn16 = bufs
        slots = {}

        XB = (0, 1, 3)

        def stage2(c):
            if do_rope:
                rope_b(slots[c], XB[c % 3], c, outs, xs, t1, t2, cs16, sn16)
            if extra is not None:
                extra(slots[c], c)

        for c in range(8):
            proj_fm(Wt, wres, col0, XB[c % 3], c)
            slots[c] = rope_a(XB[c % 3], xs, copy_eng)
            if c > 0:
                stage2(c - 1)
        stage2(7)

    def proj_fm(Wt, wres, col0, bank, c):
        for kc in range(8):
            k.op("pe", lambda e, kc=kc: e.matmul(ps[bank][:], Wt[:, kc, col0:col0 + 128],
                                                 uT[:, kc, c * 512:(c + 1) * 512], start=(kc == 0), stop=(kc == 7)),
                 reads=[wres, "uT"], writes=[psr(bank)], inc=(kc == 7))

    def phase_layer0():
        c = Carve(PH_BASE)
        QA = c.take([S], BF16)
        KA0 = c.take([S], BF16)
        KA1 = c.take([S], BF16)
        cs16 = c.take([S], BF16)
        sn16 = c.take([S], BF16)
        xs = [c.take([512], BF16) for _ in range(3)]
        t1 = [c.take([512], F32) for _ in range(2)]
        t2 = [c.take([512], F32) for _ in range(2)]
        PT = [c.take([256], BF16) for _ in range(4)]
        y4 = c.take([NT, 128], BF16)
        VT2 = c.take([2048], BF16)
        Vaug = c.take([3, NT, 65], BF16)
        gs = c.take([NT, 64], BF16)
        acc = c.take([S], F32)
        W0 = [c.take([8, 640], BF16) for _ in range(2)]
        thb = [c.take([4, 64], F32) for _ in range(2)]
        ob = [c.take([4, 64], F32) for _ in range(2)]
        rden = c.take([8], F32)
        m01 = c.take([256], BF16)
        k.op("dve", lambda e: e.tensor_scalar(m01, mask16, 0.0, None, ALU.is_equal), reads=["mask16"], writes=["m01"])

        k.dma(cs16, cs_d, writes=["cs16"], queue="pool", max_dma_last_dim=4096)
        k.dma(sn16, sn_d, writes=["sn16"], queue="pool", max_dma_last_dim=4096)
        k.op("pool", lambda e: e.memset(KA0[64:128, :], 0.0), writes=[f"KA0_{cc}" for cc in range(8)])
        k.op("pool", lambda e: e.memset(KA1[0:64, :], 0.0), writes=[f"KA1_{cc}" for cc in range(8)])
        k.op("pool", lambda e: e.memset(QA, 0.0), writes=[f"QA_{cc}" for cc in range(8)])
        k.op("pool", lambda e: e.memset(Vaug[:, :, :, 64:65], 1.0), writes=["Vaug0", "Vaug1", "Vaug2"])

        def load_w(h):
            k.dma(W0[h % 2], w0_d[h].rearrange("(kc p) c -> p kc c", p=128), writes=[f"W0_{h % 2}"], queue="pool")

        def tok(d, r, a, b):
            return slice(r + a * d, r + (b - 1) * d + 1, d)

        tr_ctr = [0]

        def v_from_xs(sx, g, cch):
            bank = 4 + tr_ctr[0] % 2
            tr_ctr[0] += 1
            pT = ps[bank][:].bitcast(BF16)
            for i in range(4):
                src = xs[sx][:, i * 128:(i + 1) * 128] if g == 0 else xs[sx][:, i:512:4]
                k.op("pe", lambda e, i=i, src=src: e.transpose(pT[:, i * 128:(i + 1) * 128], src, ident16),
                     reads=[f"xs{sx}", "ident16"], writes=[psr(bank)], inc=(i == 3))
            pv = pT[:, 0:512].rearrange("p (a b) -> p a b", a=4)
            if g == 0:
                dst = Vaug[:, 0, 4 * cch:4 * cch + 4, 0:64]
            else:
                dst = Vaug[:, 1, cch:NT:8, 0:64]
            k.op("act", lambda e: e.activation(dst, pv[:, :, 64:128], AF.Copy),
                 reads=[psr(bank)], writes=[f"Vaug{g}"])

        def gate_from_xs(sx, cch):
            bank = 4 + tr_ctr[0] % 2
            tr_ctr[0] += 1
            st_ = tr_ctr[0] % 2
            pT = ps[bank][:].bitcast(BF16)
            for i in range(4):
                k.op("pe", lambda e, i=i: e.transpose(pT[:, i * 128:(i + 1) * 128], xs[sx][:, i * 128:(i + 1) * 128], ident16),
                     reads=[f"xs{sx}", "ident16"], writes=[psr(bank)], inc=(i == 3))
            pv = pT[:, 0:512].rearrange("p (a b) -> p a b", a=4)
            k.op("act", lambda e: e.activation(thb[st_], pv[:, :, 64:128], AF.Tanh, scale=0.5),
                 reads=[psr(bank)], writes=[f"thb{st_}"])
            k.op("dve", lambda e: e.scalar_tensor_tensor(
                out=gs[:, 4 * cch:4 * cch + 4, :], in0=thb[st_], scalar=1.0, in1=pv[:, :, 64:128], op0=ALU.add, op1=ALU.mult),
                reads=[psr(bank), f"thb{st_}"], writes=["gs"])

        def v2_half(j):
            bank = 4 + tr_ctr[0] % 2
            tr_ctr[0] += 1
            pT = ps[bank][:].bitcast(BF16)
            for r in range(16):
                k.op("pe", lambda e, r=r: e.transpose(pT[:, r * 64:(r + 1) * 64], VT2[0:64, r:2048:16], ident16[0:64, 0:64]),
                     reads=["VT2", "ident16"], writes=[psr(bank)], inc=(r == 15))
            k.op("act", lambda e: e.activation(Vaug[:, 2, j:NT:2, 0:64], pT.rearrange("p (a b) -> p a b", a=16), AF.Copy),
                 reads=[psr(bank)], writes=["Vaug2"])

        def attn_group(g, Kt, kres, first):
            d = DIL[g][1]
            nb = S // d // 128
            steps = [(r, j) for r in range(d) for j in range(nb)]
            started = {}

            L_ = S // d

            def emit_S(i):
                r, j = steps[i]
                sl = i % 4
                bank = (0, 1, 4, 5)[i % 4]
                nq = 256 if j + 1 < nb else 128
                if d == 16:
                    k0 = r * L_ + j * 128
                    ksl, qsl = slice(k0, k0 + 128), slice(k0, k0 + nq)
                    rd = [f"{kres}_{cc}" for cc in range(8)] + [f"QA_{cc}" for cc in range(8)]
                else:
                    ksl, qsl = tok(d, r, j * 128, (j + 1) * 128), tok(d, r, j * 128, j * 128 + nq)
                    t_lo = j * 128 * d
                    t_hi = (j * 128 + nq) * d - 1
                    rd = [f"{kres}_{t_lo // 512}"] + [f"QA_{cc}" for cc in range(t_lo // 512, min(t_hi // 512, 7) + 1)]
                k.op("pe", lambda e: e.matmul(ps[bank][:, 0:nq], Kt[:, ksl], QA[:, qsl], start=True, stop=False),
                     reads=rd, writes=[psr(bank)], inc=False)
                k.op("pe", lambda e: e.matmul(ps[bank][:, 0:nq], ident16, mask16[:, 0:nq], start=False, stop=True),
                     reads=["ident16", "mask16"], writes=[psr(bank)])

            def emit_exp(i):
                r, j = steps[i]
                sl = i % 4
                bank = (0, 1, 4, 5)[i % 4]
                nq = 256 if j + 1 < nb else 128
                k.op("act", lambda e: e.activation(PT[sl][:, 0:nq], ps[bank][:, 0:nq], AF.Exp, scale=0.125),
                     reads=[psr(bank)], writes=[f"PT{sl}"])

            def pv_mm(b, V_b, pt_ap, sl, st, sp):
                fill = b // 4
                obank = 2 + fill % 2
                col = (b % 4) * 128
                k.op("pe", lambda e: e.matmul(ps[obank][0:65, col:col + 128], Vaug[:, g, V_b, :], pt_ap,
                                              start=st, stop=sp),
                     reads=[f"Vaug{g}", f"PT{sl}"], writes=[psr(obank)], inc=True)

            def emit_PV(i):
                r, j = steps[i]
                sl = i % 4
                b = r * nb + j
                has_next = j + 1 < nb
                pv_mm(b, b, PT[sl][:, 0:128], sl, st=(j == 0), sp=True)
                if has_next:
                    pv_mm(b + 1, b, PT[sl][:, 128:256], sl, st=True, sp=False)
                if b % 4 == 3:
                    fill = b // 4
                    obank = 2 + fill % 2
                    b0 = fill * 4
                    runs = []
                    bb = b0
                    while bb < b0 + 4:
                        rr, jj = divmod(bb, nb)
                        ln = min(4 - (bb - b0), nb - jj)
                        runs.append((rr, jj, ln, (bb - b0) * 128))
                        bb += ln
                    for (rr, jj, ln, col) in runs:
                        dst = acc[0:65, tok(d, rr, jj * 128, (jj + ln) * 128)]
                        src = ps[obank][0:65, col:col + ln * 128]
                        if first:
                            k.op("dve", lambda e, dst=dst, src=src: e.tensor_copy(dst, src),
                                 reads=[psr(obank)], writes=["acc"])
                        else:
                            k.op("dve", lambda e, dst=dst, src=src: e.tensor_tensor(out=dst, in0=src, in1=dst, op=ALU.add),
                                 reads=[psr(obank), "acc"], writes=["acc"])

            n = len(steps)
            for i0 in range(min(3, n)):
                emit_S(i0)
            for i in range(n):
                emit_exp(i)
                if i + 3 < n:
                    emit_S(i + 3)
                emit_PV(i)

        load_w(0)
        for h in range(NH0):
            Wt = W0[h % 2]
            wres = f"W0_{h % 2}"
            if h + 1 < NH0:
                load_w(h + 1)
            bufs = (xs, t1, t2, cs16, sn16)
            run_pass(Wt, wres, 256, [(QA, 0, 64, "QA", 16)], bufs, extra=lambda sx, cch: v_from_xs(sx, 0, cch))
            run_pass(Wt, wres, 384, [(KA0, 0, 64, "KA0", 16)], bufs, extra=lambda sx, cch: v_from_xs(sx, 1, cch))

            def extra_e(sx, cch):
                xb = (0, 1, 3)[cch % 3]
                k.op("act", lambda e: e.activation(VT2[0:64, (cch % 4) * 512:(cch % 4 + 1) * 512], ps[xb][0:64, :], AF.Copy),
                     reads=[psr(xb)], writes=["VT2"])
                gate_from_xs(sx, cch)
                if cch % 4 == 3:
                    v2_half(cch // 4)
            run_pass(Wt, wres, 512, [], bufs, extra=extra_e, do_rope=False)
            attn_group(2, KA0, "KA0", first=True)
            run_pass(Wt, wres, 0, [(QA, 0, 128, "QA", 1)], bufs)
            run_pass(Wt, wres, 128, [(KA0, 0, 64, "KA0", 1), (KA1, 64, 128, "KA1", 1)], bufs)
            attn_group(0, KA0, "KA0", first=False)
            attn_group(1, KA1, "KA1", first=False)
            hq = h % 2
            if True:
              for t0 in range(0, NT, 4):
                fb = (6, 7, 4, 5)[(t0 // 4) % 4]
                s = (t0 // 4) % 2
                for i in range(4):
                    tt = t0 + i
                    k.op("pe", lambda e, i=i, tt=tt: e.transpose(ps[fb][:, i * 65:(i + 1) * 65],
                                                                 acc[0:65, tt * 128:(tt + 1) * 128], identf[0:65, 0:65]),
                         reads=["acc", "identf"], writes=[psr(fb)], inc=(i == 3))
                trv = ps[fb][:, 0:260].rearrange("p (a b) -> p a b", a=4)
                k.op("dve", lambda e, trv=trv, s=s: e.reciprocal(rden[:, s * 4:s * 4 + 4], trv[:, :, 64]),
                     reads=[psr(fb)], writes=[f"rden{s}"])
                k.op("dve", lambda e, trv=trv, s=s: e.tensor_tensor(
                    out=ob[s], in0=trv[:, :, 0:64], in1=rden[:, s * 4:s * 4 + 4].unsqueeze(2).broadcast_to([128, 4, 64]),
                    op=ALU.mult), reads=[psr(fb), f"rden{s}"], writes=[f"ob{s}"])
                k.op("dve", lambda e, s=s, t0=t0: e.scalar_tensor_tensor(
                    out=y4[:, t0:t0 + 4, hq * 64:(hq + 1) * 64], in0=ob[s], scalar=0.5, in1=gs[:, t0:t0 + 4, :],
                    op0=ALU.mult, op1=ALU.mult), reads=[f"ob{s}", "gs"], writes=["y4"])
            if hq == 1:
                q4 = h // 2
                for half in range(2):
                    k.dma(y_view[:, half * 16:(half + 1) * 16, q4 * 128:(q4 + 1) * 128],
                          y4[:, half * 16:(half + 1) * 16, :], reads=["y4"], writes=["y_scr"])

    def phase_tail(layer, wo_d, res_d, dst_d, next_prenorm):
        c = Carve(PH_BASE + 5 * 8192)
        Wo = c.take([8, 1024], BF16)
        yt = [c.take([1024], BF16) for _ in range(3)]
        yT = [c.take([8, 128], BF16) for _ in range(3)]
        xr = [c.take([1024], F32) for _ in range(6)]
        hn = [c.take([1024], F32) for _ in range(3)]
        xn = [c.take([1024], BF16) for _ in range(3)]
        sqj = c.take([1024], BF16)
        sqj2 = c.take([1024], BF16)
        st2 = c.take([16], F32)
        k.dma(Wo, wo_d.rearrange("(kc p) n -> p kc n", p=128), writes=["Wo"], queue="pool")

        def st_L(tt):
            rows = slice(tt * 128, (tt + 1) * 128)
            k.dma(yt[tt % 3], y_scr[rows, :], reads=["y_scr"], writes=[f"yt{tt % 3}"])
            k.dma(xr[tt % 6], res_d[rows, :], reads=["h1_scr"] if res_d is h1_scr else [], writes=[f"xr{tt % 6}"])

        def st_A(tt):
            s3 = tt % 3
            pb = 4
            pT = ps[pb][:].bitcast(BF16)
            for kc in range(8):
                k.op("pe", lambda e, kc=kc: e.transpose(pT[:, kc * 128:(kc + 1) * 128],
                                                         yt[s3][:, kc * 128:(kc + 1) * 128], ident16),
                     reads=[f"yt{s3}", "ident16"], writes=[psr(pb)], inc=(kc == 7))
            k.op("act", lambda e: e.activation(yT[s3], pT.rearrange("p (a b) -> p a b", a=8), AF.Copy),
                 reads=[psr(pb)], writes=[f"yT{s3}"])

        def st_B(tt):
            s3 = tt % 3
            i = tt % 4
            for half in range(2):
                ob_ = ((0, 1), (2, 3), (6, 7))[tt % 3][half]
                for kc in range(8):
                    k.op("pe", lambda e, kc=kc, half=half, ob_=ob_: e.matmul(
                        ps[ob_][:], yT[s3][:, kc, :], Wo[:, kc, half * 512:(half + 1) * 512],
                        start=(kc == 0), stop=(kc == 7)),
                        reads=[f"yT{s3}", "Wo"], writes=[psr(ob_)], inc=(kc == 7))
            for half in range(2):
                ob_ = ((0, 1), (2, 3), (6, 7))[tt % 3][half]
                k.op("act", lambda e, half=half, ob_=ob_: e.activation(
                    sqj2[:, half * 512:(half + 1) * 512], ps[ob_][:], AF.Square,
                    accum_out=st2[:, 2 * i + half:2 * i + half + 1]),
                    reads=[psr(ob_)], writes=[f"sqj2_{half}", f"p_ss{i}_{half}"])

        def st_C(tt):
            s3 = tt % 3
            i = tt % 4
            rows = slice(tt * 128, (tt + 1) * 128)
            k.op("dve", lambda e: e.tensor_tensor(out=st2[:, 8 + i:9 + i], in0=st2[:, 2 * i:2 * i + 1],
                                                  in1=st2[:, 2 * i + 1:2 * i + 2], op=ALU.add),
                 reads=[f"p_ss{i}_0", f"p_ss{i}_1"], writes=[f"p_sum{i}"])
            k.op("act", lambda e: e.activation(st2[:, 8 + i:9 + i], st2[:, 8 + i:9 + i], AF.Ln, scale=1.0 / D, bias=EPS),
                 reads=[f"p_sum{i}"], writes=[f"p_sum{i}"])
            k.op("act", lambda e: e.activation(st2[:, 12 + i:13 + i], st2[:, 8 + i:9 + i], AF.Exp, scale=-0.5),
                 reads=[f"p_sum{i}"], writes=[f"p_rstd{i}"])
            for half in range(2):
                ob_ = ((0, 1), (2, 3), (6, 7))[tt % 3][half]
                k.op("dve", lambda e, half=half, ob_=ob_: e.scalar_tensor_tensor(
                    out=hn[s3][:, half * 512:(half + 1) * 512], in0=ps[ob_][:], scalar=st2[:, 12 + i:13 + i],
                    in1=gpost[:, layer, half * 512:(half + 1) * 512], op0=ALU.mult, op1=ALU.mult),
                    reads=[psr(ob_), f"p_rstd{i}", "gpost"], writes=[f"hn{s3}"])
            k.op("pool", lambda e: e.tensor_tensor(out=hn[s3], in0=hn[s3], in1=xr[tt % 6], op=ALU.add),
                 reads=[f"hn{s3}", f"xr{tt % 6}"], writes=[f"hn{s3}"])
            k.dma(dst_d[rows, :], hn[s3], reads=[f"hn{s3}"], writes=["h1_scr"] if dst_d is h1_scr else [])

        for i in range(NT + 6):
            if i < NT:
                st_L(i)
            t = i - 2
            if 0 <= t < NT:
                st_A(t)
            t = i - 3
            if 0 <= t < NT:
                st_B(t)
            t = i - 4
            if 0 <= t < NT:
                st_C(t)
            if next_prenorm:
                t = i - 5
                if 0 <= t < NT:
                    prenorm_a(hn[t % 3], f"hn{t % 3}", t, xn[t % 3], f"xn{t % 3}", sqj)
                t = i - 6
                if 0 <= t < NT:
                    prenorm_b(t, layer + 1, xn[t % 3], f"xn{t % 3}", bank=5)

    def phase_layer1():
        c = Carve(PH_BASE)
        QA = c.take([S], BF16)
        KA0 = c.take([S], BF16)
        KA1 = c.take([S], BF16)
        cs16 = c.take([S], BF16)
        sn16 = c.take([S], BF16)
        xs = [c.take([512], BF16) for _ in range(3)]
        t1 = [c.take([512], F32) for _ in range(2)]
        t2 = [c.take([512], F32) for _ in range(2)]
        PT = [c.take([512], BF16) for _ in range(4)]
        y2 = c.take([NT, 256], BF16)
        Vaug = c.take([NT, 129], BF16)
        gsl = c.take([NT, 128], BF16)
        W1 = [c.take([8, 512], BF16) for _ in range(2)]
        thb = [c.take([2, 128], F32) for _ in range(2)]
        g1b = [c.take([2, 128], F32) for _ in range(2)]
        o0 = c.take([4, 128], F32)
        od = c.take([4, 128], F32)
        sq = c.take([4, 128], F32)
        rr = c.take([32], F32)

        if stage != "full":
            k.dma(cs16, cs_d, writes=["cs16"], queue="pool", max_dma_last_dim=4096)
            k.dma(sn16, sn_d, writes=["sn16"], queue="pool", max_dma_last_dim=4096)
            k.op("pool", lambda e: e.memset(KA0[64:128, :], 0.0), writes=[f"KA0_{cc}" for cc in range(8)])
            k.op("pool", lambda e: e.memset(KA1[0:64, :], 0.0), writes=[f"KA1_{cc}" for cc in range(8)])
        k.op("pool", lambda e: e.memset(Vaug[:, :, 128:129], 1.0), writes=["Vaug"])

        def load_w(h):
            k.dma(W1[h % 2], w1_d[h].rearrange("(kc p) c -> p kc c", p=128), writes=[f"W1_{h % 2}"], queue="pool")

        load_w(0)
        for h in range(8):
            Wt = W1[h % 2]
            wres = f"W1_{h % 2}"
            if h + 1 < 8:
                load_w(h + 1)
            bufs = (xs, t1, t2, cs16, sn16)
            run_pass(Wt, wres, 0, [(QA, 0, 128, "QA", 1)], bufs)
            run_pass(Wt, wres, 128, [(KA0, 0, 64, "KA0", 1), (KA1, 64, 128, "KA1", 1)], bufs)
            for t0 in range(0, NT, 2):
                bank = 4 + (t0 // 2) % 4
                s = (t0 // 2) % 2
                for bi in range(2):
                    tt = t0 + bi
                    for kc in range(8):
                        k.op("pe", lambda e, kc=kc, bi=bi, tt=tt: e.matmul(
                            ps[bank][:, bi * 256:(bi + 1) * 256], uT[:, kc, tt * 128:(tt + 1) * 128],
                            Wt[:, kc, 256:512], start=(kc == 0), stop=(kc == 7)),
                            reads=[wres, "uT"], writes=[psr(bank)], inc=(kc == 7 and bi == 1))
                pv = ps[bank][:].rearrange("p (a b) -> p a b", a=2)
                k.op("act", lambda e, pv=pv, t0=t0: e.activation(Vaug[:, t0:t0 + 2, 0:128], pv[:, :, 0:128], AF.Copy),
                     reads=[psr(bank)], writes=["Vaug"])
                k.op("act", lambda e, pv=pv, s=s: e.activation(thb[s], pv[:, :, 128:256], AF.Tanh, scale=0.5),
                     reads=[psr(bank)], writes=[f"thb{s}"])
                k.op("dve", lambda e, pv=pv, s=s: e.scalar_tensor_tensor(
                    out=g1b[s], in0=thb[s], scalar=1.0, in1=pv[:, :, 128:256], op0=ALU.add, op1=ALU.mult),
                    reads=[psr(bank), f"thb{s}"], writes=[f"g1b{s}"])
                k.op("pool", lambda e, s=s, t0=t0: e.tensor_tensor(
                    out=gsl[:, t0:t0 + 2, :], in0=g1b[s], in1=sublnc.unsqueeze(1).broadcast_to([128, 2, 128]),
                    op=ALU.mult), reads=[f"g1b{s}", "sublnc"], writes=["gsl"])

            steps = []
            for qc in range(8):
                for comp in range(2):
                    for kb in range(4 * qc + 4):
                        steps.append((qc, comp, kb))
            cc_ctr = [0]
            started = {}

            def geom(i):
                qc, comp, kb = steps[i]
                m = kb - 4 * qc
                diag = m >= 0
                q0 = kb * 128 if diag else qc * 512
                nq = (qc + 1) * 512 - q0
                return qc, comp, kb, diag, q0, nq

            def emit_S(i):
                qc, comp, kb, diag, q0, nq = geom(i)
                sl = i % 4
                bank = 4 + sl
                Kt, kres = (KA0, "KA0") if comp == 0 else (KA1, "KA1")
                k.op("pe", lambda e: e.matmul(ps[bank][:, 0:nq], Kt[:, kb * 128:(kb + 1) * 128], QA[:, q0:q0 + nq],
                                              start=True, stop=not diag),
                     reads=[f"{kres}_{kb // 4}", f"QA_{qc}"], writes=[psr(bank)], inc=not diag)
                if diag:
                    k.op("pe", lambda e: e.matmul(ps[bank][:, 0:128], ident16, mask16[:, 0:128], start=False, stop=True),
                         reads=["ident16", "mask16"], writes=[psr(bank)])

            def emit_exp(i):
                qc, comp, kb, diag, q0, nq = geom(i)
                sl = i % 4
                bank = 4 + sl
                k.op("act", lambda e: e.activation(PT[sl][:, 0:nq], ps[bank][:, 0:nq], AF.Exp, scale=0.125),
                     reads=[psr(bank)], writes=[f"PT{sl}"])

            def emit_PV(i):
                qc, comp, kb, diag, q0, nq = geom(i)
                sl = i % 4
                cc = qc * 2 + comp
                ob0 = (cc % 2) * 2
                nt = nq // 128
                for ti in range(nt):
                    qt = q0 // 128 + ti
                    lq = qt - qc * 4
                    obank = ob0 + lq // 2
                    col = (lq % 2) * 129
                    key = (cc, lq // 2)
                    st = key not in started
                    started[key] = True
                    last = (kb == 4 * qc + 3) and (ti == nt - 1)
                    k.op("pe", lambda e, ti=ti, obank=obank, col=col, st=st: e.matmul(
                        ps[obank][:, col:col + 129], PT[sl][:, ti * 128:(ti + 1) * 128], Vaug[:, kb, :],
                        start=st, stop=True, skip_group_check=True),
                        reads=["Vaug", f"PT{sl}"], writes=[psr(obank)], inc=(ti == nt - 1))
                if kb == 4 * qc + 3:
                    t0 = qc * 4
                    for hb in range(2):
                        obank = ob0 + hb
                        ov = ps[obank][:, 0:258].rearrange("p (a b) -> p a b", a=2)
                        ro = (cc % 4) * 4 + hb * 2
                        k.op("dve", lambda e, ov=ov, ro=ro: e.reciprocal(rr[:, ro:ro + 2], ov[:, :, 128]),
                             reads=[psr(obank)], writes=[f"rr{ro}"])
                        if comp == 0:
                            k.op("dve", lambda e, ov=ov, ro=ro, hb=hb: e.tensor_tensor(
                                out=o0[:, hb * 2:hb * 2 + 2, :], in0=ov[:, :, 0:128],
                                in1=rr[:, ro:ro + 2].unsqueeze(2).broadcast_to([128, 2, 128]), op=ALU.mult),
                                reads=[psr(obank), f"rr{ro}"], writes=[f"o0_{hb}"])
                        else:
                            k.op("dve", lambda e, ro=ro: e.tensor_scalar(rr[:, ro:ro + 2], rr[:, ro:ro + 2], neglam, None, ALU.mult),
                                 reads=[f"rr{ro}", "neglam"], writes=[f"rr{ro}"])
                            k.op("dve", lambda e, ov=ov, ro=ro, hb=hb: e.tensor_tensor(
                                out=od[:, hb * 2:hb * 2 + 2, :], in0=ov[:, :, 0:128],
                                in1=rr[:, ro:ro + 2].unsqueeze(2).broadcast_to([128, 2, 128]), op=ALU.mult),
                                reads=[psr(obank), f"rr{ro}"], writes=[f"od_{hb}"])
                    if comp == 1:
                        def f1():
                            k.op("pool", lambda e: e.tensor_tensor(out=od, in0=od, in1=o0, op=ALU.add),
                                 reads=["od_0", "od_1", "o0_0", "o0_1"], writes=["od_0", "od_1"])

                        def f2():
                            k.op("dve", lambda e: e.tensor_tensor(out=sq, in0=od, in1=od, op=ALU.mult),
                                 reads=["od_0", "od_1"], writes=["sq"])
                            k.op("dve", lambda e: e.tensor_reduce(out=rr[:, 16:20], in_=sq, axis=AX.X, op=ALU.add),
                                 reads=["sq"], writes=["rr16"])

                        def f3():
                            k.op("act", lambda e: e.activation(rr[:, 20:24], rr[:, 16:20], AF.Ln, scale=1.0 / 128, bias=EPS),
                                 reads=["rr16"], writes=["rr20"])
                            k.op("act", lambda e: e.activation(rr[:, 24:28], rr[:, 20:24], AF.Exp, scale=-0.5),
                                 reads=["rr20"], writes=["rr24"])

                        def f4():
                            k.op("dve", lambda e: e.tensor_tensor(
                                out=sq, in0=od, in1=rr[:, 24:28].unsqueeze(2).broadcast_to([128, 4, 128]), op=ALU.mult),
                                reads=["od_0", "od_1", "rr24", "sq"], writes=["sq"])

                        def f5(t0=t0):
                            hp = h % 2
                            k.op("pool", lambda e: e.tensor_tensor(
                                out=y2[:, t0:t0 + 4, hp * 128:(hp + 1) * 128], in0=sq, in1=gsl[:, t0:t0 + 4, :], op=ALU.mult),
                                reads=["sq", "gsl"], writes=["y2"])
                        for dly, fn_ in ((4, f1), (8, f2), (12, f3), (15, f4), (18, f5)):
                            deferred.append((i + dly, fn_))

            deferred = []
            n = len(steps)
            emit_S(0)
            emit_S(1)
            emit_S(2)
            for i in range(n):
                emit_exp(i)
                if i + 3 < n:
                    emit_S(i + 3)
                emit_PV(i)
                deferred.sort(key=lambda x: x[0])
                while deferred and deferred[0][0] <= i:
                    deferred.pop(0)[1]()
            while deferred:
                deferred.pop(0)[1]()
            if h % 2 == 1:
                q2 = h // 2
                for half in range(2):
                    k.dma(y_view[:, half * 16:(half + 1) * 16, q2 * 256:(q2 + 1) * 256],
                          y2[:, half * 16:(half + 1) * 16, :], reads=["y2"], writes=["y_scr"])

    if stage.startswith("d_"):
        if stage >= "d_1":
            phase_prenorm_dram(x_d, 0)
            k.barrier()
        if stage >= "d_2":
            phase_layer0()
            k.barrier()
        dbg = Carve(PH_BASE).take([1024], F32)
        k.dma(dbg, x_d[0:128, :], writes=["dbg"])
        k.dma(out_d[0:128, :], dbg, reads=["dbg"])
    elif stage == "l0":
        phase_prenorm_dram(x_d, 0)
        k.barrier()
        phase_layer0()
        k.barrier()
        phase_tail(0, w0o_d, x_d, out_d, next_prenorm=False)
    elif stage == "l1":
        phase_prenorm_dram(x_d, 1)
        k.barrier()
        phase_layer1()
        k.barrier()
        phase_tail(1, w1o_d, x_d, out_d, next_prenorm=False)
    else:
        phase_prenorm_dram(x_d, 0)
        k.barrier()
        phase_layer0()
        k.barrier()
        phase_tail(0, w0o_d, x_d, h1_scr, next_prenorm=True)
        k.barrier()
        phase_layer1()
        k.barrier()
        phase_tail(1, w1o_d, h1_scr, out_d, next_prenorm=False)
    k.finish()
    return nc, k


def _consts():
    f32 = np.float32
    freqs = (np.float32(10000.0) ** (-(np.arange(0, 64, 2, dtype=f32)) / np.float32(64))).astype(f32)
    pos = np.arange(S, dtype=f32)
    ang = (pos[:, None] * freqs[None, :]).astype(f32)
    cos = np.cos(ang).astype(f32).T
    sin = np.sin(ang).astype(f32).T
    cs = np.ascontiguousarray(np.tile(cos, (4, 1)))
    sn = np.ascontiguousarray(np.tile(sin, (4, 1)))
    perm = np.zeros((128, 128), f32)
    for m in range(128):
        if (m % 64) < 32:
            perm[m + 32, m] = -1.0
        else:
            perm[m - 32, m] = 1.0
    ident = np.eye(128, dtype=f32)
    kk = np.arange(128)[:, None]
    qq = np.arange(128)[None, :]
    tri_cur = np.where(kk <= qq, 0.0, NEG).astype(f32)
    tri_prev = np.where(kk >= qq, 0.0, NEG).astype(f32)
    mask0 = np.ascontiguousarray(np.concatenate([tri_cur, tri_prev], axis=1))
    return cs, sn, perm, ident, mask0


def _layout_weights(dil_w_in, dil_w_out, diff_w_in, diff_w_out):
    w = dil_w_in[0]
    w0 = np.empty((16, 1024, 640), np.float32)
    for h in range(16):
        def qc(g):
            return slice((g * 16 + h) * 64, (g * 16 + h) * 64 + 64)

        def kc_(g):
            return slice(3072 + (g * 16 + h) * 64, 3072 + (g * 16 + h) * 64 + 64)

        def vc(g):
            return slice(6144 + (g * 16 + h) * 64, 6144 + (g * 16 + h) * 64 + 64)
        gc = slice(9216 + h * 64, 9216 + h * 64 + 64)
        cols = [qc(0), qc(1), kc_(0), kc_(1), qc(2), vc(0), kc_(2), vc(1), vc(2), gc]
        for i, sl in enumerate(cols):
            w0[h, :, i * 64:(i + 1) * 64] = w[:, sl]
    w1i = diff_w_in[0]
    w1 = np.empty((8, 1024, 512), np.float32)
    for h in range(8):
        w1[h, :, 0:128] = w1i[:, (2 * h) * 64:(2 * h + 2) * 64]
        w1[h, :, 128:256] = w1i[:, 1024 + (2 * h) * 64:1024 + (2 * h + 2) * 64]
        w1[h, :, 256:384] = w1i[:, 2048 + h * 128:2048 + (h + 1) * 128]
        w1[h, :, 384:512] = w1i[:, 3072 + h * 128:3072 + (h + 1) * 128]
    return w0, np.ascontiguousarray(dil_w_out[0]), w1, np.ascontiguousarray(diff_w_out[0])


_CACHE = {}


def _get_program(stage):
    if stage not in _CACHE:
        _CACHE[stage] = build_program(stage)[0]
    return _CACHE[stage]


def _common_maps(norm_pre, norm_post, dil_w_in, dil_w_out, diff_w_in, diff_w_out,
                 diff_lambda_q1, diff_lambda_k1, diff_lambda_q2, diff_lambda_k2, diff_subln):
    cs, sn, perm, ident, mask0 = _consts()
    w0, w0o, w1, w1o = _layout_weights(np.asarray(dil_w_in, np.float32), np.asarray(dil_w_out, np.float32),
                                       np.asarray(diff_w_in, np.float32), np.asarray(diff_w_out, np.float32))
    npre = np.asarray(norm_pre, np.float32)
    gpre = np.ascontiguousarray(npre.reshape(2, 8, 128).transpose(2, 0, 1).reshape(128, 16))
    lamv = np.ascontiguousarray(np.concatenate([np.asarray(a, np.float32).reshape(1, 64) for a in
                                                (diff_lambda_q1, diff_lambda_k1, diff_lambda_q2, diff_lambda_k2)], 0))
    return {"w0": w0, "w0o": w0o, "w1": w1, "w1o": w1o, "gpre": gpre,
            "gpost": np.ascontiguousarray(np.asarray(norm_post, np.float32)),
            "lamv": lamv, "subln": np.ascontiguousarray(np.asarray(diff_subln, np.float32).reshape(1, 128)),
            "cs": cs, "sn": sn, "perm": perm, "ident": ident, "mask0": mask0}


def run_stage(stage, xs, common):
    nc = _get_program(stage)
    in_maps = [dict(common, x=np.ascontiguousarray(xs[b])) for b in range(len(xs))]
    res = run_bass_kernel_spmd(nc, in_maps, core_ids=list(range(len(xs))))
    return np.stack([r["out"] for r in res.results], 0)


def kernel(x, norm_pre, norm_post, dil_w_in, dil_w_out, diff_w_in, diff_w_out,
           diff_lambda_q1, diff_lambda_k1, diff_lambda_q2, diff_lambda_k2, diff_subln):
    x = np.asarray(x, np.float32)
    common = _common_maps(norm_pre, norm_post, dil_w_in, dil_w_out, diff_w_in, diff_w_out,
                          diff_lambda_q1, diff_lambda_k1, diff_lambda_q2, diff_lambda_k2, diff_subln)
    out = run_stage("full", [x[b] for b in range(8)], common)
    return out.astype(np.float32)
```

```python
import contextlib
import math
import numpy as np
import concourse.bass as bass
import concourse.mybir as mybir
from concourse.bass_utils import run_bass_kernel_spmd

F32 = mybir.dt.float32
BF16 = mybir.dt.bfloat16
AF = mybir.ActivationFunctionType
ALU = mybir.AluOpType
AX = mybir.AxisListType

SEM_ROT = 30000
N_DMA_SEMS = 24

S = 4096
D = 1024
NT = 32
NEG = -30000.0
NH0 = 16
DBGV = 0
ROPE_STEPS = 5
L0_STEPS = 10
EPS = 1e-6
LAMBDA_INIT = 0.8 - 0.6 * math.exp(-0.3 * 1)
DIL = ((128, 1), (512, 4), (2048, 16))


class _Rec:
    def __init__(self):
        self.call = None

    def __getattr__(self, name):
        def f(*a, **kw):
            self.call = (name, a, kw)
            return self
        return f


def _capture(fn):
    r = _Rec()
    fn(r)
    name, a, kw = r.call
    return lambda e: getattr(e, name)(*a, **kw)


class KB:
    ENGS = ("pe", "act", "dve", "pool", "sp")

    def __init__(self, nc):
        self.nc = nc
        self.stack = contextlib.ExitStack()
        self.prog = {e: [] for e in self.ENGS}
        self.sem = {}
        self.cnt = {}
        self.nsem = 0
        self.waited = {e: {} for e in self.ENGS}
        self.res = {}
        self.pending = {e: [] for e in self.ENGS}
        self.last_tok = {e: None for e in self.ENGS}
        for e in ("pe", "act", "dve", "pool"):
            self._new_eng_sem(e)
        self.dma_sems = []
        self.dma_pools = {"sp": [], "pool": []}
        for i in range(N_DMA_SEMS):
            h = self.stack.enter_context(nc.semaphore(f"dq{i}"))
            self.dma_sems.append([h, 0, f"dq{i}"])
            self.dma_pools["sp" if i < 16 else "pool"].append(self.dma_sems[-1])
        self.dma_rr = {"sp": 0, "pool": 0}
        self.n_instr = {e: 0 for e in self.ENGS}

    def sb(self, name, shape, dt):
        return self.stack.enter_context(self.nc.sbuf_tensor(name, list(shape), dt))

    def ps(self, name, shape, dt):
        return self.stack.enter_context(self.nc.psum_tensor(name, list(shape), dt))

    def _new_eng_sem(self, e):
        name = f"s_{e}_{self.nsem}"
        self.nsem += 1
        h = self.stack.enter_context(self.nc.semaphore(name))
        self.sem[e] = (h, name)
        self.cnt[e] = 0

    def _wait(self, eng, tok):
        if tok is None:
            return
        h, name, val, src = tok
        if src == eng and eng == "pe":
            return
        w = self.waited[eng]
        if w.get(name, 0) >= val:
            return
        w[name] = val
        self.prog[eng].append(lambda e, h=h, val=val: e.wait_ge(h, val))

    def _deps(self, reads, writes):
        deps = []
        for r in reads:
            st = self.res.get(r)
            if st and st[0] is not None:
                deps.append(st[0])
        for w in writes:
            st = self.res.get(w)
            if st:
                if st[0] is not None:
                    deps.append(st[0])
                deps.extend(st[1])
        return deps

    @staticmethod
    def _max_per_sem(deps):
        best = {}
        for t in deps:
            if t is None:
                continue
            if t[1] not in best or best[t[1]][2] < t[2]:
                best[t[1]] = t
        return list(best.values())

    def _register(self, tok, reads, writes):
        for r in reads:
            st = self.res.setdefault(r, [None, []])
            st[1].append(tok)
            if len(st[1]) > 48:
                best = {}
                for t in st[1]:
                    if t[1] not in best or best[t[1]][2] < t[2]:
                        best[t[1]] = t
                st[1] = list(best.values())
        for w in writes:
            self.res[w] = [tok, []]

    def op(self, eng, fn, reads=(), writes=(), inc=True):
        fn = _capture(fn)
        writes = tuple(writes) + tuple(r for r in reads if r.startswith("ps"))
        reads = tuple(r for r in reads if not r.startswith("ps"))
        for tok in self._max_per_sem(self._deps(reads, writes)):
            self._wait(eng, tok)
        self.n_instr[eng] += 1
        if not inc:
            self.pending[eng].append((reads, writes))
            self.prog[eng].append(lambda e, fn=fn: fn(e))
            return None
        if self.cnt[eng] >= SEM_ROT:
            self._new_eng_sem(eng)
        h, name = self.sem[eng]
        self.cnt[eng] += 1
        tok = (h, name, self.cnt[eng], eng)
        self.prog[eng].append(lambda e, fn=fn, h=h: fn(e).then_inc(h, 1))
        for (r, w) in self.pending[eng]:
            self._register(tok, r, w)
        self.pending[eng] = []
        self._register(tok, reads, writes)
        self.last_tok[eng] = tok
        return tok

    def dma(self, out, in_, reads=(), writes=(), queue="sp", **kw):
        reads = tuple(reads)
        writes = tuple(writes)
        pool = self.dma_pools[queue]
        slot = pool[self.dma_rr[queue]]
        self.dma_rr[queue] = (self.dma_rr[queue] + 1) % len(pool)
        h, cur, name = slot
        if cur > 0:
            self._wait(queue, (h, name, cur, "dma"))
        for tok in self._max_per_sem(self._deps(reads, writes)):
            self._wait(queue, tok)
        slot[1] = cur + 16
        tok = (h, name, cur + 16, "dma")
        self.prog[queue].append(
            lambda e, out=out, in_=in_, h=h, kw=kw: e.dma_start(out=out, in_=in_, **kw).then_inc(h, 16))
        self._register(tok, reads, writes)
        self.n_instr[queue] += 1
        return tok

    def barrier(self):
        toks = [t for t in self.last_tok.values() if t is not None]
        dtoks = [(s[0], s[2], s[1], "dma") for s in self.dma_sems if s[1] > 0]
        for e in self.ENGS:
            for t in toks + dtoks:
                self._wait(e, t)

    def finish(self):
        toks = [t for t in self.last_tok.values() if t is not None]
        dtoks = [(s[0], s[2], s[1], "dma") for s in self.dma_sems if s[1] > 0]
        for t in toks + dtoks:
            self._wait("sp", t)
        with self.nc.Block() as block:
            @block.tensor
            def _(e):
                for f in self.prog["pe"]:
                    f(e)

            @block.scalar
            def _(e):
                for f in self.prog["act"]:
                    f(e)

            @block.vector
            def _(e):
                for f in self.prog["dve"]:
                    f(e)

            @block.gpsimd
            def _(e):
                for f in self.prog["pool"]:
                    f(e)

            @block.sync
            def _(e):
                for f in self.prog["sp"]:
                    f(e)
        self.stack.close()


def _prod(xs):
    p = 1
    for v in xs:
        p *= v
    return p


def build_program(stage):
    nc = bass.Bass("TRN2", target_bir_lowering=False)

    def din(name, shape):
        return nc.dram_tensor(name, list(shape), F32, kind="ExternalInput").ap()

    x_d = din("x", [S, D])
    w0_d = din("w0", [16, 1024, 640])
    w0o_d = din("w0o", [1024, 1024])
    w1_d = din("w1", [8, 1024, 512])
    w1o_d = din("w1o", [1024, 1024])
    gpre_d = din("gpre", [128, 16])
    gpost_d = din("gpost", [2, 1024])
    lamv_d = din("lamv", [4, 64])
    subln_d = din("subln", [1, 128])
    cs_d = din("cs", [128, S])
    sn_d = din("sn", [128, S])
    perm_d = din("perm", [128, 128])
    ident_d = din("ident", [128, 128])
    mask_d = din("mask0", [128, 256])
    out_d = nc.dram_tensor("out", [S, D], F32, kind="ExternalOutput").ap()
    y_scr = nc.dram_tensor("y_scr", [S, D], BF16, kind="Internal").ap()
    h1_scr = nc.dram_tensor("h1_scr", [S, D], F32, kind="Internal").ap()

    k = KB(nc)
    TOTAL = 206 * 1024
    M = k.sb("M", [128, TOTAL // 2], BF16)

    class Carve:
        def __init__(self, base):
            self.off = base

        def take(self, fs, dt):
            esz = 2 if dt == BF16 else 4
            nb = _prod(fs) * esz
            assert self.off % 4 == 0
            v = M[:, self.off // 2:(self.off + nb) // 2]
            if dt == F32:
                v = v.bitcast(F32)
            if len(fs) == 2:
                v = v.rearrange("p (a b) -> p a b", a=fs[0])
            elif len(fs) == 3:
                v = v.rearrange("p (a b c) -> p a b c", a=fs[0], b=fs[1])
            self.off += (nb + 63) // 64 * 64
            assert self.off <= TOTAL, (self.off, TOTAL)
            return v

    cm = Carve(0)
    uT = cm.take([8, S], BF16)
    ident16 = cm.take([128], BF16)
    perm16 = cm.take([128], BF16)
    mask16 = cm.take([256], BF16)
    identf = cm.take([128], F32)
    gpre = cm.take([16], F32)
    gpost = cm.take([2, 1024], F32)
    lamb = cm.take([4, 64], F32)
    lprod = cm.take([2, 64], F32)
    lsc = cm.take([8], F32)
    sublnc = cm.take([128], F32)
    stat = cm.take([16], F32)
    PH_BASE = cm.off

    ps = [k.ps(f"ps{i}", [128, 512], F32) for i in range(8)]

    def psr(i):
        return f"ps{i}"

    k.dma(ident16, ident_d, writes=["ident16"], queue="pool")
    k.dma(perm16, perm_d, writes=["perm16"], queue="pool")
    k.dma(mask16, mask_d, writes=["mask16"], queue="pool")
    k.dma(identf, ident_d, writes=["identf"])
    k.dma(gpre, gpre_d, writes=["gpre"])
    k.dma(gpost, gpost_d.unsqueeze(0).broadcast_to([128, 2, 1024]), writes=["gpost"])
    for i in range(8):
        k.op("dve", lambda e, i=i: e.memset(ps[i][:], 0.0), writes=[psr(i)])

    lam_needed = stage in ("full", "l1")
    if lam_needed:
        k.dma(lamb, lamv_d.unsqueeze(0).broadcast_to([128, 4, 64]), writes=["lamb"])
        k.dma(sublnc, subln_d.broadcast_to([128, 128]), writes=["sublnc_raw"])
        lv = lamb.rearrange("p (a b) c -> p a b c", a=2)
        k.op("dve", lambda e: e.tensor_tensor(out=lprod, in0=lv[:, :, 0, :], in1=lv[:, :, 1, :], op=ALU.mult),
             reads=["lamb"], writes=["lprod"])
        k.op("dve", lambda e: e.tensor_reduce(out=lsc[:, 0:2], in_=lprod, axis=AX.X, op=ALU.add),
             reads=["lprod"], writes=["lsc01"])
        k.op("act", lambda e: e.activation(lsc[:, 2:4], lsc[:, 0:2], AF.Exp), reads=["lsc01"], writes=["lsc23"])
        k.op("dve", lambda e: e.tensor_tensor(out=lsc[:, 4:5], in0=lsc[:, 3:4], in1=lsc[:, 2:3], op=ALU.subtract),
             reads=["lsc23"], writes=["lsc4"])
        k.op("dve", lambda e: e.tensor_scalar(lsc[:, 5:6], lsc[:, 4:5], -LAMBDA_INIT, None, ALU.add),
             reads=["lsc4"], writes=["neglam"])
        k.op("dve", lambda e: e.tensor_scalar(sublnc, sublnc, 0.5 * (1.0 - LAMBDA_INIT), None, ALU.mult),
             reads=["sublnc_raw"], writes=["sublnc"])
    neglam = lsc[:, 5:6]

    y_view = y_scr.rearrange("(t p) c -> p t c", p=128)

    def prenorm_a(src, src_res, tt, xn, xn_res, sqj):
        i = tt % 4
        k.op("act", lambda e: e.activation(sqj, src, AF.Square, accum_out=stat[:, i:i + 1]),
             reads=[src_res], writes=["sqj", f"ss{i}"])
        k.op("act", lambda e: e.activation(stat[:, 4 + i:5 + i], stat[:, i:i + 1], AF.Ln, scale=1.0 / D, bias=EPS),
             reads=[f"ss{i}"], writes=[f"lnv{i}"])
        k.op("act", lambda e: e.activation(stat[:, 8 + i:9 + i], stat[:, 4 + i:5 + i], AF.Exp, scale=-0.5),
             reads=[f"lnv{i}"], writes=[f"rstd{i}"])
        k.op("dve", lambda e: e.tensor_scalar(xn, src, stat[:, 8 + i:9 + i], None, ALU.mult),
             reads=[src_res, f"rstd{i}"], writes=[xn_res])

    def prenorm_b(tt, layer, xn, xn_res, bank=None):
        pb = 6 + (tt % 2) if bank is None else bank
        pT = ps[pb][:].bitcast(BF16)
        for kc in range(8):
            k.op("pe", lambda e, kc=kc: e.transpose(pT[:, kc * 128:(kc + 1) * 128], xn[:, kc * 128:(kc + 1) * 128],
                                                     ident16),
                 reads=[xn_res, "ident16"], writes=[psr(pb)], inc=(kc == 7))
        gb = gpre[:, layer * 8:(layer + 1) * 8].unsqueeze(2).broadcast_to([128, 8, 128])
        k.op("dve", lambda e: e.tensor_tensor(out=uT[:, :, tt * 128:(tt + 1) * 128],
                                              in0=pT.rearrange("p (a b) -> p a b", a=8), in1=gb, op=ALU.mult),
             reads=[psr(pb), "gpre"], writes=["uT"])

    def phase_prenorm_dram(src_d, layer):
        c = Carve(PH_BASE)
        xt = [c.take([1024], F32) for _ in range(4)]
        xn = [c.take([1024], BF16) for _ in range(3)]
        sqj = c.take([1024], BF16)
        for i in range(NT + 3):
            t = i
            if t < NT:
                k.dma(xt[t % 4], src_d[t * 128:(t + 1) * 128, :], writes=[f"xt{t % 4}"])
            t = i - 2
            if 0 <= t < NT:
                prenorm_a(xt[t % 4], f"xt{t % 4}", t, xn[t % 3], f"xn{t % 3}", sqj)
            t = i - 3
            if 0 <= t < NT:
                prenorm_b(t, layer, xn[t % 3], f"xn{t % 3}")

    rope_ctr = [0]

    def rope_a(bank, xs, eng="act"):
        s = rope_ctr[0] % 3
        rope_ctr[0] += 1
        if eng == "act":
            k.op("act", lambda e: e.activation(xs[s], ps[bank][:], AF.Copy), reads=[psr(bank)], writes=[f"xs{s}"])
        else:
            k.op("dve", lambda e: e.tensor_copy(xs[s], ps[bank][:]), reads=[psr(bank)], writes=[f"xs{s}"])
        return s

    def rope_b(s, bank, c, outs, xs, t1, t2, cs16, sn16):
        rb = 2
        ch = slice(c * 512, (c + 1) * 512)
        xs_s = s
        s = c % 2
        k.op("pe", lambda e: e.matmul(ps[rb][:], perm16, xs[xs_s], start=True, stop=True),
             reads=[f"xs{xs_s}", "perm16"], writes=[psr(rb)])
        k.op("dve", lambda e: e.tensor_tensor(out=t1[s], in0=ps[bank][:], in1=cs16[:, ch], op=ALU.mult),
             reads=[psr(bank), "cs16"], writes=[f"t1{s}"])
        k.op("dve", lambda e: e.tensor_tensor(out=t2[s], in0=ps[rb][:], in1=sn16[:, ch], op=ALU.mult),
             reads=[psr(rb), "sn16"], writes=[f"t2{s}"])
        for oi, (dest, lo, hi, res, dd) in enumerate(outs):
            eng = "pool"
            if dd == 1:
                k.op(eng, lambda e, dest=dest, lo=lo, hi=hi: e.tensor_tensor(
                    out=dest[lo:hi, ch], in0=t1[s][lo:hi, :], in1=t2[s][lo:hi, :], op=ALU.add),
                    reads=[f"t1{s}", f"t2{s}"], writes=[f"{res}_{c}"])
            else:
                n = 512 // dd
                dv = dest[lo:hi, :].rearrange("p (r i) -> p r i", r=dd)[:, :, c * n:(c + 1) * n]
                a0 = t1[s][lo:hi, :].rearrange("p (i r) -> p r i", r=dd)
                a1 = t2[s][lo:hi, :].rearrange("p (i r) -> p r i", r=dd)
                k.op(eng, lambda e, dv=dv, a0=a0, a1=a1: e.tensor_tensor(out=dv, in0=a0, in1=a1, op=ALU.add),
                     reads=[f"t1{s}", f"t2{s}"], writes=[f"{res}_{cc}" for cc in range(8)])

    def run_pass(Wt, wres, col0, outs, bufs, extra=None, do_rope=True, copy_eng="act"):
        xs, t1, t2, cs16, sn16 = bufs
        slots = {}

        XB = (0, 1, 3)

        def stage2(c):
            if do_rope:
                rope_b(slots[c], XB[c % 3], c, outs, xs, t1, t2, cs16, sn16)
            if extra is not None:
                extra(slots[c], c)

        for c in range(8):
            proj_fm(Wt, wres, col0, XB[c % 3], c)
            slots[c] = rope_a(XB[c % 3], xs, copy_eng)
            if c > 0:
                stage2(c - 1)
        stage2(7)

    def proj_fm(Wt, wres, col0, bank, c):
        for kc in range(8):
            k.op("pe", lambda e, kc=kc: e.matmul(ps[bank][:], Wt[:, kc, col0:col0 + 128],
                                                 uT[:, kc, c * 512:(c + 1) * 512], start=(kc == 0), stop=(kc == 7)),
                 reads=[wres, "uT"], writes=[psr(bank)], inc=(kc == 7))

    def phase_layer0():
        c = Carve(PH_BASE)
        QA = c.take([S], BF16)
        KA0 = c.take([S], BF16)
        KA1 = c.take([S], BF16)
        cs16 = c.take([S], BF16)
        sn16 = c.take([S], BF16)
        xs = [c.take([512], BF16) for _ in range(3)]
        t1 = [c.take([512], F32) for _ in range(2)]
        t2 = [c.take([512], F32) for _ in range(2)]
        PT = [c.take([256], BF16) for _ in range(4)]
        y4 = c.take([NT, 128], BF16)
        VT2 = c.take([2048], BF16)
        Vaug = c.take([3, NT, 65], BF16)
        gs = c.take([NT, 64], BF16)
        acc = c.take([S], F32)
        W0 = [c.take([8, 640], BF16) for _ in range(2)]
        thb = [c.take([4, 64], F32) for _ in range(2)]
        ob = [c.take([4, 64], F32) for _ in range(2)]
        rden = c.take([8], F32)
        m01 = c.take([256], BF16)
        k.op("dve", lambda e: e.tensor_scalar(m01, mask16, 0.0, None, ALU.is_equal), reads=["mask16"], writes=["m01"])

        k.dma(cs16, cs_d, writes=["cs16"], queue="pool", max_dma_last_dim=4096)
        k.dma(sn16, sn_d, writes=["sn16"], queue="pool", max_dma_last_dim=4096)
        k.op("pool", lambda e: e.memset(KA0[64:128, :], 0.0), writes=[f"KA0_{cc}" for cc in range(8)])
        k.op("pool", lambda e: e.memset(KA1[0:64, :], 0.0), writes=[f"KA1_{cc}" for cc in range(8)])
        k.op("pool", lambda e: e.memset(QA, 0.0), writes=[f"QA_{cc}" for cc in range(8)])
        k.op("pool", lambda e: e.memset(Vaug[:, :, :, 64:65], 1.0), writes=["Vaug0", "Vaug1", "Vaug2"])

        def load_w(h):
            k.dma(W0[h % 2], w0_d[h].rearrange("(kc p) c -> p kc c", p=128), writes=[f"W0_{h % 2}"], queue="pool")

        def tok(d, r, a, b):
            return slice(r + a * d, r + (b - 1) * d + 1, d)

        tr_ctr = [0]

        def v_from_xs(sx, g, cch):
            bank = 4 + tr_ctr[0] % 4
            tr_ctr[0] += 1
            pT = ps[bank][:].bitcast(BF16)
            for i in range(4):
                src = xs[sx][:, i * 128:(i + 1) * 128] if g == 0 else xs[sx][:, i:512:4]
                k.op("pe", lambda e, i=i, src=src: e.transpose(pT[:, i * 128:(i + 1) * 128], src, ident16),
                     reads=[f"xs{sx}", "ident16"], writes=[psr(bank)], inc=(i == 3))
            pv = pT[:, 0:512].rearrange("p (a b) -> p a b", a=4)
            if g == 0:
                dst = Vaug[:, 0, 4 * cch:4 * cch + 4, 0:64]
            else:
                dst = Vaug[:, 1, cch:NT:8, 0:64]
            k.op("act", lambda e: e.activation(dst, pv[:, :, 64:128], AF.Copy),
                 reads=[psr(bank)], writes=[f"Vaug{g}"])

        def gate_from_xs(sx, cch):
            bank = 4 + tr_ctr[0] % 4
            tr_ctr[0] += 1
            st_ = tr_ctr[0] % 2
            pT = ps[bank][:].bitcast(BF16)
            for i in range(4):
                k.op("pe", lambda e, i=i: e.transpose(pT[:, i * 128:(i + 1) * 128], xs[sx][:, i * 128:(i + 1) * 128], ident16),
                     reads=[f"xs{sx}", "ident16"], writes=[psr(bank)], inc=(i == 3))
            pv = pT[:, 0:512].rearrange("p (a b) -> p a b", a=4)
            k.op("act", lambda e: e.activation(thb[st_], pv[:, :, 64:128], AF.Tanh, scale=0.5),
                 reads=[psr(bank)], writes=[f"thb{st_}"])
            k.op("dve", lambda e: e.scalar_tensor_tensor(
                out=gs[:, 4 * cch:4 * cch + 4, :], in0=thb[st_], scalar=1.0, in1=pv[:, :, 64:128], op0=ALU.add, op1=ALU.mult),
                reads=[psr(bank), f"thb{st_}"], writes=["gs"])

        def v2_half(j):
            bank = 4 + tr_ctr[0] % 4
            tr_ctr[0] += 1
            pT = ps[bank][:].bitcast(BF16)
            for r in range(16):
                k.op("pe", lambda e, r=r: e.transpose(pT[:, r * 64:(r + 1) * 64], VT2[0:64, r:2048:16], ident16[0:64, 0:64]),
                     reads=["VT2", "ident16"], writes=[psr(bank)], inc=(r == 15))
            k.op("act", lambda e: e.activation(Vaug[:, 2, j:NT:2, 0:64], pT.rearrange("p (a b) -> p a b", a=16), AF.Copy),
                 reads=[psr(bank)], writes=["Vaug2"])

        def attn_group(g, Kt, kres, first):
            d = DIL[g][1]
            nb = S // d // 128
            steps = [(r, j) for r in range(d) for j in range(nb)]
            started = {}

            L_ = S // d

            def emit_S(i):
                r, j = steps[i]
                sl = i % 4
                bank = (0, 1, 4, 5)[i % 4]
                nq = 256 if j + 1 < nb else 128
                if d == 16:
                    k0 = r * L_ + j * 128
                    ksl, qsl = slice(k0, k0 + 128), slice(k0, k0 + nq)
                    rd = [f"{kres}_{cc}" for cc in range(8)] + [f"QA_{cc}" for cc in range(8)]
                else:
                    ksl, qsl = tok(d, r, j * 128, (j + 1) * 128), tok(d, r, j * 128, j * 128 + nq)
                    t_lo = j * 128 * d
                    t_hi = (j * 128 + nq) * d - 1
                    rd = [f"{kres}_{t_lo // 512}"] + [f"QA_{cc}" for cc in range(t_lo // 512, min(t_hi // 512, 7) + 1)]
                k.op("pe", lambda e: e.matmul(ps[bank][:, 0:nq], Kt[:, ksl], QA[:, qsl], start=True, stop=False),
                     reads=rd, writes=[psr(bank)], inc=False)
                k.op("pe", lambda e: e.matmul(ps[bank][:, 0:nq], ident16, mask16[:, 0:nq], start=False, stop=True),
                     reads=["ident16", "mask16"], writes=[psr(bank)])

            def emit_exp(i):
                r, j = steps[i]
                sl = i % 4
                bank = (0, 1, 4, 5)[i % 4]
                nq = 256 if j + 1 < nb else 128
                k.op("act", lambda e: e.activation(PT[sl][:, 0:nq], ps[bank][:, 0:nq], AF.Exp, scale=0.125),
                     reads=[psr(bank)], writes=[f"PT{sl}"])

            def pv_mm(b, V_b, pt_ap, sl, st, sp):
                fill = b // 4
                obank = 2 + fill % 2
                col = (b % 4) * 128
                k.op("pe", lambda e: e.matmul(ps[obank][0:65, col:col + 128], Vaug[:, g, V_b, :], pt_ap,
                                              start=st, stop=sp),
                     reads=[f"Vaug{g}", f"PT{sl}"], writes=[psr(obank)], inc=True)

            def emit_PV(i):
                r, j = steps[i]
                sl = i % 4
                b = r * nb + j
                has_next = j + 1 < nb
                pv_mm(b, b, PT[sl][:, 0:128], sl, st=(j == 0), sp=True)
                if has_next:
                    pv_mm(b + 1, b, PT[sl][:, 128:256], sl, st=True, sp=False)
                if b % 4 == 3:
                    fill = b // 4
                    obank = 2 + fill % 2
                    b0 = fill * 4
                    runs = []
                    bb = b0
                    while bb < b0 + 4:
                        rr, jj = divmod(bb, nb)
                        ln = min(4 - (bb - b0), nb - jj)
                        runs.append((rr, jj, ln, (bb - b0) * 128))
                        bb += ln
                    for (rr, jj, ln, col) in runs:
                        dst = acc[0:65, tok(d, rr, jj * 128, (jj + ln) * 128)]
                        src = ps[obank][0:65, col:col + ln * 128]
                        if first:
                            k.op("dve", lambda e, dst=dst, src=src: e.tensor_copy(dst, src),
                                 reads=[psr(obank)], writes=["acc"])
                        else:
                            k.op("dve", lambda e, dst=dst, src=src: e.tensor_tensor(out=dst, in0=src, in1=dst, op=ALU.add),
                                 reads=[psr(obank), "acc"], writes=["acc"])

            n = len(steps)
            for i0 in range(min(3, n)):
                emit_S(i0)
            for i in range(n):
                emit_exp(i)
                if i + 3 < n:
                    emit_S(i + 3)
                emit_PV(i)

        load_w(0)
        for h in range(NH0):
            Wt = W0[h % 2]
            wres = f"W0_{h % 2}"
            if h + 1 < NH0:
                load_w(h + 1)
            bufs = (xs, t1, t2, cs16, sn16)
            run_pass(Wt, wres, 256, [(QA, 0, 64, "QA", 16)], bufs, extra=lambda sx, cch: v_from_xs(sx, 0, cch))
            run_pass(Wt, wres, 384, [(KA0, 0, 64, "KA0", 16)], bufs, extra=lambda sx, cch: v_from_xs(sx, 1, cch))

            def extra_e(sx, cch):
                xb = (0, 1, 3)[cch % 3]
                k.op("act", lambda e: e.activation(VT2[0:64, (cch % 4) * 512:(cch % 4 + 1) * 512], ps[xb][0:64, :], AF.Copy),
                     reads=[psr(xb)], writes=["VT2"])
                gate_from_xs(sx, cch)
                if cch % 4 == 3:
                    v2_half(cch // 4)
            run_pass(Wt, wres, 512, [], bufs, extra=extra_e, do_rope=False)
            attn_group(2, KA0, "KA0", first=True)
            run_pass(Wt, wres, 0, [(QA, 0, 128, "QA", 1)], bufs)
            run_pass(Wt, wres, 128, [(KA0, 0, 64, "KA0", 1), (KA1, 64, 128, "KA1", 1)], bufs)
            attn_group(0, KA0, "KA0", first=False)
            attn_group(1, KA1, "KA1", first=False)
            hq = h % 2
            if True:
              for t0 in range(0, NT, 4):
                fb = (6, 7, 4, 5)[(t0 // 4) % 4]
                s = (t0 // 4) % 2
                for i in range(4):
                    tt = t0 + i
                    k.op("pe", lambda e, i=i, tt=tt: e.transpose(ps[fb][:, i * 65:(i + 1) * 65],
                                                                 acc[0:65, tt * 128:(tt + 1) * 128], identf[0:65, 0:65]),
                         reads=["acc", "identf"], writes=[psr(fb)], inc=(i == 3))
                trv = ps[fb][:, 0:260].rearrange("p (a b) -> p a b", a=4)
                k.op("dve", lambda e, trv=trv, s=s: e.reciprocal(rden[:, s * 4:s * 4 + 4], trv[:, :, 64]),
                     reads=[psr(fb)], writes=[f"rden{s}"])
                k.op("dve", lambda e, trv=trv, s=s: e.tensor_tensor(
                    out=ob[s], in0=trv[:, :, 0:64], in1=rden[:, s * 4:s * 4 + 4].unsqueeze(2).broadcast_to([128, 4, 64]),
                    op=ALU.mult), reads=[psr(fb), f"rden{s}"], writes=[f"ob{s}"])
                k.op("dve", lambda e, s=s, t0=t0: e.scalar_tensor_tensor(
                    out=y4[:, t0:t0 + 4, hq * 64:(hq + 1) * 64], in0=ob[s], scalar=0.5, in1=gs[:, t0:t0 + 4, :],
                    op0=ALU.mult, op1=ALU.mult), reads=[f"ob{s}", "gs"], writes=["y4"])
            if hq == 1:
                q4 = h // 2
                for half in range(2):
                    k.dma(y_view[:, half * 16:(half + 1) * 16, q4 * 128:(q4 + 1) * 128],
                          y4[:, half * 16:(half + 1) * 16, :], reads=["y4"], writes=["y_scr"])

    def phase_tail(layer, wo_d, res_d, dst_d, next_prenorm):
        c = Carve(PH_BASE + 5 * 8192)
        Wo = c.take([8, 1024], BF16)
        yt = [c.take([1024], BF16) for _ in range(3)]
        yT = [c.take([8, 128], BF16) for _ in range(3)]
        xr = [c.take([1024], F32) for _ in range(6)]
        hn = [c.take([1024], F32) for _ in range(3)]
        xn = [c.take([1024], BF16) for _ in range(3)]
        sqj = c.take([1024], BF16)
        sqj2 = c.take([1024], BF16)
        st2 = c.take([16], F32)
        k.dma(Wo, wo_d.rearrange("(kc p) n -> p kc n", p=128), writes=["Wo"], queue="pool")

        def st_L(tt):
            rows = slice(tt * 128, (tt + 1) * 128)
            k.dma(yt[tt % 3], y_scr[rows, :], reads=["y_scr"], writes=[f"yt{tt % 3}"])
            k.dma(xr[tt % 6], res_d[rows, :], reads=["h1_scr"] if res_d is h1_scr else [], writes=[f"xr{tt % 6}"])

        def st_A(tt):
            s3 = tt % 3
            pb = 4
            pT = ps[pb][:].bitcast(BF16)
            for kc in range(8):
                k.op("pe", lambda e, kc=kc: e.transpose(pT[:, kc * 128:(kc + 1) * 128],
                                                         yt[s3][:, kc * 128:(kc + 1) * 128], ident16),
                     reads=[f"yt{s3}", "ident16"], writes=[psr(pb)], inc=(kc == 7))
            k.op("act", lambda e: e.activation(yT[s3], pT.rearrange("p (a b) -> p a b", a=8), AF.Copy),
                 reads=[psr(pb)], writes=[f"yT{s3}"])

        def st_B(tt):
            s3 = tt % 3
            i = tt % 4
            for half in range(2):
                ob_ = ((0, 1), (2, 3), (6, 7))[tt % 3][half]
                for kc in range(8):
                    k.op("pe", lambda e, kc=kc, half=half, ob_=ob_: e.matmul(
                        ps[ob_][:], yT[s3][:, kc, :], Wo[:, kc, half * 512:(half + 1) * 512],
                        start=(kc == 0), stop=(kc == 7)),
                        reads=[f"yT{s3}", "Wo"], writes=[psr(ob_)], inc=(kc == 7))
            for half in range(2):
                ob_ = ((0, 1), (2, 3), (6, 7))[tt % 3][half]
                k.op("act", lambda e, half=half, ob_=ob_: e.activation(
                    sqj2[:, half * 512:(half + 1) * 512], ps[ob_][:], AF.Square,
                    accum_out=st2[:, 2 * i + half:2 * i + half + 1]),
                    reads=[psr(ob_)], writes=[f"sqj2_{half}", f"p_ss{i}_{half}"])

        def st_C(tt):
            s3 = tt % 3
            i = tt % 4
            rows = slice(tt * 128, (tt + 1) * 128)
            k.op("dve", lambda e: e.tensor_tensor(out=st2[:, 8 + i:9 + i], in0=st2[:, 2 * i:2 * i + 1],
                                                  in1=st2[:, 2 * i + 1:2 * i + 2], op=ALU.add),
                 reads=[f"p_ss{i}_0", f"p_ss{i}_1"], writes=[f"p_sum{i}"])
            k.op("act", lambda e: e.activation(st2[:, 8 + i:9 + i], st2[:, 8 + i:9 + i], AF.Ln, scale=1.0 / D, bias=EPS),
                 reads=[f"p_sum{i}"], writes=[f"p_sum{i}"])
            k.op("act", lambda e: e.activation(st2[:, 12 + i:13 + i], st2[:, 8 + i:9 + i], AF.Exp, scale=-0.5),
                 reads=[f"p_sum{i}"], writes=[f"p_rstd{i}"])
            for half in range(2):
                ob_ = ((0, 1), (2, 3), (6, 7))[tt % 3][half]
                k.op("dve", lambda e, half=half, ob_=ob_: e.scalar_tensor_tensor(
                    out=hn[s3][:, half * 512:(half + 1) * 512], in0=ps[ob_][:], scalar=st2[:, 12 + i:13 + i],
                    in1=gpost[:, layer, half * 512:(half + 1) * 512], op0=ALU.mult, op1=ALU.mult),
                    reads=[psr(ob_), f"p_rstd{i}", "gpost"], writes=[f"hn{s3}"])
            k.op("pool", lambda e: e.tensor_tensor(out=hn[s3], in0=hn[s3], in1=xr[tt % 6], op=ALU.add),
                 reads=[f"hn{s3}", f"xr{tt % 6}"], writes=[f"hn{s3}"])
            k.dma(dst_d[rows, :], hn[s3], reads=[f"hn{s3}"], writes=["h1_scr"] if dst_d is h1_scr else [])

        for i in range(NT + 6):
            if i < NT:
                st_L(i)
            t = i - 2
            if 0 <= t < NT:
                st_A(t)
            t = i - 3
            if 0 <= t < NT:
                st_B(t)
            t = i - 4
            if 0 <= t < NT:
                st_C(t)
            if next_prenorm:
                t = i - 5
                if 0 <= t < NT:
                    prenorm_a(hn[t % 3], f"hn{t % 3}", t, xn[t % 3], f"xn{t % 3}", sqj)
                t = i - 6
                if 0 <= t < NT:
                    prenorm_b(t, layer + 1, xn[t % 3], f"xn{t % 3}", bank=5)

    def phase_layer1():
        c = Carve(PH_BASE)
        QA = c.take([S], BF16)
        KA0 = c.take([S], BF16)
        KA1 = c.take([S], BF16)
        cs16 = c.take([S], BF16)
        sn16 = c.take([S], BF16)
        xs = [c.take([512], BF16) for _ in range(3)]
        t1 = [c.take([512], F32) for _ in range(2)]
        t2 = [c.take([512], F32) for _ in range(2)]
        PT = [c.take([512], BF16) for _ in range(4)]
        y2 = c.take([NT, 256], BF16)
        Vaug = c.take([NT, 129], BF16)
        gsl = c.take([NT, 128], BF16)
        W1 = [c.take([8, 512], BF16) for _ in range(2)]
        thb = [c.take([2, 128], F32) for _ in range(2)]
        g1b = [c.take([2, 128], F32) for _ in range(2)]
        o0 = c.take([4, 128], F32)
        od = c.take([4, 128], F32)
        sq = c.take([4, 128], F32)
        rr = c.take([32], F32)

        if stage != "full":
            k.dma(cs16, cs_d, writes=["cs16"], queue="pool", max_dma_last_dim=4096)
            k.dma(sn16, sn_d, writes=["sn16"], queue="pool", max_dma_last_dim=4096)
            k.op("pool", lambda e: e.memset(KA0[64:128, :], 0.0), writes=[f"KA0_{cc}" for cc in range(8)])
            k.op("pool", lambda e: e.memset(KA1[0:64, :], 0.0), writes=[f"KA1_{cc}" for cc in range(8)])
        k.op("pool", lambda e: e.memset(Vaug[:, :, 128:129], 1.0), writes=["Vaug"])

        def load_w(h):
            k.dma(W1[h % 2], w1_d[h].rearrange("(kc p) c -> p kc c", p=128), writes=[f"W1_{h % 2}"], queue="pool")

        load_w(0)
        for h in range(8):
            Wt = W1[h % 2]
            wres = f"W1_{h % 2}"
            if h + 1 < 8:
                load_w(h + 1)
            bufs = (xs, t1, t2, cs16, sn16)
            run_pass(Wt, wres, 0, [(QA, 0, 128, "QA", 1)], bufs)
            run_pass(Wt, wres, 128, [(KA0, 0, 64, "KA0", 1), (KA1, 64, 128, "KA1", 1)], bufs)
            for t0 in range(0, NT, 2):
                bank = 4 + (t0 // 2) % 4
                s = (t0 // 2) % 2
                for bi in range(2):
                    tt = t0 + bi
                    for kc in range(8):
                        k.op("pe", lambda e, kc=kc, bi=bi, tt=tt: e.matmul(
                            ps[bank][:, bi * 256:(bi + 1) * 256], uT[:, kc, tt * 128:(tt + 1) * 128],
                            Wt[:, kc, 256:512], start=(kc == 0), stop=(kc == 7)),
                            reads=[wres, "uT"], writes=[psr(bank)], inc=(kc == 7 and bi == 1))
                pv = ps[bank][:].rearrange("p (a b) -> p a b", a=2)
                k.op("act", lambda e, pv=pv, t0=t0: e.activation(Vaug[:, t0:t0 + 2, 0:128], pv[:, :, 0:128], AF.Copy),
                     reads=[psr(bank)], writes=["Vaug"])
                k.op("act", lambda e, pv=pv, s=s: e.activation(thb[s], pv[:, :, 128:256], AF.Tanh, scale=0.5),
                     reads=[psr(bank)], writes=[f"thb{s}"])
                k.op("dve", lambda e, pv=pv, s=s: e.scalar_tensor_tensor(
                    out=g1b[s], in0=thb[s], scalar=1.0, in1=pv[:, :, 128:256], op0=ALU.add, op1=ALU.mult),
                    reads=[psr(bank), f"thb{s}"], writes=[f"g1b{s}"])
                k.op("pool", lambda e, s=s, t0=t0: e.tensor_tensor(
                    out=gsl[:, t0:t0 + 2, :], in0=g1b[s], in1=sublnc.unsqueeze(1).broadcast_to([128, 2, 128]),
                    op=ALU.mult), reads=[f"g1b{s}", "sublnc"], writes=["gsl"])

            steps = []
            for qc in range(8):
                for comp in range(2):
                    for kb in range(4 * qc + 4):
                        steps.append((qc, comp, kb))
            cc_ctr = [0]
            started = {}

            def geom(i):
                qc, comp, kb = steps[i]
                m = kb - 4 * qc
                diag = m >= 0
                q0 = kb * 128 if diag else qc * 512
                nq = (qc + 1) * 512 - q0
                return qc, comp, kb, diag, q0, nq

            def emit_S(i):
                qc, comp, kb, diag, q0, nq = geom(i)
                sl = i % 4
                bank = 4 + sl
                Kt, kres = (KA0, "KA0") if comp == 0 else (KA1, "KA1")
                k.op("pe", lambda e: e.matmul(ps[bank][:, 0:nq], Kt[:, kb * 128:(kb + 1) * 128], QA[:, q0:q0 + nq],
                                              start=True, stop=not diag),
                     reads=[f"{kres}_{kb // 4}", f"QA_{qc}"], writes=[psr(bank)], inc=not diag)
                if diag:
                    k.op("pe", lambda e: e.matmul(ps[bank][:, 0:128], ident16, mask16[:, 0:128], start=False, stop=True),
                         reads=["ident16", "mask16"], writes=[psr(bank)])

            def emit_exp(i):
                qc, comp, kb, diag, q0, nq = geom(i)
                sl = i % 4
                bank = 4 + sl
                k.op("act", lambda e: e.activation(PT[sl][:, 0:nq], ps[bank][:, 0:nq], AF.Exp, scale=0.125),
                     reads=[psr(bank)], writes=[f"PT{sl}"])

            def emit_PV(i):
                qc, comp, kb, diag, q0, nq = geom(i)
                sl = i % 4
                cc = qc * 2 + comp
                ob0 = (cc % 2) * 2
                nt = nq // 128
                for ti in range(nt):
                    qt = q0 // 128 + ti
                    lq = qt - qc * 4
                    obank = ob0 + lq // 2
                    col = (lq % 2) * 129
                    key = (cc, lq // 2)
                    st = key not in started
                    started[key] = True
                    last = (kb == 4 * qc + 3) and (ti == nt - 1)
                    k.op("pe", lambda e, ti=ti, obank=obank, col=col, st=st: e.matmul(
                        ps[obank][:, col:col + 129], PT[sl][:, ti * 128:(ti + 1) * 128], Vaug[:, kb, :],
                        start=st, stop=True, skip_group_check=True),
                        reads=["Vaug", f"PT{sl}"], writes=[psr(obank)], inc=(ti == nt - 1))
                if kb == 4 * qc + 3:
                    t0 = qc * 4
                    for hb in range(2):
                        obank = ob0 + hb
                        ov = ps[obank][:, 0:258].rearrange("p (a b) -> p a b", a=2)
                        ro = (cc % 4) * 4 + hb * 2
                        k.op("dve", lambda e, ov=ov, ro=ro: e.reciprocal(rr[:, ro:ro + 2], ov[:, :, 128]),
                             reads=[psr(obank)], writes=[f"rr{ro}"])
                        if comp == 0:
                            k.op("dve", lambda e, ov=ov, ro=ro, hb=hb: e.tensor_tensor(
                                out=o0[:, hb * 2:hb * 2 + 2, :], in0=ov[:, :, 0:128],
                                in1=rr[:, ro:ro + 2].unsqueeze(2).broadcast_to([128, 2, 128]), op=ALU.mult),
                                reads=[psr(obank), f"rr{ro}"], writes=[f"o0_{hb}"])
                        else:
                            k.op("dve", lambda e, ro=ro: e.tensor_scalar(rr[:, ro:ro + 2], rr[:, ro:ro + 2], neglam, None, ALU.mult),
                                 reads=[f"rr{ro}", "neglam"], writes=[f"rr{ro}"])
                            k.op("dve", lambda e, ov=ov, ro=ro, hb=hb: e.tensor_tensor(
                                out=od[:, hb * 2:hb * 2 + 2, :], in0=ov[:, :, 0:128],
                                in1=rr[:, ro:ro + 2].unsqueeze(2).broadcast_to([128, 2, 128]), op=ALU.mult),
                                reads=[psr(obank), f"rr{ro}"], writes=[f"od_{hb}"])
                    if comp == 1:
                        def f1():
                            k.op("pool", lambda e: e.tensor_tensor(out=od, in0=od, in1=o0, op=ALU.add),
                                 reads=["od_0", "od_1", "o0_0", "o0_1"], writes=["od_0", "od_1"])

                        def f2():
                            k.op("dve", lambda e: e.tensor_tensor(out=sq, in0=od, in1=od, op=ALU.mult),
                                 reads=["od_0", "od_1"], writes=["sq"])
                            k.op("dve", lambda e: e.tensor_reduce(out=rr[:, 16:20], in_=sq, axis=AX.X, op=ALU.add),
                                 reads=["sq"], writes=["rr16"])

                        def f3():
                            k.op("act", lambda e: e.activation(rr[:, 20:24], rr[:, 16:20], AF.Ln, scale=1.0 / 128, bias=EPS),
                                 reads=["rr16"], writes=["rr20"])
                            k.op("act", lambda e: e.activation(rr[:, 24:28], rr[:, 20:24], AF.Exp, scale=-0.5),
                                 reads=["rr20"], writes=["rr24"])

                        def f4():
                            k.op("dve", lambda e: e.tensor_tensor(
                                out=sq, in0=od, in1=rr[:, 24:28].unsqueeze(2).broadcast_to([128, 4, 128]), op=ALU.mult),
                                reads=["od_0", "od_1", "rr24", "sq"], writes=["sq"])

                        def f5(t0=t0):
                            hp = h % 2
                            k.op("pool", lambda e: e.tensor_tensor(
                                out=y2[:, t0:t0 + 4, hp * 128:(hp + 1) * 128], in0=sq, in1=gsl[:, t0:t0 + 4, :], op=ALU.mult),
                                reads=["sq", "gsl"], writes=["y2"])
                        for dly, fn_ in ((4, f1), (8, f2), (12, f3), (15, f4), (18, f5)):
                            deferred.append((i + dly, fn_))

            deferred = []
            n = len(steps)
            emit_S(0)
            emit_S(1)
            emit_S(2)
            for i in range(n):
                emit_exp(i)
                if i + 3 < n:
                    emit_S(i + 3)
                emit_PV(i)
                deferred.sort(key=lambda x: x[0])
                while deferred and deferred[0][0] <= i:
                    deferred.pop(0)[1]()
            while deferred:
                deferred.pop(0)[1]()
            if h % 2 == 1:
                q2 = h // 2
                for half in range(2):
                    k.dma(y_view[:, half * 16:(half + 1) * 16, q2 * 256:(q2 + 1) * 256],
                          y2[:, half * 16:(half + 1) * 16, :], reads=["y2"], writes=["y_scr"])

    if stage.startswith("d_"):
        if stage >= "d_1":
            phase_prenorm_dram(x_d, 0)
            k.barrier()
        if stage >= "d_2":
            phase_layer0()
            k.barrier()
        dbg = Carve(PH_BASE).take([1024], F32)
        k.dma(dbg, x_d[0:128, :], writes=["dbg"])
        k.dma(out_d[0:128, :], dbg, reads=["dbg"])
    elif stage == "l0":
        phase_prenorm_dram(x_d, 0)
        k.barrier()
        phase_layer0()
        k.barrier()
        phase_tail(0, w0o_d, x_d, out_d, next_prenorm=False)
    elif stage == "l1":
        phase_prenorm_dram(x_d, 1)
        k.barrier()
        phase_layer1()
        k.barrier()
        phase_tail(1, w1o_d, x_d, out_d, next_prenorm=False)
    else:
        phase_prenorm_dram(x_d, 0)
        k.barrier()
        phase_layer0()
        k.barrier()
        phase_tail(0, w0o_d, x_d, h1_scr, next_prenorm=True)
        k.barrier()
        phase_layer1()
        k.barrier()
        phase_tail(1, w1o_d, h1_scr, out_d, next_prenorm=False)
    k.finish()
    return nc, k


def _consts():
    f32 = np.float32
    freqs = (np.float32(10000.0) ** (-(np.arange(0, 64, 2, dtype=f32)) / np.float32(64))).astype(f32)
    pos = np.arange(S, dtype=f32)
    ang = (pos[:, None] * freqs[None, :]).astype(f32)
    cos = np.cos(ang).astype(f32).T
    sin = np.sin(ang).astype(f32).T
    cs = np.ascontiguousarray(np.tile(cos, (4, 1)))
    sn = np.ascontiguousarray(np.tile(sin, (4, 1)))
    perm = np.zeros((128, 128), f32)
    for m in range(128):
        if (m % 64) < 32:
            perm[m + 32, m] = -1.0
        else:
            perm[m - 32, m] = 1.0
    ident = np.eye(128, dtype=f32)
    kk = np.arange(128)[:, None]
    qq = np.arange(128)[None, :]
    tri_cur = np.where(kk <= qq, 0.0, NEG).astype(f32)
    tri_prev = np.where(kk >= qq, 0.0, NEG).astype(f32)
    mask0 = np.ascontiguousarray(np.concatenate([tri_cur, tri_prev], axis=1))
    return cs, sn, perm, ident, mask0


def _layout_weights(dil_w_in, dil_w_out, diff_w_in, diff_w_out):
    w = dil_w_in[0]
    w0 = np.empty((16, 1024, 640), np.float32)
    for h in range(16):
        def qc(g):
            return slice((g * 16 + h) * 64, (g * 16 + h) * 64 + 64)

        def kc_(g):
            return slice(3072 + (g * 16 + h) * 64, 3072 + (g * 16 + h) * 64 + 64)

        def vc(g):
            return slice(6144 + (g * 16 + h) * 64, 6144 + (g * 16 + h) * 64 + 64)
        gc = slice(9216 + h * 64, 9216 + h * 64 + 64)
        cols = [qc(0), qc(1), kc_(0), kc_(1), qc(2), vc(0), kc_(2), vc(1), vc(2), gc]
        for i, sl in enumerate(cols):
            w0[h, :, i * 64:(i + 1) * 64] = w[:, sl]
    w1i = diff_w_in[0]
    w1 = np.empty((8, 1024, 512), np.float32)
    for h in range(8):
        w1[h, :, 0:128] = w1i[:, (2 * h) * 64:(2 * h + 2) * 64]
        w1[h, :, 128:256] = w1i[:, 1024 + (2 * h) * 64:1024 + (2 * h + 2) * 64]
        w1[h, :, 256:384] = w1i[:, 2048 + h * 128:2048 + (h + 1) * 128]
        w1[h, :, 384:512] = w1i[:, 3072 + h * 128:3072 + (h + 1) * 128]
    return w0, np.ascontiguousarray(dil_w_out[0]), w1, np.ascontiguousarray(diff_w_out[0])


_CACHE = {}


def _get_program(stage):
    if stage not in _CACHE:
        _CACHE[stage] = build_program(stage)[0]
    return _CACHE[stage]


def _common_maps(norm_pre, norm_post, dil_w_in, dil_w_out, diff_w_in, diff_w_out,
                 diff_lambda_q1, diff_lambda_k1, diff_lambda_q2, diff_lambda_k2, diff_subln):
    cs, sn, perm, ident, mask0 = _consts()
    w0, w0o, w1, w1o = _layout_weights(np.asarray(dil_w_in, np.float32), np.asarray(dil_w_out, np.float32),
                                       np.asarray(diff_w_in, np.float32), np.asarray(diff_w_out, np.float32))
    npre = np.asarray(norm_pre, np.float32)
    gpre = np.ascontiguousarray(npre.reshape(2, 8, 128).transpose(2, 0, 1).reshape(128, 16))
    lamv = np.ascontiguousarray(np.concatenate([np.asarray(a, np.float32).reshape(1, 64) for a in
                                                (diff_lambda_q1, diff_lambda_k1, diff_lambda_q2, diff_lambda_k2)], 0))
    return {"w0": w0, "w0o": w0o, "w1": w1, "w1o": w1o, "gpre": gpre,
            "gpost": np.ascontiguousarray(np.asarray(norm_post, np.float32)),
            "lamv": lamv, "subln": np.ascontiguousarray(np.asarray(diff_subln, np.float32).reshape(1, 128)),
            "cs": cs, "sn": sn, "perm": perm, "ident": ident, "mask0": mask0}


def run_stage(stage, xs, common):
    nc = _get_program(stage)
    in_maps = [dict(common, x=np.ascontiguousarray(xs[b])) for b in range(len(xs))]
    res = run_bass_kernel_spmd(nc, in_maps, core_ids=list(range(len(xs))))
    return np.stack([r["out"] for r in res.results], 0)


def kernel(x, norm_pre, norm_post, dil_w_in, dil_w_out, diff_w_in, diff_w_out,
           diff_lambda_q1, diff_lambda_k1, diff_lambda_q2, diff_lambda_k2, diff_subln):
    x = np.asarray(x, np.float32)
    common = _common_maps(norm_pre, norm_post, dil_w_in, dil_w_out, diff_w_in, diff_w_out,
                          diff_lambda_q1, diff_lambda_k1, diff_lambda_q2, diff_lambda_k2, diff_subln)
    out = run_stage("full", [x[b] for b in range(8)], common)
    return out.astype(np.float32)
```

```python
import contextlib
import math
import numpy as np
import concourse.bass as bass
import concourse.mybir as mybir
from concourse.bass_utils import run_bass_kernel_spmd

F32 = mybir.dt.float32
BF16 = mybir.dt.bfloat16
AF = mybir.ActivationFunctionType
ALU = mybir.AluOpType
AX = mybir.AxisListType

SEM_ROT = 30000
N_DMA_SEMS = 24

S = 4096
D = 1024
NT = 32
NEG = -30000.0
NH0 = 16
DBGV = 0
ROPE_STEPS = 5
L0_STEPS = 10
EPS = 1e-6
LAMBDA_INIT = 0.8 - 0.6 * math.exp(-0.3 * 1)
DIL = ((128, 1), (512, 4), (2048, 16))


class _Rec:
    def __init__(self):
        self.call = None

    def __getattr__(self, name):
        def f(*a, **kw):
            self.call = (name, a, kw)
            return self
        return f


def _capture(fn):
    r = _Rec()
    fn(r)
    name, a, kw = r.call
    return lambda e: getattr(e, name)(*a, **kw)


class KB:
    ENGS = ("pe", "act", "dve", "pool", "sp")

    def __init__(self, nc):
        self.nc = nc
        self.stack = contextlib.ExitStack()
        self.prog = {e: [] for e in self.ENGS}
        self.sem = {}
        self.cnt = {}
        self.nsem = 0
        self.waited = {e: {} for e in self.ENGS}
        self.res = {}
        self.pending = {e: [] for e in self.ENGS}
        self.last_tok = {e: None for e in self.ENGS}
        for e in ("pe", "act", "dve", "pool"):
            self._new_eng_sem(e)
        self.dma_sems = []
        self.dma_pools = {"sp": [], "pool": []}
        for i in range(N_DMA_SEMS):
            h = self.stack.enter_context(nc.semaphore(f"dq{i}"))
            self.dma_sems.append([h, 0, f"dq{i}"])
            self.dma_pools["sp" if i < 16 else "pool"].append(self.dma_sems[-1])
        self.dma_rr = {"sp": 0, "pool": 0}
        self.n_instr = {e: 0 for e in self.ENGS}

    def sb(self, name, shape, dt):
        return self.stack.enter_context(self.nc.sbuf_tensor(name, list(shape), dt))

    def ps(self, name, shape, dt):
        return self.stack.enter_context(self.nc.psum_tensor(name, list(shape), dt))

    def _new_eng_sem(self, e):
        name = f"s_{e}_{self.nsem}"
        self.nsem += 1
        h = self.stack.enter_context(self.nc.semaphore(name))
        self.sem[e] = (h, name)
        self.cnt[e] = 0

    def _wait(self, eng, tok):
        if tok is None:
            return
        h, name, val, src = tok
        if src == eng and eng == "pe":
            return
        w = self.waited[eng]
        if w.get(name, 0) >= val:
            return
        w[name] = val
        self.prog[eng].append(lambda e, h=h, val=val: e.wait_ge(h, val))

    def _deps(self, reads, writes):
        deps = []
        for r in reads:
            st = self.res.get(r)
            if st and st[0] is not None:
                deps.append(st[0])
        for w in writes:
            st = self.res.get(w)
            if st:
                if st[0] is not None:
                    deps.append(st[0])
                deps.extend(st[1])
        return deps

    @staticmethod
    def _max_per_sem(deps):
        best = {}
        for t in deps:
            if t is None:
                continue
            if t[1] not in best or best[t[1]][2] < t[2]:
                best[t[1]] = t
        return list(best.values())

    def _register(self, tok, reads, writes):
        for r in reads:
            st = self.res.setdefault(r, [None, []])
            st[1].append(tok)
            if len(st[1]) > 48:
                best = {}
                for t in st[1]:
                    if t[1] not in best or best[t[1]][2] < t[2]:
                        best[t[1]] = t
                st[1] = list(best.values())
        for w in writes:
            self.res[w] = [tok, []]

    def op(self, eng, fn, reads=(), writes=(), inc=True):
        fn = _capture(fn)
        writes = tuple(writes) + tuple(r for r in reads if r.startswith("ps"))
        reads = tuple(r for r in reads if not r.startswith("ps"))
        for tok in self._max_per_sem(self._deps(reads, writes)):
            self._wait(eng, tok)
        self.n_instr[eng] += 1
        if not inc:
            self.pending[eng].append((reads, writes))
            self.prog[eng].append(lambda e, fn=fn: fn(e))
            return None
        if self.cnt[eng] >= SEM_ROT:
            self._new_eng_sem(eng)
        h, name = self.sem[eng]
        self.cnt[eng] += 1
        tok = (h, name, self.cnt[eng], eng)
        self.prog[eng].append(lambda e, fn=fn, h=h: fn(e).then_inc(h, 1))
        for (r, w) in self.pending[eng]:
            self._register(tok, r, w)
        self.pending[eng] = []
        self._register(tok, reads, writes)
        self.last_tok[eng] = tok
        return tok

    def dma(self, out, in_, reads=(), writes=(), queue="sp", **kw):
        reads = tuple(reads)
        writes = tuple(writes)
        pool = self.dma_pools[queue]
        slot = pool[self.dma_rr[queue]]
        self.dma_rr[queue] = (self.dma_rr[queue] + 1) % len(pool)
        h, cur, name = slot
        if cur > 0:
            self._wait(queue, (h, name, cur, "dma"))
        for tok in self._max_per_sem(self._deps(reads, writes)):
            self._wait(queue, tok)
        slot[1] = cur + 16
        tok = (h, name, cur + 16, "dma")
        self.prog[queue].append(
            lambda e, out=out, in_=in_, h=h, kw=kw: e.dma_start(out=out, in_=in_, **kw).then_inc(h, 16))
        self._register(tok, reads, writes)
        self.n_instr[queue] += 1
        return tok

    def barrier(self):
        toks = [t for t in self.last_tok.values() if t is not None]
        dtoks = [(s[0], s[2], s[1], "dma") for s in self.dma_sems if s[1] > 0]
        for e in self.ENGS:
            for t in toks + dtoks:
                self._wait(e, t)

    def finish(self):
        toks = [t for t in self.last_tok.values() if t is not None]
        dtoks = [(s[0], s[2], s[1], "dma") for s in self.dma_sems if s[1] > 0]
        for t in toks + dtoks:
            self._wait("sp", t)
        with self.nc.Block() as block:
            @block.tensor
            def _(e):
                for f in self.prog["pe"]:
                    f(e)

            @block.scalar
            def _(e):
                for f in self.prog["act"]:
                    f(e)

            @block.vector
            def _(e):
                for f in self.prog["dve"]:
                    f(e)

            @block.gpsimd
            def _(e):
                for f in self.prog["pool"]:
                    f(e)

            @block.sync
            def _(e):
                for f in self.prog["sp"]:
                    f(e)
        self.stack.close()


def _prod(xs):
    p = 1
    for v in xs:
        p *= v
    return p


def build_program(stage):
    nc = bass.Bass("TRN2", target_bir_lowering=False)

    def din(name, shape):
        return nc.dram_tensor(name, list(shape), F32, kind="ExternalInput").ap()

    x_d = din("x", [S, D])
    w0_d = din("w0", [16, 1024, 640])
    w0o_d = din("w0o", [1024, 1024])
    w1_d = din("w1", [8, 1024, 512])
    w1o_d = din("w1o", [1024, 1024])
    gpre_d = din("gpre", [128, 16])
    gpost_d = din("gpost", [2, 1024])
    lamv_d = din("lamv", [4, 64])
    subln_d = din("subln", [1, 128])
    cs_d = din("cs", [128, S])
    sn_d = din("sn", [128, S])
    perm_d = din("perm", [128, 128])
    ident_d = din("ident", [128, 128])
    mask_d = din("mask0", [128, 256])
    out_d = nc.dram_tensor("out", [S, D], F32, kind="ExternalOutput").ap()
    y_scr = nc.dram_tensor("y_scr", [S, D], BF16, kind="Internal").ap()
    h1_scr = nc.dram_tensor("h1_scr", [S, D], F32, kind="Internal").ap()

    k = KB(nc)
    TOTAL = 206 * 1024
    M = k.sb("M", [128, TOTAL // 2], BF16)

    class Carve:
        def __init__(self, base):
            self.off = base

        def take(self, fs, dt):
            esz = 2 if dt == BF16 else 4
            nb = _prod(fs) * esz
            assert self.off % 4 == 0
            v = M[:, self.off // 2:(self.off + nb) // 2]
            if dt == F32:
                v = v.bitcast(F32)
            if len(fs) == 2:
                v = v.rearrange("p (a b) -> p a b", a=fs[0])
            elif len(fs) == 3:
                v = v.rearrange("p (a b c) -> p a b c", a=fs[0], b=fs[1])
            self.off += (nb + 63) // 64 * 64
            assert self.off <= TOTAL, (self.off, TOTAL)
            return v

    cm = Carve(0)
    uT = cm.take([8, S], BF16)
    ident16 = cm.take([128], BF16)
    perm16 = cm.take([128], BF16)
    mask16 = cm.take([256], BF16)
    identf = cm.take([128], F32)
    gpre = cm.take([16], F32)
    gpost = cm.take([2, 1024], F32)
    lamb = cm.take([4, 64], F32)
    lprod = cm.take([2, 64], F32)
    lsc = cm.take([8], F32)
    sublnc = cm.take([128], F32)
    stat = cm.take([16], F32)
    PH_BASE = cm.off

    ps = [k.ps(f"ps{i}", [128, 512], F32) for i in range(8)]

    def psr(i):
        return f"ps{i}"

    k.dma(ident16, ident_d, writes=["ident16"], queue="pool")
    k.dma(perm16, perm_d, writes=["perm16"], queue="pool")
    k.dma(mask16, mask_d, writes=["mask16"], queue="pool")
    k.dma(identf, ident_d, writes=["identf"])
    k.dma(gpre, gpre_d, writes=["gpre"])
    k.dma(gpost, gpost_d.unsqueeze(0).broadcast_to([128, 2, 1024]), writes=["gpost"])
    for i in range(8):
        k.op("dve", lambda e, i=i: e.memset(ps[i][:], 0.0), writes=[psr(i)])

    lam_needed = stage in ("full", "l1")
    if lam_needed:
        k.dma(lamb, lamv_d.unsqueeze(0).broadcast_to([128, 4, 64]), writes=["lamb"])
        k.dma(sublnc, subln_d.broadcast_to([128, 128]), writes=["sublnc_raw"])
        lv = lamb.rearrange("p (a b) c -> p a b c", a=2)
        k.op("dve", lambda e: e.tensor_tensor(out=lprod, in0=lv[:, :, 0, :], in1=lv[:, :, 1, :], op=ALU.mult),
             reads=["lamb"], writes=["lprod"])
        k.op("dve", lambda e: e.tensor_reduce(out=lsc[:, 0:2], in_=lprod, axis=AX.X, op=ALU.add),
             reads=["lprod"], writes=["lsc01"])
        k.op("act", lambda e: e.activation(lsc[:, 2:4], lsc[:, 0:2], AF.Exp), reads=["lsc01"], writes=["lsc23"])
        k.op("dve", lambda e: e.tensor_tensor(out=lsc[:, 4:5], in0=lsc[:, 3:4], in1=lsc[:, 2:3], op=ALU.subtract),
             reads=["lsc23"], writes=["lsc4"])
        k.op("dve", lambda e: e.tensor_scalar(lsc[:, 5:6], lsc[:, 4:5], -LAMBDA_INIT, None, ALU.add),
             reads=["lsc4"], writes=["neglam"])
        k.op("dve", lambda e: e.tensor_scalar(sublnc, sublnc, 0.5 * (1.0 - LAMBDA_INIT), None, ALU.mult),
             reads=["sublnc_raw"], writes=["sublnc"])
    neglam = lsc[:, 5:6]

    y_view = y_scr.rearrange("(t p) c -> p t c", p=128)

    def prenorm_a(src, src_res, tt, xn, xn_res, sqj):
        i = tt % 4
        k.op("act", lambda e: e.activation(sqj, src, AF.Square, accum_out=stat[:, i:i + 1]),
             reads=[src_res], writes=["sqj", f"ss{i}"])
        k.op("act", lambda e: e.activation(stat[:, 4 + i:5 + i], stat[:, i:i + 1], AF.Ln, scale=1.0 / D, bias=EPS),
             reads=[f"ss{i}"], writes=[f"lnv{i}"])
        k.op("act", lambda e: e.activation(stat[:, 8 + i:9 + i], stat[:, 4 + i:5 + i], AF.Exp, scale=-0.5),
             reads=[f"lnv{i}"], writes=[f"rstd{i}"])
        k.op("dve", lambda e: e.tensor_scalar(xn, src, stat[:, 8 + i:9 + i], None, ALU.mult),
             reads=[src_res, f"rstd{i}"], writes=[xn_res])

    def prenorm_b(tt, layer, xn, xn_res, bank=None):
        pb = 6 + (tt % 2) if bank is None else bank
        pT = ps[pb][:].bitcast(BF16)
        for kc in range(8):
            k.op("pe", lambda e, kc=kc: e.transpose(pT[:, kc * 128:(kc + 1) * 128], xn[:, kc * 128:(kc + 1) * 128],
                                                     ident16),
                 reads=[xn_res, "ident16"], writes=[psr(pb)], inc=(kc == 7))
        gb = gpre[:, layer * 8:(layer + 1) * 8].unsqueeze(2).broadcast_to([128, 8, 128])
        k.op("dve", lambda e: e.tensor_tensor(out=uT[:, :, tt * 128:(tt + 1) * 128],
                                              in0=pT.rearrange("p (a b) -> p a b", a=8), in1=gb, op=ALU.mult),
             reads=[psr(pb), "gpre"], writes=["uT"])

    def phase_prenorm_dram(src_d, layer):
        c = Carve(PH_BASE)
        xt = [c.take([1024], F32) for _ in range(4)]
        xn = [c.take([1024], BF16) for _ in range(3)]
        sqj = c.take([1024], BF16)
        for i in range(NT + 3):
            t = i
            if t < NT:
                k.dma(xt[t % 4], src_d[t * 128:(t + 1) * 128, :], writes=[f"xt{t % 4}"])
            t = i - 2
            if 0 <= t < NT:
                prenorm_a(xt[t % 4], f"xt{t % 4}", t, xn[t % 3], f"xn{t % 3}", sqj)
            t = i - 3
            if 0 <= t < NT:
                prenorm_b(t, layer, xn[t % 3], f"xn{t % 3}")

    rope_ctr = [0]

    def rope_a(bank, xs, eng="act"):
        s = rope_ctr[0] % 3
        rope_ctr[0] += 1
        if eng == "act":
            k.op("act", lambda e: e.activation(xs[s], ps[bank][:], AF.Copy), reads=[psr(bank)], writes=[f"xs{s}"])
        else:
            k.op("dve", lambda e: e.tensor_copy(xs[s], ps[bank][:]), reads=[psr(bank)], writes=[f"xs{s}"])
        return s

    def rope_b(s, bank, c, outs, xs, t1, t2, cs16, sn16):
        rb = 2
        ch = slice(c * 512, (c + 1) * 512)
        xs_s = s
        s = c % 2
        k.op("pe", lambda e: e.matmul(ps[rb][:], perm16, xs[xs_s], start=True, stop=True),
             reads=[f"xs{xs_s}", "perm16"], writes=[psr(rb)])
        k.op("dve", lambda e: e.tensor_tensor(out=t1[s], in0=ps[bank][:], in1=cs16[:, ch], op=ALU.mult),
             reads=[psr(bank), "cs16"], writes=[f"t1{s}"])
        k.op("dve", lambda e: e.tensor_tensor(out=t2[s], in0=ps[rb][:], in1=sn16[:, ch], op=ALU.mult),
             reads=[psr(rb), "sn16"], writes=[f"t2{s}"])
        for oi, (dest, lo, hi, res, dd) in enumerate(outs):
            eng = "pool"
            if dd == 1:
                k.op(eng, lambda e, dest=dest, lo=lo, hi=hi: e.tensor_tensor(
                    out=dest[lo:hi, ch], in0=t1[s][lo:hi, :], in1=t2[s][lo:hi, :], op=ALU.add),
                    reads=[f"t1{s}", f"t2{s}"], writes=[f"{res}_{c}"])
            else:
                n = 512 // dd
                dv = dest[lo:hi, :].rearrange("p (r i) -> p r i", r=dd)[:, :, c * n:(c + 1) * n]
                a0 = t1[s][lo:hi, :].rearrange("p (i r) -> p r i", r=dd)
                a1 = t2[s][lo:hi, :].rearrange("p (i r) -> p r i", r=dd)
                k.op(eng, lambda e, dv=dv, a0=a0, a1=a1: e.tensor_tensor(out=dv, in0=a0, in1=a1, op=ALU.add),
                     reads=[f"t1{s}", f"t2{s}"], writes=[f"{res}_{cc}" for cc in range(8)])

    def run_pass(Wt, wres, col0, outs, bufs, extra=None, do_rope=True, copy_eng="act"):
        xs, t1, t2, cs16, sn16 = bufs
        slots = {}

        XB = (0, 1, 3)

        def stage2(c):
            if do_rope:
                rope_b(slots[c], XB[c % 3], c, outs, xs, t1, t2, cs16, sn16)
            if extra is not None:
                extra(slots[c], c)

        for c in range(8):
            proj_fm(Wt, wres, col0, XB[c % 3], c)
            slots[c] = rope_a(XB[c % 3], xs, copy_eng)
            if c > 0:
                stage2(c - 1)
        stage2(7)

    def proj_fm(Wt, wres, col0, bank, c):
        for kc in range(8):
            k.op("pe", lambda e, kc=kc: e.matmul(ps[bank][:], Wt[:, kc, col0:col0 + 128],
                                                 uT[:, kc, c * 512:(c + 1) * 512], start=(kc == 0), stop=(kc == 7)),
                 reads=[wres, "uT"], writes=[psr(bank)], inc=(kc == 7))

    def phase_layer0():
        c = Carve(PH_BASE)
        QA = c.take([S], BF16)
        KA0 = c.take([S], BF16)
        KA1 = c.take([S], BF16)
        cs16 = c.take([S], BF16)
        sn16 = c.take([S], BF16)
        xs = [c.take([512], BF16) for _ in range(3)]
        t1 = [c.take([512], F32) for _ in range(2)]
        t2 = [c.take([512], F32) for _ in range(2)]
        PT = [c.take([256], BF16) for _ in range(4)]
        y4 = c.take([NT, 128], BF16)
        VT2 = c.take([2048], BF16)
        Vaug = c.take([3, NT, 65], BF16)
        gs = c.take([NT, 64], BF16)
        acc = c.take([S], F32)
        W0 = [c.take([8, 640], BF16) for _ in range(2)]
        thb = [c.take([4, 64], F32) for _ in range(2)]
        ob = [c.take([4, 64], F32) for _ in range(2)]
        rden = c.take([8], F32)
        m01 = c.take([256], BF16)
        k.op("dve", lambda e: e.tensor_scalar(m01, mask16, 0.0, None, ALU.is_equal), reads=["mask16"], writes=["m01"])

        k.dma(cs16, cs_d, writes=["cs16"], queue="pool", max_dma_last_dim=4096)
        k.dma(sn16, sn_d, writes=["sn16"], queue="pool", max_dma_last_dim=4096)
        k.op("pool", lambda e: e.memset(KA0[64:128, :], 0.0), writes=[f"KA0_{cc}" for cc in range(8)])
        k.op("pool", lambda e: e.memset(KA1[0:64, :], 0.0), writes=[f"KA1_{cc}" for cc in range(8)])
        k.op("pool", lambda e: e.memset(QA, 0.0), writes=[f"QA_{cc}" for cc in range(8)])
        k.op("pool", lambda e: e.memset(Vaug[:, :, :, 64:65], 1.0), writes=["Vaug0", "Vaug1", "Vaug2"])

        def load_w(h):
            k.dma(W0[h % 2], w0_d[h].rearrange("(kc p) c -> p kc c", p=128), writes=[f"W0_{h % 2}"], queue="pool")

        def tok(d, r, a, b):
            return slice(r + a * d, r + (b - 1) * d + 1, d)

        tr_ctr = [0]

        def v_from_xs(sx, g, cch):
            bank = 4 + tr_ctr[0] % 2
            tr_ctr[0] += 1
            pT = ps[bank][:].bitcast(BF16)
            for i in range(4):
                src = xs[sx][:, i * 128:(i + 1) * 128] if g == 0 else xs[sx][:, i:512:4]
                k.op("pe", lambda e, i=i, src=src: e.transpose(pT[:, i * 128:(i + 1) * 128], src, ident16),
                     reads=[f"xs{sx}", "ident16"], writes=[psr(bank)], inc=(i == 3))
            pv = pT[:, 0:512].rearrange("p (a b) -> p a b", a=4)
            if g == 0:
                dst = Vaug[:, 0, 4 * cch:4 * cch + 4, 0:64]
            else:
                dst = Vaug[:, 1, cch:NT:8, 0:64]
            k.op("act", lambda e: e.activation(dst, pv[:, :, 64:128], AF.Copy),
                 reads=[psr(bank)], writes=[f"Vaug{g}"])

        def gate_from_xs(sx, cch):
            bank = 4 + tr_ctr[0] % 2
            tr_ctr[0] += 1
            st_ = tr_ctr[0] % 2
            pT = ps[bank][:].bitcast(BF16)
            for i in range(4):
                k.op("pe", lambda e, i=i: e.transpose(pT[:, i * 128:(i + 1) * 128], xs[sx][:, i * 128:(i + 1) * 128], ident16),
                     reads=[f"xs{sx}", "ident16"], writes=[psr(bank)], inc=(i == 3))
            pv = pT[:, 0:512].rearrange("p (a b) -> p a b", a=4)
            k.op("act", lambda e: e.activation(thb[st_], pv[:, :, 64:128], AF.Tanh, scale=0.5),
                 reads=[psr(bank)], writes=[f"thb{st_}"])
            k.op("dve", lambda e: e.scalar_tensor_tensor(
                out=gs[:, 4 * cch:4 * cch + 4, :], in0=thb[st_], scalar=1.0, in1=pv[:, :, 64:128], op0=ALU.add, op1=ALU.mult),
                reads=[psr(bank), f"thb{st_}"], writes=["gs"])

        def v2_half(j):
            bank = 4 + tr_ctr[0] % 2
            tr_ctr[0] += 1
            pT = ps[bank][:].bitcast(BF16)
            for r in range(16):
                k.op("pe", lambda e, r=r: e.transpose(pT[:, r * 64:(r + 1) * 64], VT2[0:64, r:2048:16], ident16[0:64, 0:64]),
                     reads=["VT2", "ident16"], writes=[psr(bank)], inc=(r == 15))
            k.op("act", lambda e: e.activation(Vaug[:, 2, j:NT:2, 0:64], pT.rearrange("p (a b) -> p a b", a=16), AF.Copy),
                 reads=[psr(bank)], writes=["Vaug2"])

        def attn_group(g, Kt, kres, first):
            d = DIL[g][1]
            nb = S // d // 128
            steps = [(r, j) for r in range(d) for j in range(nb)]
            started = {}

            L_ = S // d

            def emit_S(i):
                r, j = steps[i]
                sl = i % 4
                bank = (0, 1, 4, 5)[i % 4]
                nq = 256 if j + 1 < nb else 128
                if d == 16:
                    k0 = r * L_ + j * 128
                    ksl, qsl = slice(k0, k0 + 128), slice(k0, k0 + nq)
                    rd = [f"{kres}_{cc}" for cc in range(8)] + [f"QA_{cc}" for cc in range(8)]
                else:
                    ksl, qsl = tok(d, r, j * 128, (j + 1) * 128), tok(d, r, j * 128, j * 128 + nq)
                    t_lo = j * 128 * d
                    t_hi = (j * 128 + nq) * d - 1
                    rd = [f"{kres}_{t_lo // 512}"] + [f"QA_{cc}" for cc in range(t_lo // 512, min(t_hi // 512, 7) + 1)]
                k.op("pe", lambda e: e.matmul(ps[bank][:, 0:nq], Kt[:, ksl], QA[:, qsl], start=True, stop=False),
                     reads=rd, writes=[psr(bank)], inc=False)
                k.op("pe", lambda e: e.matmul(ps[bank][:, 0:nq], ident16, mask16[:, 0:nq], start=False, stop=True),
                     reads=["ident16", "mask16"], writes=[psr(bank)])

            def emit_exp(i):
                r, j = steps[i]
                sl = i % 4
                bank = (0, 1, 4, 5)[i % 4]
                nq = 256 if j + 1 < nb else 128
                k.op("act", lambda e: e.activation(PT[sl][:, 0:nq], ps[bank][:, 0:nq], AF.Exp, scale=0.125),
                     reads=[psr(bank)], writes=[f"PT{sl}"])

            def pv_mm(b, V_b, pt_ap, sl, st, sp):
                fill = b // 4
                obank = 2 + fill % 2
                col = (b % 4) * 128
                k.op("pe", lambda e: e.matmul(ps[obank][0:65, col:col + 128], Vaug[:, g, V_b, :], pt_ap,
                                              start=st, stop=sp),
                     reads=[f"Vaug{g}", f"PT{sl}"], writes=[psr(obank)], inc=True)

            def emit_PV(i):
                r, j = steps[i]
                sl = i % 4
                b = r * nb + j
                has_next = j + 1 < nb
                pv_mm(b, b, PT[sl][:, 0:128], sl, st=(j == 0), sp=True)
                if has_next:
                    pv_mm(b + 1, b, PT[sl][:, 128:256], sl, st=True, sp=False)
                if b % 4 == 3:
                    fill = b // 4
                    obank = 2 + fill % 2
                    b0 = fill * 4
                    runs = []
                    bb = b0
                    while bb < b0 + 4:
                        rr, jj = divmod(bb, nb)
                        ln = min(4 - (bb - b0), nb - jj)
                        runs.append((rr, jj, ln, (bb - b0) * 128))
                        bb += ln
                    for (rr, jj, ln, col) in runs:
                        dst = acc[0:65, tok(d, rr, jj * 128, (jj + ln) * 128)]
                        src = ps[obank][0:65, col:col + ln * 128]
                        if first:
                            k.op("dve", lambda e, dst=dst, src=src: e.tensor_copy(dst, src),
                                 reads=[psr(obank)], writes=["acc"])
                        else:
                            k.op("dve", lambda e, dst=dst, src=src: e.tensor_tensor(out=dst, in0=src, in1=dst, op=ALU.add),
                                 reads=[psr(obank), "acc"], writes=["acc"])

            n = len(steps)
            for i0 in range(min(3, n)):
                emit_S(i0)
            for i in range(n):
                emit_exp(i)
                if i + 3 < n:
                    emit_S(i + 3)
                emit_PV(i)

        load_w(0)
        for h in range(NH0):
            Wt = W0[h % 2]
            wres = f"W0_{h % 2}"
            if h + 1 < NH0:
                load_w(h + 1)
            bufs = (xs, t1, t2, cs16, sn16)
            run_pass(Wt, wres, 256, [(QA, 0, 64, "QA", 16)], bufs, extra=lambda sx, cch: v_from_xs(sx, 0, cch))
            run_pass(Wt, wres, 384, [(KA0, 0, 64, "KA0", 16)], bufs, extra=lambda sx, cch: v_from_xs(sx, 1, cch))

            def extra_e(sx, cch):
                xb = (0, 1, 3)[cch % 3]
                k.op("act", lambda e: e.activation(VT2[0:64, (cch % 4) * 512:(cch % 4 + 1) * 512], ps[xb][0:64, :], AF.Copy),
                     reads=[psr(xb)], writes=["VT2"])
                gate_from_xs(sx, cch)
                if cch % 4 == 3:
                    v2_half(cch // 4)
            run_pass(Wt, wres, 512, [], bufs, extra=extra_e, do_rope=False)
            attn_group(2, KA0, "KA0", first=True)
            run_pass(Wt, wres, 0, [(QA, 0, 128, "QA", 1)], bufs)
            run_pass(Wt, wres, 128, [(KA0, 0, 64, "KA0", 1), (KA1, 64, 128, "KA1", 1)], bufs)
            attn_group(0, KA0, "KA0", first=False)
            attn_group(1, KA1, "KA1", first=False)
            hq = h % 2
            if True:
              for t0 in range(0, NT, 4):
                fb = (6, 7, 4, 5)[(t0 // 4) % 4]
                s = (t0 // 4) % 2
                for i in range(4):
                    tt = t0 + i
                    k.op("pe", lambda e, i=i, tt=tt: e.transpose(ps[fb][:, i * 65:(i + 1) * 65],
                                                                 acc[0:65, tt * 128:(tt + 1) * 128], identf[0:65, 0:65]),
                         reads=["acc", "identf"], writes=[psr(fb)], inc=(i == 3))
                trv = ps[fb][:, 0:260].rearrange("p (a b) -> p a b", a=4)
                k.op("dve", lambda e, trv=trv, s=s: e.reciprocal(rden[:, s * 4:s * 4 + 4], trv[:, :, 64]),
                     reads=[psr(fb)], writes=[f"rden{s}"])
                k.op("dve", lambda e, trv=trv, s=s: e.tensor_tensor(
                    out=ob[s], in0=trv[:, :, 0:64], in1=rden[:, s * 4:s * 4 + 4].unsqueeze(2).broadcast_to([128, 4, 64]),
                    op=ALU.mult), reads=[psr(fb), f"rden{s}"], writes=[f"ob{s}"])
                k.op("dve", lambda e, s=s, t0=t0: e.scalar_tensor_tensor(
                    out=y4[:, t0:t0 + 4, hq * 64:(hq + 1) * 64], in0=ob[s], scalar=0.5, in1=gs[:, t0:t0 + 4, :],
                    op0=ALU.mult, op1=ALU.mult), reads=[f"ob{s}", "gs"], writes=["y4"])
            if hq == 1:
                q4 = h // 2
                for half in range(2):
                    k.dma(y_view[:, half * 16:(half + 1) * 16, q4 * 128:(q4 + 1) * 128],
                          y4[:, half * 16:(half + 1) * 16, :], reads=["y4"], writes=["y_scr"])

    def phase_tail(layer, wo_d, res_d, dst_d, next_prenorm):
        c = Carve(PH_BASE + 5 * 8192)
        Wo = c.take([8, 1024], BF16)
        yt = [c.take([1024], BF16) for _ in range(3)]
        yT = [c.take([8, 128], BF16) for _ in range(3)]
        xr = [c.take([1024], F32) for _ in range(6)]
        hn = [c.take([1024], F32) for _ in range(3)]
        xn = [c.take([1024], BF16) for _ in range(3)]
        sqj = c.take([1024], BF16)
        sqj2 = c.take([1024], BF16)
        st2 = c.take([16], F32)
        k.dma(Wo, wo_d.rearrange("(kc p) n -> p kc n", p=128), writes=["Wo"], queue="pool")

        def st_L(tt):
            rows = slice(tt * 128, (tt + 1) * 128)
            k.dma(yt[tt % 3], y_scr[rows, :], reads=["y_scr"], writes=[f"yt{tt % 3}"])
            k.dma(xr[tt % 6], res_d[rows, :], reads=["h1_scr"] if res_d is h1_scr else [], writes=[f"xr{tt % 6}"])

        def st_A(tt):
            s3 = tt % 3
            pb = 4 if next_prenorm else 4 + tt % 2
            pT = ps[pb][:].bitcast(BF16)
            for kc in range(8):
                k.op("pe", lambda e, kc=kc: e.transpose(pT[:, kc * 128:(kc + 1) * 128],
                                                         yt[s3][:, kc * 128:(kc + 1) * 128], ident16),
                     reads=[f"yt{s3}", "ident16"], writes=[psr(pb)], inc=(kc == 7))
            k.op("act", lambda e: e.activation(yT[s3], pT.rearrange("p (a b) -> p a b", a=8), AF.Copy),
                 reads=[psr(pb)], writes=[f"yT{s3}"])

        def st_B(tt):
            s3 = tt % 3
            i = tt % 4
            for half in range(2):
                ob_ = ((0, 1), (2, 3), (6, 7))[tt % 3][half]
                for kc in range(8):
                    k.op("pe", lambda e, kc=kc, half=half, ob_=ob_: e.matmul(
                        ps[ob_][:], yT[s3][:, kc, :], Wo[:, kc, half * 512:(half + 1) * 512],
                        start=(kc == 0), stop=(kc == 7)),
                        reads=[f"yT{s3}", "Wo"], writes=[psr(ob_)], inc=(kc == 7))
            for half in range(2):
                ob_ = ((0, 1), (2, 3), (6, 7))[tt % 3][half]
                k.op("act", lambda e, half=half, ob_=ob_: e.activation(
                    sqj2[:, half * 512:(half + 1) * 512], ps[ob_][:], AF.Square,
                    accum_out=st2[:, 2 * i + half:2 * i + half + 1]),
                    reads=[psr(ob_)], writes=[f"sqj2_{half}", f"p_ss{i}_{half}"])

        def st_C(tt):
            s3 = tt % 3
            i = tt % 4
            rows = slice(tt * 128, (tt + 1) * 128)
            k.op("dve", lambda e: e.tensor_tensor(out=st2[:, 8 + i:9 + i], in0=st2[:, 2 * i:2 * i + 1],
                                                  in1=st2[:, 2 * i + 1:2 * i + 2], op=ALU.add),
                 reads=[f"p_ss{i}_0", f"p_ss{i}_1"], writes=[f"p_sum{i}"])
            k.op("act", lambda e: e.activation(st2[:, 8 + i:9 + i], st2[:, 8 + i:9 + i], AF.Ln, scale=1.0 / D, bias=EPS),
                 reads=[f"p_sum{i}"], writes=[f"p_sum{i}"])
            k.op("act", lambda e: e.activation(st2[:, 12 + i:13 + i], st2[:, 8 + i:9 + i], AF.Exp, scale=-0.5),
                 reads=[f"p_sum{i}"], writes=[f"p_rstd{i}"])
            for half in range(2):
                ob_ = ((0, 1), (2, 3), (6, 7))[tt % 3][half]
                k.op("dve", lambda e, half=half, ob_=ob_: e.scalar_tensor_tensor(
                    out=hn[s3][:, half * 512:(half + 1) * 512], in0=ps[ob_][:], scalar=st2[:, 12 + i:13 + i],
                    in1=gpost[:, layer, half * 512:(half + 1) * 512], op0=ALU.mult, op1=ALU.mult),
                    reads=[psr(ob_), f"p_rstd{i}", "gpost"], writes=[f"hn{s3}"])
            k.op("pool", lambda e: e.tensor_tensor(out=hn[s3], in0=hn[s3], in1=xr[tt % 6], op=ALU.add),
                 reads=[f"hn{s3}", f"xr{tt % 6}"], writes=[f"hn{s3}"])
            k.dma(dst_d[rows, :], hn[s3], reads=[f"hn{s3}"], writes=["h1_scr"] if dst_d is h1_scr else [])

        for i in range(NT + 6):
            if i < NT:
                st_L(i)
            t = i - 2
            if 0 <= t < NT:
                st_A(t)
            t = i - 3
            if 0 <= t < NT:
                st_B(t)
            t = i - 4
            if 0 <= t < NT:
                st_C(t)
            if next_prenorm:
                t = i - 5
                if 0 <= t < NT:
                    prenorm_a(hn[t % 3], f"hn{t % 3}", t, xn[t % 3], f"xn{t % 3}", sqj)
                t = i - 6
                if 0 <= t < NT:
                    prenorm_b(t, layer + 1, xn[t % 3], f"xn{t % 3}", bank=5)

    def phase_layer1():
        c = Carve(PH_BASE)
        QA = c.take([S], BF16)
        KA0 = c.take([S], BF16)
        KA1 = c.take([S], BF16)
        cs16 = c.take([S], BF16)
        sn16 = c.take([S], BF16)
        xs = [c.take([512], BF16) for _ in range(3)]
        t1 = [c.take([512], F32) for _ in range(2)]
        t2 = [c.take([512], F32) for _ in range(2)]
        PT = [c.take([512], BF16) for _ in range(4)]
        y2 = c.take([NT, 256], BF16)
        Vaug = c.take([NT, 129], BF16)
        gsl = c.take([NT, 128], BF16)
        W1 = [c.take([8, 512], BF16) for _ in range(2)]
        thb = [c.take([2, 128], F32) for _ in range(2)]
        g1b = [c.take([2, 128], F32) for _ in range(2)]
        o0 = c.take([4, 128], F32)
        od = c.take([4, 128], F32)
        sq = c.take([4, 128], F32)
        rr = c.take([32], F32)

        if stage != "full":
            k.dma(cs16, cs_d, writes=["cs16"], queue="pool", max_dma_last_dim=4096)
            k.dma(sn16, sn_d, writes=["sn16"], queue="pool", max_dma_last_dim=4096)
            k.op("pool", lambda e: e.memset(KA0[64:128, :], 0.0), writes=[f"KA0_{cc}" for cc in range(8)])
            k.op("pool", lambda e: e.memset(KA1[0:64, :], 0.0), writes=[f"KA1_{cc}" for cc in range(8)])
        k.op("pool", lambda e: e.memset(Vaug[:, :, 128:129], 1.0), writes=["Vaug"])

        def load_w(h):
            k.dma(W1[h % 2], w1_d[h].rearrange("(kc p) c -> p kc c", p=128), writes=[f"W1_{h % 2}"], queue="pool")

        load_w(0)
        for h in range(8):
            Wt = W1[h % 2]
            wres = f"W1_{h % 2}"
            if h + 1 < 8:
                load_w(h + 1)
            bufs = (xs, t1, t2, cs16, sn16)
            run_pass(Wt, wres, 0, [(QA, 0, 128, "QA", 1)], bufs)
            run_pass(Wt, wres, 128, [(KA0, 0, 64, "KA0", 1), (KA1, 64, 128, "KA1", 1)], bufs)
            for t0 in range(0, NT, 2):
                bank = 4 + (t0 // 2) % 4
                s = (t0 // 2) % 2
                for bi in range(2):
                    tt = t0 + bi
                    for kc in range(8):
                        k.op("pe", lambda e, kc=kc, bi=bi, tt=tt: e.matmul(
                            ps[bank][:, bi * 256:(bi + 1) * 256], uT[:, kc, tt * 128:(tt + 1) * 128],
                            Wt[:, kc, 256:512], start=(kc == 0), stop=(kc == 7)),
                            reads=[wres, "uT"], writes=[psr(bank)], inc=(kc == 7 and bi == 1))
                pv = ps[bank][:].rearrange("p (a b) -> p a b", a=2)
                k.op("act", lambda e, pv=pv, t0=t0: e.activation(Vaug[:, t0:t0 + 2, 0:128], pv[:, :, 0:128], AF.Copy),
                     reads=[psr(bank)], writes=["Vaug"])
                k.op("act", lambda e, pv=pv, s=s: e.activation(thb[s], pv[:, :, 128:256], AF.Tanh, scale=0.5),
                     reads=[psr(bank)], writes=[f"thb{s}"])
                k.op("dve", lambda e, pv=pv, s=s: e.scalar_tensor_tensor(
                    out=g1b[s], in0=thb[s], scalar=1.0, in1=pv[:, :, 128:256], op0=ALU.add, op1=ALU.mult),
                    reads=[psr(bank), f"thb{s}"], writes=[f"g1b{s}"])
                k.op("pool", lambda e, s=s, t0=t0: e.tensor_tensor(
                    out=gsl[:, t0:t0 + 2, :], in0=g1b[s], in1=sublnc.unsqueeze(1).broadcast_to([128, 2, 128]),
                    op=ALU.mult), reads=[f"g1b{s}", "sublnc"], writes=["gsl"])

            steps = []
            for qc in range(8):
                for comp in range(2):
                    for kb in range(4 * qc + 4):
                        steps.append((qc, comp, kb))
            cc_ctr = [0]
            started = {}

            def geom(i):
                qc, comp, kb = steps[i]
                m = kb - 4 * qc
                diag = m >= 0
                q0 = kb * 128 if diag else qc * 512
                nq = (qc + 1) * 512 - q0
                return qc, comp, kb, diag, q0, nq

            def emit_S(i):
                qc, comp, kb, diag, q0, nq = geom(i)
                sl = i % 4
                bank = 4 + sl
                Kt, kres = (KA0, "KA0") if comp == 0 else (KA1, "KA1")
                k.op("pe", lambda e: e.matmul(ps[bank][:, 0:nq], Kt[:, kb * 128:(kb + 1) * 128], QA[:, q0:q0 + nq],
                                              start=True, stop=not diag),
                     reads=[f"{kres}_{kb // 4}", f"QA_{qc}"], writes=[psr(bank)], inc=not diag)
                if diag:
                    k.op("pe", lambda e: e.matmul(ps[bank][:, 0:128], ident16, mask16[:, 0:128], start=False, stop=True),
                         reads=["ident16", "mask16"], writes=[psr(bank)])

            def emit_exp(i):
                qc, comp, kb, diag, q0, nq = geom(i)
                sl = i % 4
                bank = 4 + sl
                k.op("act", lambda e: e.activation(PT[sl][:, 0:nq], ps[bank][:, 0:nq], AF.Exp, scale=0.125),
                     reads=[psr(bank)], writes=[f"PT{sl}"])

            def emit_PV(i):
                qc, comp, kb, diag, q0, nq = geom(i)
                sl = i % 4
                cc = qc * 2 + comp
                ob0 = (cc % 2) * 2
                nt = nq // 128
                for ti in range(nt):
                    qt = q0 // 128 + ti
                    lq = qt - qc * 4
                    obank = ob0 + lq // 2
                    col = (lq % 2) * 129
                    key = (cc, lq // 2)
                    st = key not in started
                    started[key] = True
                    last = (kb == 4 * qc + 3) and (ti == nt - 1)
                    k.op("pe", lambda e, ti=ti, obank=obank, col=col, st=st: e.matmul(
                        ps[obank][:, col:col + 129], PT[sl][:, ti * 128:(ti + 1) * 128], Vaug[:, kb, :],
                        start=st, stop=True, skip_group_check=True),
                        reads=["Vaug", f"PT{sl}"], writes=[psr(obank)], inc=(ti == nt - 1))
                if kb == 4 * qc + 3:
                    t0 = qc * 4
                    for hb in range(2):
                        obank = ob0 + hb
                        ov = ps[obank][:, 0:258].rearrange("p (a b) -> p a b", a=2)
                        ro = (cc % 4) * 4 + hb * 2
                        k.op("dve", lambda e, ov=ov, ro=ro: e.reciprocal(rr[:, ro:ro + 2], ov[:, :, 128]),
                             reads=[psr(obank)], writes=[f"rr{ro}"])
                        if comp == 0:
                            k.op("dve", lambda e, ov=ov, ro=ro, hb=hb: e.tensor_tensor(
                                out=o0[:, hb * 2:hb * 2 + 2, :], in0=ov[:, :, 0:128],
                                in1=rr[:, ro:ro + 2].unsqueeze(2).broadcast_to([128, 2, 128]), op=ALU.mult),
                                reads=[psr(obank), f"rr{ro}"], writes=[f"o0_{hb}"])
                        else:
                            k.op("dve", lambda e, ro=ro: e.tensor_scalar(rr[:, ro:ro + 2], rr[:, ro:ro + 2], neglam, None, ALU.mult),
                                 reads=[f"rr{ro}", "neglam"], writes=[f"rr{ro}"])
                            k.op("dve", lambda e, ov=ov, ro=ro, hb=hb: e.tensor_tensor(
                                out=od[:, hb * 2:hb * 2 + 2, :], in0=ov[:, :, 0:128],
                                in1=rr[:, ro:ro + 2].unsqueeze(2).broadcast_to([128, 2, 128]), op=ALU.mult),
                                reads=[psr(obank), f"rr{ro}"], writes=[f"od_{hb}"])
                    if comp == 1:
                        def f1():
                            k.op("pool", lambda e: e.tensor_tensor(out=od, in0=od, in1=o0, op=ALU.add),
                                 reads=["od_0", "od_1", "o0_0", "o0_1"], writes=["od_0", "od_1"])

                        def f2():
                            k.op("dve", lambda e: e.tensor_tensor(out=sq, in0=od, in1=od, op=ALU.mult),
                                 reads=["od_0", "od_1"], writes=["sq"])
                            k.op("dve", lambda e: e.tensor_reduce(out=rr[:, 16:20], in_=sq, axis=AX.X, op=ALU.add),
                                 reads=["sq"], writes=["rr16"])

                        def f3():
                            k.op("act", lambda e: e.activation(rr[:, 20:24], rr[:, 16:20], AF.Ln, scale=1.0 / 128, bias=EPS),
                                 reads=["rr16"], writes=["rr20"])
                            k.op("act", lambda e: e.activation(rr[:, 24:28], rr[:, 20:24], AF.Exp, scale=-0.5),
                                 reads=["rr20"], writes=["rr24"])

                        def f4():
                            k.op("dve", lambda e: e.tensor_tensor(
                                out=sq, in0=od, in1=rr[:, 24:28].unsqueeze(2).broadcast_to([128, 4, 128]), op=ALU.mult),
                                reads=["od_0", "od_1", "rr24", "sq"], writes=["sq"])

                        def f5(t0=t0):
                            hp = h % 2
                            k.op("pool", lambda e: e.tensor_tensor(
                                out=y2[:, t0:t0 + 4, hp * 128:(hp + 1) * 128], in0=sq, in1=gsl[:, t0:t0 + 4, :], op=ALU.mult),
                                reads=["sq", "gsl"], writes=["y2"])
                        for dly, fn_ in ((4, f1), (8, f2), (12, f3), (15, f4), (18, f5)):
                            deferred.append((i + dly, fn_))

            deferred = []
            n = len(steps)
            emit_S(0)
            emit_S(1)
            emit_S(2)
            for i in range(n):
                emit_exp(i)
                if i + 3 < n:
                    emit_S(i + 3)
                emit_PV(i)
                deferred.sort(key=lambda x: x[0])
                while deferred and deferred[0][0] <= i:
                    deferred.pop(0)[1]()
            while deferred:
                deferred.pop(0)[1]()
            if h % 2 == 1:
                q2 = h // 2
                for half in range(2):
                    k.dma(y_view[:, half * 16:(half + 1) * 16, q2 * 256:(q2 + 1) * 256],
                          y2[:, half * 16:(half + 1) * 16, :], reads=["y2"], writes=["y_scr"])

    if stage.startswith("d_"):
        if stage >= "d_1":
            phase_prenorm_dram(x_d, 0)
            k.barrier()
        if stage >= "d_2":
            phase_layer0()
            k.barrier()
        dbg = Carve(PH_BASE).take([1024], F32)
        k.dma(dbg, x_d[0:128, :], writes=["dbg"])
        k.dma(out_d[0:128, :], dbg, reads=["dbg"])
    elif stage == "l0":
        phase_prenorm_dram(x_d, 0)
        k.barrier()
        phase_layer0()
        k.barrier()
        phase_tail(0, w0o_d, x_d, out_d, next_prenorm=False)
    elif stage == "l1":
        phase_prenorm_dram(x_d, 1)
        k.barrier()
        phase_layer1()
        k.barrier()
        phase_tail(1, w1o_d, x_d, out_d, next_prenorm=False)
    else:
        phase_prenorm_dram(x_d, 0)
        k.barrier()
        phase_layer0()
        k.barrier()
        phase_tail(0, w0o_d, x_d, h1_scr, next_prenorm=True)
        k.barrier()
        phase_layer1()
        k.barrier()
        phase_tail(1, w1o_d, h1_scr, out_d, next_prenorm=False)
    k.finish()
    return nc, k


def _consts():
    f32 = np.float32
    freqs = (np.float32(10000.0) ** (-(np.arange(0, 64, 2, dtype=f32)) / np.float32(64))).astype(f32)
    pos = np.arange(S, dtype=f32)
    ang = (pos[:, None] * freqs[None, :]).astype(f32)
    cos = np.cos(ang).astype(f32).T
    sin = np.sin(ang).astype(f32).T
    cs = np.ascontiguousarray(np.tile(cos, (4, 1)))
    sn = np.ascontiguousarray(np.tile(sin, (4, 1)))
    perm = np.zeros((128, 128), f32)
    for m in range(128):
        if (m % 64) < 32:
            perm[m + 32, m] = -1.0
        else:
            perm[m - 32, m] = 1.0
    ident = np.eye(128, dtype=f32)
    kk = np.arange(128)[:, None]
    qq = np.arange(128)[None, :]
    tri_cur = np.where(kk <= qq, 0.0, NEG).astype(f32)
    tri_prev = np.where(kk >= qq, 0.0, NEG).astype(f32)
    mask0 = np.ascontiguousarray(np.concatenate([tri_cur, tri_prev], axis=1))
    return cs, sn, perm, ident, mask0


def _layout_weights(dil_w_in, dil_w_out, diff_w_in, diff_w_out):
    w = dil_w_in[0]
    w0 = np.empty((16, 1024, 640), np.float32)
    for h in range(16):
        def qc(g):
            return slice((g * 16 + h) * 64, (g * 16 + h) * 64 + 64)

        def kc_(g):
            return slice(3072 + (g * 16 + h) * 64, 3072 + (g * 16 + h) * 64 + 64)

        def vc(g):
            return slice(6144 + (g * 16 + h) * 64, 6144 + (g * 16 + h) * 64 + 64)
        gc = slice(9216 + h * 64, 9216 + h * 64 + 64)
        cols = [qc(0), qc(1), kc_(0), kc_(1), qc(2), vc(0), kc_(2), vc(1), vc(2), gc]
        for i, sl in enumerate(cols):
            w0[h, :, i * 64:(i + 1) * 64] = w[:, sl]
    w1i = diff_w_in[0]
    w1 = np.empty((8, 1024, 512), np.float32)
    for h in range(8):
        w1[h, :, 0:128] = w1i[:, (2 * h) * 64:(2 * h + 2) * 64]
        w1[h, :, 128:256] = w1i[:, 1024 + (2 * h) * 64:1024 + (2 * h + 2) * 64]
        w1[h, :, 256:384] = w1i[:, 2048 + h * 128:2048 + (h + 1) * 128]
        w1[h, :, 384:512] = w1i[:, 3072 + h * 128:3072 + (h + 1) * 128]
    return w0, np.ascontiguousarray(dil_w_out[0]), w1, np.ascontiguousarray(diff_w_out[0])


_CACHE = {}


def _get_program(stage):
    if stage not in _CACHE:
        _CACHE[stage] = build_program(stage)[0]
    return _CACHE[stage]


def _common_maps(norm_pre, norm_post, dil_w_in, dil_w_out, diff_w_in, diff_w_out,
                 diff_lambda_q1, diff_lambda_k1, diff_lambda_q2, diff_lambda_k2, diff_subln):
    cs, sn, perm, ident, mask0 = _consts()
    w0, w0o, w1, w1o = _layout_weights(np.asarray(dil_w_in, np.float32), np.asarray(dil_w_out, np.float32),
                                       np.asarray(diff_w_in, np.float32), np.asarray(diff_w_out, np.float32))
    npre = np.asarray(norm_pre, np.float32)
    gpre = np.ascontiguousarray(npre.reshape(2, 8, 128).transpose(2, 0, 1).reshape(128, 16))
    lamv = np.ascontiguousarray(np.concatenate([np.asarray(a, np.float32).reshape(1, 64) for a in
                                                (diff_lambda_q1, diff_lambda_k1, diff_lambda_q2, diff_lambda_k2)], 0))
    return {"w0": w0, "w0o": w0o, "w1": w1, "w1o": w1o, "gpre": gpre,
            "gpost": np.ascontiguousarray(np.asarray(norm_post, np.float32)),
            "lamv": lamv, "subln": np.ascontiguousarray(np.asarray(diff_subln, np.float32).reshape(1, 128)),
            "cs": cs, "sn": sn, "perm": perm, "ident": ident, "mask0": mask0}


def run_stage(stage, xs, common):
    nc = _get_program(stage)
    in_maps = [dict(common, x=np.ascontiguousarray(xs[b])) for b in range(len(xs))]
    res = run_bass_kernel_spmd(nc, in_maps, core_ids=list(range(len(xs))))
    return np.stack([r["out"] for r in res.results], 0)


def kernel(x, norm_pre, norm_post, dil_w_in, dil_w_out, diff_w_in, diff_w_out,
           diff_lambda_q1, diff_lambda_k1, diff_lambda_q2, diff_lambda_k2, diff_subln):
    x = np.asarray(x, np.float32)
    common = _common_maps(norm_pre, norm_post, dil_w_in, dil_w_out, diff_w_in, diff_w_out,
                          diff_lambda_q1, diff_lambda_k1, diff_lambda_q2, diff_lambda_k2, diff_subln)
    out = run_stage("full", [x[b] for b in range(8)], common)
    return out.astype(np.float32)
```
